# Optimizing a Trainium2 kernel written in Bass

```python
import math
import jax
import jax.numpy as jnp
from jax import lax
import numpy as np

D_MODEL = 2048
BATCH = 4
SEQ = 4096
DEPTH = 1

HEAD_DIM = 128
DIL_GROUPS = ((128, 1), (512, 4), (2048, 16))
DIL_HEADS_PER_GROUP = 4
N_DIL_HEADS = len(DIL_GROUPS) * DIL_HEADS_PER_GROUP
N_DIFF_HEADS = 4
DIFF_DIM = HEAD_DIM // 2
BRANCH_WIDTH = DIL_HEADS_PER_GROUP * HEAD_DIM
N_BRANCHES = 2
ROPE_THETA = 500000.0
ROPE_FRACTION = 4
N_EXPERTS = 16
EXPERT_FF = 2048
EC_CAPACITY = 2
LN_EPS = 1e-5
DIFF_NORM_EPS = 1e-5
Q_BLOCK = 128
ALPHA = (2 * DEPTH) ** 0.25
BETA = (8 * DEPTH) ** -0.25
NEG_INF = -1e30

A_QKV = N_DIL_HEADS * HEAD_DIM
B_QK = N_DIFF_HEADS * 2 * DIFF_DIM
B_V = N_DIFF_HEADS * HEAD_DIM
GATE_COLS = N_BRANCHES * D_MODEL
IN_COLS = 3 * A_QKV + 2 * B_QK + B_V + GATE_COLS

kernel_name = "dilated_diff_attn_ec_moe_deepnorm_block"


def _normal(key, shape, scale):
    return jax.random.normal(key, shape, jnp.float32) * scale


def layer_norm(x, g, b):
    x32 = x.astype(jnp.float32)
    mu = jnp.mean(x32, axis=-1, keepdims=True)
    var = jnp.mean(jnp.square(x32 - mu), axis=-1, keepdims=True)
    y = (x32 - mu) * lax.rsqrt(var + LN_EPS) * g.astype(jnp.float32) + b.astype(jnp.float32)
    return y.astype(x.dtype)


def rope_partial(x, pos):
    dh = x.shape[-1]
    rot = dh // ROPE_FRACTION
    half = rot // 2
    inv_freq = ROPE_THETA ** (-2.0 * jnp.arange(half, dtype=jnp.float32) / rot)
    ang = pos.astype(jnp.float32)[:, None] * inv_freq[None, :]
    shape = (1, pos.shape[0]) + (1,) * (x.ndim - 3) + (half,)
    cos = jnp.cos(ang).reshape(shape).astype(x.dtype)
    sin = jnp.sin(ang).reshape(shape).astype(x.dtype)
    x1 = x[..., :half]
    x2 = x[..., half:rot]
    return jnp.concatenate([x1 * cos - x2 * sin, x2 * cos + x1 * sin, x[..., rot:]], axis=-1)


def band_attention(q, k, v, half):
    b_, n_, L, dh = q.shape
    qb = half
    nb = -(-L // qb)
    Lp = nb * qb
    qp = jnp.pad(q, ((0, 0), (0, 0), (0, Lp - L), (0, 0)))
    kv_pad = ((0, 0), (0, 0), (half, Lp - L + half), (0, 0))
    kp = jnp.pad(k, kv_pad)
    vp = jnp.pad(v, kv_pad)
    kb = qb + 2 * half
    idx = (jnp.arange(nb) * qb)[:, None] + jnp.arange(kb)[None, :]
    k_blk = kp[:, :, idx]
    v_blk = vp[:, :, idx]
    q_blk = qp.reshape(b_, n_, nb, qb, dh)
    s = jnp.einsum('bniqd,bnikd->bniqk', q_blk, k_blk).astype(jnp.float32) * (dh ** -0.5)
    qpos = (jnp.arange(nb) * qb)[:, None] + jnp.arange(qb)[None, :]
    kpos = (idx - half)[:, None, :]
    valid = (jnp.abs(qpos[:, :, None] - kpos) <= half) & (kpos >= 0) & (kpos < L)
    s = jnp.where(valid, s, NEG_INF)
    m = jnp.max(s, axis=-1, keepdims=True)
    p = jnp.exp(s - m)
    den = jnp.sum(p, axis=-1, keepdims=True)
    o = jnp.einsum('bniqk,bnikd->bniqd', (p / den).astype(v.dtype), v_blk)
    lse = (m + jnp.log(den))[..., 0]
    o = o.reshape(b_, n_, Lp, dh)[:, :, :L]
    lse = lse.reshape(b_, n_, Lp)[:, :, :L]
    return o, lse


def dilated_attention(q, k, v):
    b_, S, _, H, dh = q.shape
    outs, lses = [], []
    for g, (window, r) in enumerate(DIL_GROUPS):
        L = S // r

        def to_strided(t):
            return t.reshape(b_, L, r, H, dh).transpose(0, 3, 2, 1, 4).reshape(b_, H * r, L, dh)

        o, lse = band_attention(to_strided(q[:, :, g]), to_strided(k[:, :, g]),
                                to_strided(v[:, :, g]), window // (2 * r))
        outs.append(o.reshape(b_, H, r, L, dh).transpose(0, 3, 2, 1, 4).reshape(b_, S, H, dh))
        lses.append(lse.reshape(b_, H, r, L).transpose(0, 3, 2, 1).reshape(b_, S, H))
    o = jnp.stack(outs, axis=2)
    w = jax.nn.softmax(jnp.stack(lses, axis=2), axis=2)
    return jnp.sum(o * w[..., None].astype(o.dtype), axis=2)


def diff_attention(q, k, v, lam, norm_w, lambda_init):
    b_, S, H, _, dc = q.shape
    dv = v.shape[-1]
    nb = S // Q_BLOCK
    kt = k.transpose(0, 2, 3, 1, 4)
    vt = v.transpose(0, 2, 1, 3)
    q_blocks = q.transpose(0, 2, 3, 1, 4).reshape(b_, H, 2, nb, Q_BLOCK, dc).transpose(3, 0, 1, 2, 4, 5)
    scale = dc ** -0.5

    def one_block(qb):
        s = jnp.einsum('bhcqd,bhckd->bhcqk', qb, kt).astype(jnp.float32) * scale
        p = jax.nn.softmax(s, axis=-1)
        a = p[:, :, 0] - lam * p[:, :, 1]
        return jnp.einsum('bhqk,bhkd->bhqd', a.astype(vt.dtype), vt)

    o = lax.map(one_block, q_blocks)
    o = o.transpose(1, 0, 3, 2, 4).reshape(b_, S, H, dv)
    o32 = o.astype(jnp.float32)
    o32 = o32 * lax.rsqrt(jnp.mean(jnp.square(o32), axis=-1, keepdims=True) + DIFF_NORM_EPS)
    o32 = o32 * norm_w.astype(jnp.float32) * (1.0 - lambda_init)
    return o32.astype(v.dtype)


def expert_choice_ffn(x, w_router, w_gate, w_up, w_down):
    b_, S, D = x.shape
    cap = EC_CAPACITY * S // N_EXPERTS
    aff = jax.nn.softmax(jnp.einsum('bsd,de->bse', x, w_router).astype(jnp.float32), axis=-1)
    gates, idx = lax.top_k(aff.transpose(0, 2, 1), cap)
    xin = jax.vmap(lambda xb, ib: xb[ib])(x, idx)
    h = jax.nn.silu(jnp.einsum('becd,edf->becf', xin, w_gate)) * jnp.einsum('becd,edf->becf', xin, w_up)
    out = jnp.einsum('becf,efd->becd', h, w_down) * gates[..., None].astype(x.dtype)
    y = jax.vmap(lambda ib, ob: jnp.zeros((S, D), x.dtype).at[ib.reshape(-1)].add(ob.reshape(-1, D)))(idx, out)
    return y


def setup_inputs(seed: int = 0) -> dict:
    key = jax.random.key(seed)
    ks = jax.random.split(key, 20)
    d_sc = D_MODEL ** -0.5
    x = _normal(ks[0], (BATCH, SEQ, D_MODEL), 1.0)
    w_in = jnp.concatenate([
        _normal(ks[1], (DEPTH, D_MODEL, 2 * A_QKV), d_sc),
        _normal(ks[2], (DEPTH, D_MODEL, A_QKV), d_sc * BETA),
        _normal(ks[3], (DEPTH, D_MODEL, 2 * B_QK), d_sc),
        _normal(ks[4], (DEPTH, D_MODEL, B_V), d_sc * BETA),
        _normal(ks[5], (DEPTH, D_MODEL, GATE_COLS), d_sc),
    ], axis=-1)
    lambda_q1 = _normal(ks[6], (DEPTH, DIFF_DIM), 0.1)
    lambda_k1 = _normal(ks[7], (DEPTH, DIFF_DIM), 0.1)
    lambda_q2 = _normal(ks[8], (DEPTH, DIFF_DIM), 0.1)
    lambda_k2 = _normal(ks[9], (DEPTH, DIFF_DIM), 0.1)
    diff_norm_w = 1.0 + _normal(ks[10], (DEPTH, HEAD_DIM), 0.01)
    w_branch = _normal(ks[11], (DEPTH, N_BRANCHES, BRANCH_WIDTH, D_MODEL), BRANCH_WIDTH ** -0.5 * BETA)
    w_out = _normal(ks[12], (DEPTH, D_MODEL, D_MODEL), d_sc * BETA)
    ln1_g = 1.0 + _normal(ks[13], (DEPTH, D_MODEL), 0.01)
    ln1_b = _normal(ks[14], (DEPTH, D_MODEL), 0.01)
    w_router = _normal(ks[15], (DEPTH, D_MODEL, N_EXPERTS), d_sc)
    w_gate = _normal(ks[16], (DEPTH, N_EXPERTS, D_MODEL, EXPERT_FF), d_sc * BETA)
    w_up = _normal(ks[17], (DEPTH, N_EXPERTS, D_MODEL, EXPERT_FF), d_sc * BETA)
    w_down = _normal(ks[18], (DEPTH, N_EXPERTS, EXPERT_FF, D_MODEL), EXPERT_FF ** -0.5 * BETA)
    k_ln2g, k_ln2b = jax.random.split(ks[19])
    ln2_g = 1.0 + _normal(k_ln2g, (DEPTH, D_MODEL), 0.01)
    ln2_b = _normal(k_ln2b, (DEPTH, D_MODEL), 0.01)
    return {"x": x, "w_in": w_in, "lambda_q1": lambda_q1, "lambda_k1": lambda_k1,
            "lambda_q2": lambda_q2, "lambda_k2": lambda_k2, "diff_norm_w": diff_norm_w,
            "w_branch": w_branch, "w_out": w_out, "ln1_g": ln1_g, "ln1_b": ln1_b,
            "w_router": w_router, "w_gate": w_gate, "w_up": w_up, "w_down": w_down,
            "ln2_g": ln2_g, "ln2_b": ln2_b}


def reference(x, w_in, lambda_q1, lambda_k1, lambda_q2, lambda_k2, diff_norm_w,
              w_branch, w_out, ln1_g, ln1_b, w_router, w_gate, w_up, w_down, ln2_g, ln2_b):
    b_, S, D = x.shape
    pos = jnp.arange(S)
    o1 = A_QKV
    o2 = 2 * A_QKV
    o3 = 3 * A_QKV
    o4 = o3 + B_QK
    o5 = o4 + B_QK
    o6 = o5 + B_V
    n_groups = len(DIL_GROUPS)
    for l in range(DEPTH):
        lambda_init = 0.8 - 0.6 * math.exp(-0.3 * l)
        z = jnp.einsum('bsd,dc->bsc', x, w_in[l])
        qa = z[..., :o1].reshape(b_, S, n_groups, DIL_HEADS_PER_GROUP, HEAD_DIM)
        ka = z[..., o1:o2].reshape(b_, S, n_groups, DIL_HEADS_PER_GROUP, HEAD_DIM)
        va = z[..., o2:o3].reshape(b_, S, n_groups, DIL_HEADS_PER_GROUP, HEAD_DIM)
        qb = z[..., o3:o4].reshape(b_, S, N_DIFF_HEADS, 2, DIFF_DIM)
        kb = z[..., o4:o5].reshape(b_, S, N_DIFF_HEADS, 2, DIFF_DIM)
        vb = z[..., o5:o6].reshape(b_, S, N_DIFF_HEADS, HEAD_DIM)
        gate = jax.nn.sigmoid(z[..., o6:].reshape(b_, S, N_BRANCHES, D))
        out_a = dilated_attention(rope_partial(qa, pos), rope_partial(ka, pos), va)
        lam = (jnp.exp(jnp.sum(lambda_q1[l].astype(jnp.float32) * lambda_k1[l].astype(jnp.float32)))
               - jnp.exp(jnp.sum(lambda_q2[l].astype(jnp.float32) * lambda_k2[l].astype(jnp.float32)))
               + lambda_init)
        out_b = diff_attention(rope_partial(qb, pos), rope_partial(kb, pos), vb, lam, diff_norm_w[l], lambda_init)
        branches = jnp.stack([out_a.reshape(b_, S, BRANCH_WIDTH), out_b.reshape(b_, S, BRANCH_WIDTH)], axis=2)
        branch_d = jnp.einsum('bsgc,gcd->bsgd', branches, w_branch[l])
        merged = jnp.sum(gate * branch_d, axis=2)
        mix = jnp.einsum('bsd,de->bse', merged, w_out[l])
        x = layer_norm(ALPHA * x + mix, ln1_g[l], ln1_b[l])
        y = expert_choice_ffn(x, w_router[l], w_gate[l], w_up[l], w_down[l])
        x = layer_norm(ALPHA * x + y, ln2_g[l], ln2_b[l])
    return x
```

```python
import contextlib
import numpy as np
import concourse.bass as bass
import concourse.mybir as mybir
from concourse.bass_utils import run_bass_kernel_spmd

F32 = mybir.dt.float32
BF16 = mybir.dt.bfloat16
I32 = mybir.dt.int32
ALU = mybir.AluOpType
AF = mybir.ActivationFunctionType
AX = mybir.AxisListType

S = 4096
D = 2048
NT = S // 128
NE = 16
CAP = 512
ALPHA = 2.0 ** 0.25
LN_EPS = 1e-5
BIG = float(1 << 20)
DEBUG = False


class Ev:
    __slots__ = ("sem", "val")

    def __init__(self, sem, val):
        self.sem = sem
        self.val = val


class DSem:
    def __init__(self, h):
        self.h = h
        self.n = 0


class Res:
    def __init__(self, name):
        self.name = name
        self.w = None
        self.r = {}
        self.din = None
        self.dout = None


class Q:
    def __init__(self, eng, sem, is_pe=False):
        self.eng = eng
        self.sem = sem
        self.n = 0
        self.known = {}
        self.is_pe = is_pe

    def wait(self, ev):
        if ev is None:
            return
        if ev.sem is self.sem and self.is_pe:
            return
        k = id(ev.sem)
        if self.known.get(k, 0) >= ev.val:
            return
        self.eng.wait_ge(ev.sem, ev.val)
        self.known[k] = ev.val


class KB:
    def __init__(self, nc, es):
        self.nc = nc
        self.es = es
        self.nsem = 0
        self.pe = Q(nc.tensor, self.newsem("q_pe"), True)
        self.act = Q(nc.scalar, self.newsem("q_act"))
        self.dve = Q(nc.vector, self.newsem("q_dve"))
        self.pool = Q(nc.gpsimd, self.newsem("q_pool"))
        self.sp = Q(nc.sync, self.newsem("q_sp"))
        self.queues = [self.pe, self.act, self.dve, self.pool, self.sp]
        self.dsems = []

    def newsem(self, name):
        self.nsem += 1
        return self.es.enter_context(self.nc.semaphore(name))

    def dsem(self):
        d = DSem(self.newsem("d%d" % len(self.dsems)))
        self.dsems.append(d)
        return d

    def _pre(self, q, reads, writes):
        for r in reads:
            q.wait(r.w)
        for w in writes:
            q.wait(w.w)
            for ev in w.r.values():
                q.wait(ev)

    def _post(self, ev, reads, writes):
        for r in reads:
            r.r[id(ev.sem)] = ev
        for w in writes:
            w.w = ev
            w.r = {}

    def op(self, q, reads, writes, fn):
        self._pre(q, reads, writes)
        inst = fn()
        q.n += 1
        inst.then_inc(q.sem, 1)
        self._post(Ev(q.sem, q.n), reads, writes)

    def dma(self, q, out, in_, reads, writes, ds, fn=None):
        self._pre(q, reads, writes)
        if fn is None:
            inst = q.eng.dma_start(out=out, in_=in_)
        else:
            inst = fn()
        ds.n += 16
        inst.then_inc(ds.h, 16)
        self._post(Ev(ds.h, ds.n), reads, writes)

    def load(self, q, res, out, in_, src=()):
        if res.din is None:
            res.din = self.dsem()
        self.dma(q, out, in_, list(src), [res], res.din)

    def store(self, q, res, out, in_, dst=()):
        if res.dout is None:
            res.dout = self.dsem()
        self.dma(q, out, in_, [res], list(dst), res.dout)

    def barrier(self):
        self.marks = getattr(self, "marks", [])
        self.marks.append({"pe": self.pe.n, "act": self.act.n, "dve": self.dve.n, "pool": self.pool.n})
        for q in self.queues:
            for p in self.queues:
                if p is not q and p.n > 0:
                    q.wait(Ev(p.sem, p.n))
            for d in self.dsems:
                if d.n > 0:
                    q.wait(Ev(d.h, d.n))


class Stream:
    def __init__(self, kb, q, bufs, srcs, hold=1):
        self.kb, self.q, self.bufs, self.srcs = kb, q, bufs, srcs
        self.n = 0
        self.hold = hold

    def get(self, i):
        last = min(i + len(self.bufs) - self.hold, len(self.srcs) - 1)
        while self.n <= last:
            b = self.bufs[self.n % len(self.bufs)]
            self.kb.load(self.q, b.res, b[:], self.srcs[self.n])
            self.n += 1
        return self.bufs[i % len(self.bufs)]


class Buf:
    def __init__(self, t, name):
        self.t = t
        self.res = Res(name)

    def __getitem__(self, k):
        return self.t[k]


def build_program():
    nc = bass.Bass("TRN2", target_bir_lowering=False)
    es = contextlib.ExitStack()

    def din(name, shape, dt=F32):
        return nc.dram_tensor(name, list(shape), dt, kind="ExternalInput").ap()

    def dscr(name, shape, dt):
        kind = "ExternalOutput" if (DEBUG and name in DEBUG_OUT) else "Internal"
        return nc.dram_tensor(name, list(shape), dt, kind=kind).ap()

    x_d = din("x", [S, D])
    win_d = din("w_in", [D, 10240])
    wbr_d = din("w_branch", [1024, D])
    wout_d = din("w_out", [D, D])
    wr_d = din("w_router", [D, NE])
    wg_d = din("w_gate", [NE * D, D])
    wu_d = din("w_up", [NE * D, D])
    wd_d = din("w_down", [NE * D, D])
    lam_d = din("lam4", [128, 256])
    nw_d = din("nw", [128, 1])
    ln_d = din("lnp", [128, 4 * D])
    ccA_d = din("ccA", [S, 128])
    ssA_d = din("ssA", [S, 128])
    ccB_d = din("ccB", [S, 128])
    ssB_d = din("ssB", [S, 128])
    msk_d = din("bandmask", [128, 384], BF16)
    idf_d = din("ident_f", [128, 128])
    idb_d = din("ident_b", [128, 128], BF16)
    tri_d = din("tri_b", [128, 128], BF16)
    eoff_d = din("eoff", [128, NT * NE])
    out_d = nc.dram_tensor("out", [S, D], F32, kind="ExternalOutput").ap()

    QT_d = dscr("QT_s", [16, 128, S], BF16)
    KT_d = dscr("KT_s", [16, 128, S], BF16)
    V_d = dscr("V_s", [S, D], BF16)
    oT_d = dscr("oT_s", [8, 128, S], BF16)
    x1_d = dscr("x1_s", [S, D], F32)
    x1b_d = dscr("x1b_s", [S, D], BF16)
    xin_d = dscr("xin_s", [NE * CAP, D], BF16)
    yo_d = dscr("yo_s", [NE * CAP, D], F32)
    aff_d = dscr("aff_s", [128, NT * NE], F32)
    G_d = dscr("G_s", [S, 2 * D], BF16)

    with es:
        kb = KB(nc, es)
        pe, act, dve, pool, sp = kb.pe, kb.act, kb.dve, kb.pool, kb.sp

        def sbuf(st, name, shape, dt):
            return Buf(st.enter_context(nc.sbuf_tensor("sb_" + name, list(shape), dt)), name)

        def psum(st, name, dt=F32):
            shape = [128, 512] if dt == F32 else [128, 1024]
            return Buf(st.enter_context(nc.psum_tensor("ps_" + name, shape, dt)), name)

        def mm(out_b, out_ap, l_b, l_ap, r_b, r_ap, start=True, stop=True):
            kb.op(pe, [l_b.res, r_b.res], [out_b.res],
                  lambda: nc.tensor.matmul(out_ap, l_ap, r_ap, start=start, stop=stop))

        def tr(out_b, out_ap, in_b, in_ap, id_b):
            kb.op(pe, [in_b.res, id_b.res], [out_b.res],
                  lambda: nc.tensor.transpose(out_ap, in_ap, id_b[:]))

        def ex(q, reads, writes, fn):
            kb.op(q, [b.res for b in reads], [b.res for b in writes], fn)

        bc_reg = nc.gpsimd.alloc_register("bc_reg")
        nc.gpsimd.reg_mov(bc_reg, NE * CAP - 1)
        dram = {n: Res(n) for n in ["xT", "QT", "KT", "V", "oT", "x1", "x1b", "xin", "yo", "G"]}

        gst = es
        ident_f = sbuf(gst, "ident_f", [128, 128], F32)
        ident_b = sbuf(gst, "ident_b", [128, 128], BF16)
        ones_b = sbuf(gst, "ones_b", [128, 128], BF16)
        ones_f = sbuf(gst, "ones_f", [128, 128], F32)
        lam4 = sbuf(gst, "lam4", [128, 256], F32)
        nw = sbuf(gst, "nw", [128, 1], F32)
        neglam = sbuf(gst, "neglam", [128, 1], F32)
        ltmp = sbuf(gst, "ltmp", [128, 128], F32)
        lsum = sbuf(gst, "lsum", [128, 2], F32)
        aff = sbuf(gst, "aff", [128, NT * NE], F32)
        idx_i = sbuf(gst, "idx_i", [128, NT * NE], I32)
        gm = sbuf(gst, "gm", [128, NT * NE], F32)
        kb.load(sp, ident_f.res, ident_f[:], idf_d[:, :])
        kb.load(sp, ident_b.res, ident_b[:], idb_d[:, :])
        kb.load(sp, lam4.res, lam4[:], lam_d[:, :])
        kb.load(sp, nw.res, nw[:], nw_d[:, :])
        ex(pool, [], [ones_b], lambda: nc.gpsimd.memset(ones_b[:], 1.0))
        ex(pool, [], [ones_f], lambda: nc.gpsimd.memset(ones_f[:], 1.0))
        ex(dve, [lam4], [ltmp], lambda: nc.vector.tensor_tensor(
            out=ltmp[:].rearrange("p (a c) -> p a c", a=2),
            in0=lam4[:].rearrange("p (a b c) -> p a b c", a=2, b=2)[:, :, 0, :],
            in1=lam4[:].rearrange("p (a b c) -> p a b c", a=2, b=2)[:, :, 1, :], op=ALU.mult))
        ex(dve, [ltmp], [lsum], lambda: nc.vector.reduce_sum(
            out=lsum[:], in_=ltmp[:].rearrange("p (a c) -> p a c", a=2), axis=AX.X))
        ex(act, [lsum], [lsum], lambda: nc.scalar.activation(out=lsum[:], in_=lsum[:], func=AF.Exp))
        ex(dve, [lsum], [neglam], lambda: nc.vector.tensor_tensor(
            out=neglam[:], in0=lsum[:, 1:2], in1=lsum[:, 0:1], op=ALU.subtract))
        ex(dve, [neglam], [neglam], lambda: nc.vector.tensor_scalar_add(
            out=neglam[:], in0=neglam[:], scalar1=-0.2))
        ex(dve, [nw], [nw], lambda: nc.vector.tensor_scalar_mul(out=nw[:], in0=nw[:], scalar1=0.8))

        with contextlib.ExitStack() as st:
            xT = sbuf(st, "A_xT", [128, 16, 2048], BF16)
            tabs = [sbuf(st, "A_tab%d" % i, [128, 16, 128], F32) for i in range(4)]
            xt = [sbuf(st, "A_xt%d" % i, [128, D], F32) for i in range(2)]
            wp = [sbuf(st, "A_wp%d" % i, [128, 16, 512], BF16) for i in range(2)]
            zt = [sbuf(st, "A_zt%d" % i, [128, 512], F32) for i in range(2)]
            uu = [sbuf(st, "A_uu%d" % i, [128, 128], F32) for i in range(2)]
            vv = [sbuf(st, "A_vv%d" % i, [128, 128], F32) for i in range(2)]
            zb = [sbuf(st, "A_zb%d" % i, [128, 512], BF16) for i in range(4)]
            qst = [sbuf(st, "A_qst%d" % i, [128, 4, 2048], BF16) for i in range(2)]
            pmm = [psum(st, "A_pm%d" % i) for i in range(4)]
            ptr = [psum(st, "A_pt%d" % i, BF16) for i in range(2)]
            tab_src = [ccA_d, ssA_d, ccB_d, ssB_d]
            nxt = 0
            for hf in range(2):
                t0 = hf * 2048
                for i in range(4):
                    kb.load(sp, tabs[i].res, tabs[i][:],
                            tab_src[i][t0:t0 + 2048, :].rearrange("(t p) c -> p t c", p=128))
                for tl in range(16):
                    xs = xt[tl % 2]
                    r0 = t0 + tl * 128
                    kb.load(sp, xs.res, xs[:], x_d[r0:r0 + 128, :])
                    for kk in range(4):
                        pb = pmm[kk]
                        for j in range(4):
                            k = kk * 4 + j
                            tr(pb, pb[:, j * 128:(j + 1) * 128], xs, xs[:, k * 128:(k + 1) * 128], ident_f)
                        dst = xT[:, kk * 4:(kk + 1) * 4, tl * 128:(tl + 1) * 128]
                        src = pb[:].rearrange("p (a b) -> p a b", a=4)
                        if kk % 2 == 0:
                            ex(act, [pb], [xT], lambda dst=dst, src=src: nc.scalar.copy(out=dst, in_=src))
                        else:
                            ex(dve, [pb], [xT], lambda dst=dst, src=src: nc.vector.tensor_copy(dst, src))
                NCB = 20

                def col_of(cb):
                    return cb * 512 if cb < 12 else 6144 + (cb - 12) * 512

                wsrcs = [win_d[:, col_of(cb):col_of(cb) + 512].rearrange("(k p) c -> p k c", p=128)
                         for cb in range(NCB)]
                wstream = Stream(kb, pool, wp, wsrcs)
                for cb in range(NCB):
                    w = wstream.get(cb)
                    is_v = cb in (6, 7, 8, 11)
                    is_g = cb >= 12
                    is_b = cb in (9, 10)
                    qs = qst[cb % 2]
                    pend = []

                    def flush_one():
                        z_b, tl_ = pend.pop(0)
                        pt = ptr[tl_ % 2]
                        for h in range(4):
                            tr(pt, pt[:, h * 128:(h + 1) * 128], z_b, z_b[:, h * 128:(h + 1) * 128], ident_b)
                        dst = qs[:, :, tl_ * 128:(tl_ + 1) * 128]
                        src = pt[:, 0:512].rearrange("p (a b) -> p a b", a=4)
                        ex(dve, [pt], [qs], lambda: nc.vector.tensor_copy(dst, src))

                    for tl in range(16):
                        pb = pmm[nxt % 4]
                        nxt += 1
                        for k in range(16):
                            mm(pb, pb[:], xT, xT[:, k, tl * 128:(tl + 1) * 128], w, w[:, k, :],
                               start=(k == 0), stop=(k == 15))
                        z_b = zb[tl % 4]
                        r0 = t0 + tl * 128
                        if is_v:
                            ex(act, [pb], [z_b], lambda: nc.scalar.copy(out=z_b[:], in_=pb[:]))
                            c0 = (cb - 6) * 512 if cb < 9 else 1536
                            kb.store(sp, z_b.res, V_d[r0:r0 + 128, c0:c0 + 512], z_b[:], [dram["V"]])
                            continue
                        if is_g:
                            ex(act, [pb], [z_b], lambda: nc.scalar.activation(out=z_b[:], in_=pb[:], func=AF.Sigmoid))
                            c0 = (cb - 12) * 512
                            kb.store(sp, z_b.res, G_d[r0:r0 + 128, c0:c0 + 512], z_b[:], [dram["G"]])
                            continue
                        z_t = zt[tl % 2]
                        u_t = uu[tl % 2]
                        v_t = vv[tl % 2]
                        ex(act, [pb], [z_t], lambda: nc.scalar.copy(out=z_t[:], in_=pb[:]))
                        if not is_b:
                            nb, bw, hh_ = 4, 128, 16
                            cc, ss = tabs[0], tabs[1]
                        else:
                            nb, bw, hh_ = 8, 64, 8
                            cc, ss = tabs[2], tabs[3]
                        z3 = z_t[:].rearrange("p (a b) -> p a b", a=nb)
                        zb3 = z_b[:].rearrange("p (a b) -> p a b", a=nb)
                        u3 = u_t[:].rearrange("p (a b) -> p a b", a=nb)
                        v3 = v_t[:].rearrange("p (a b) -> p a b", a=nb)
                        c3 = cc[:, tl, :].rearrange("p (a b) -> p a b", a=nb)
                        s3 = ss[:, tl, :].rearrange("p (a b) -> p a b", a=nb)
                        rw = 2 * hh_
                        ex(dve, [z_t, cc], [u_t], lambda: nc.vector.tensor_tensor(
                            out=u3, in0=z3[:, :, 0:rw], in1=c3, op=ALU.mult))
                        ex(dve, [z_t, ss], [v_t], lambda: nc.vector.tensor_tensor(
                            out=v3, in0=z3[:, :, 0:rw], in1=s3, op=ALU.mult))
                        ex(pool, [u_t, v_t], [z_b], lambda: nc.gpsimd.tensor_tensor(
                            out=zb3[:, :, 0:hh_], in0=u3[:, :, 0:hh_], in1=v3[:, :, hh_:rw], op=ALU.add))
                        ex(pool, [u_t, v_t], [z_b], lambda: nc.gpsimd.tensor_tensor(
                            out=zb3[:, :, hh_:rw], in0=u3[:, :, hh_:rw], in1=v3[:, :, 0:hh_], op=ALU.add))
                        ex(act, [z_t], [z_b], lambda: nc.scalar.copy(out=zb3[:, :, rw:bw], in_=z3[:, :, rw:bw]))
                        pend.append((z_b, tl))
                        if len(pend) > 2:
                            flush_one()
                    while pend:
                        flush_one()
                    if not (is_v or is_g):
                        if cb < 3:
                            tgt, key, h0 = QT_d, "QT", cb * 4
                        elif cb < 6:
                            tgt, key, h0 = KT_d, "KT", (cb - 3) * 4
                        elif cb == 9:
                            tgt, key, h0 = QT_d, "QT", 12
                        else:
                            tgt, key, h0 = KT_d, "KT", 12
                        for h in range(4):
                            kb.store(sp, qs.res, tgt[h0 + h, :, t0:t0 + 2048], qs[:, h, :], [dram[key]])
        kb.barrier()

        with contextlib.ExitStack() as st:
            msk = sbuf(st, "B_msk", [128, 384], BF16)
            kb.load(sp, msk.res, msk[:], msk_d[:, :])
            qT = [sbuf(st, "B_q%d" % i, [128, S], BF16) for i in range(2)]
            kT = [sbuf(st, "B_k%d" % i, [128, S], BF16) for i in range(2)]
            vs = [sbuf(st, "B_v%d" % i, [128, 32, 128], BF16) for i in range(2)]
            accn = sbuf(st, "B_accn", [128, S], F32)
            accd = sbuf(st, "B_accd", [128, S], F32)
            ob = sbuf(st, "B_ob", [128, S], BF16)
            pT = [sbuf(st, "B_pT%d" % i, [128, 384], BF16) for i in range(3)]
            ps_s = [psum(st, "B_ps%d" % i) for i in range(3)]
            ps_n = [psum(st, "B_pn%d" % i) for i in range(2)]
            ps_d = [psum(st, "B_pd%d" % i) for i in range(2)]
            it = 0
            blk = 0
            for hh in range(4):
                for g, r in enumerate((1, 4, 16)):
                    hd = g * 4 + hh
                    L = S // r
                    nj = L // 128
                    q_, k_, v_ = qT[it % 2], kT[it % 2], vs[it % 2]
                    it += 1
                    kb.load(sp, q_.res, q_[:], QT_d[hd, :, :], [dram["QT"]])
                    kb.load(sp, k_.res, k_[:], KT_d[hd, :, :], [dram["KT"]])
                    for ph in range(r):
                        kb.load(sp, v_.res, v_[:, ph * nj:(ph + 1) * nj, :],
                                V_d[ph:S:r, hd * 128:(hd + 1) * 128].rearrange("(j a) c -> a j c", a=128),
                                [dram["V"]])
                    nblk = r * nj
                    for b0 in range(0, nblk, 4):
                        pn = ps_n[(b0 // 4) % 2]
                        pd = ps_d[(b0 // 4) % 2]
                        for bi in range(b0, b0 + 4):
                            ph, j = bi // nj, bi % nj
                            kts = [kt for kt in (j - 1, j, j + 1) if 0 <= kt < nj]
                            lo = (kts[0] - j + 1) * 128
                            hi = (kts[-1] - j + 2) * 128
                            pss = ps_s[blk % 3]
                            p_t = pT[blk % 3]
                            blk += 1
                            qsl = q_[:, ph + r * 128 * j: ph + r * 128 * j + r * 127 + 1: r]
                            for kt in kts:
                                off = (kt - j + 1) * 128
                                ksl = k_[:, ph + r * 128 * kt: ph + r * 128 * kt + r * 127 + 1: r]
                                mm(pss, pss[:, off:off + 128], k_, ksl, q_, qsl)
                            ex(act, [pss], [p_t], lambda: nc.scalar.activation(
                                out=p_t[:, lo:hi], in_=pss[:, lo:hi], func=AF.Exp, scale=128.0 ** -0.5))
                            ex(pool, [p_t, msk], [p_t], lambda: nc.gpsimd.tensor_tensor(
                                out=p_t[:, lo:hi], in0=p_t[:, lo:hi], in1=msk[:, lo:hi], op=ALU.mult))
                            c0 = (bi - b0) * 128
                            for n_, kt in enumerate(kts):
                                off = (kt - j + 1) * 128
                                mm(pn, pn[:, c0:c0 + 128], v_, v_[:, ph * nj + kt, :], p_t, p_t[:, off:off + 128],
                                   start=(n_ == 0), stop=(n_ == len(kts) - 1))
                            for n_, kt in enumerate(kts):
                                off = (kt - j + 1) * 128
                                mm(pd, pd[:, c0:c0 + 128], ones_b, ones_b[:], p_t, p_t[:, off:off + 128],
                                   start=(n_ == 0), stop=(n_ == len(kts) - 1))
                        ph0, j0 = b0 // nj, b0 % nj
                        outs = []
                        for acc in (accn, accd):
                            av = acc[:].rearrange("p (m r) -> p r m", r=r)
                            if r <= 4:
                                outs.append((av[:, ph0, j0 * 128:j0 * 128 + 512], None))
                            else:
                                outs.append((av[:, ph0:ph0 + 2, :], 2))
                        for (dst, f), pb_, acc, q in ((outs[0], pn, accn, act), (outs[1], pd, accd, dve)):
                            src = pb_[:] if f is None else pb_[:].rearrange("p (f m) -> p f m", f=f)
                            if g == 0:
                                if q is act:
                                    ex(act, [pb_], [acc], lambda dst=dst, src=src: nc.scalar.copy(out=dst, in_=src))
                                else:
                                    ex(dve, [pb_], [acc], lambda dst=dst, src=src: nc.vector.tensor_copy(dst, src))
                            else:
                                ex(dve, [pb_, acc], [acc], lambda dst=dst, src=src: nc.vector.tensor_tensor(
                                    out=dst, in0=dst, in1=src, op=ALU.add))
                ex(dve, [accd], [accd], lambda: nc.vector.reciprocal(out=accd[:], in_=accd[:]))
                ex(dve, [accn, accd], [ob], lambda: nc.vector.tensor_tensor(
                    out=ob[:], in0=accn[:], in1=accd[:], op=ALU.mult))
                kb.store(sp, ob.res, oT_d[hh, :, :], ob[:], [dram["oT"]])
        kb.barrier()

        with contextlib.ExitStack() as st:
            qT = [sbuf(st, "C_q%d" % i, [128, S], BF16) for i in range(2)]
            kT = [sbuf(st, "C_k%d" % i, [128, S], BF16) for i in range(2)]
            vn = [sbuf(st, "C_v%d" % i, [128, 32, 128], BF16) for i in range(2)]
            ob = sbuf(st, "C_ob", [128, S], BF16)
            pT = [sbuf(st, "C_pT%d" % i, [128, 512], BF16) for i in range(4)]
            sacc = [[sbuf(st, "C_sa%d%d" % (i, c), [128, 512], F32) for c in range(2)] for i in range(2)]
            rr = [sbuf(st, "C_rr%d" % i, [128, 512], F32) for i in range(2)]
            aa = [sbuf(st, "C_aa%d" % i, [128, 512], F32) for i in range(2)]
            A_ = sbuf(st, "C_A", [128, 512], F32)
            sq = sbuf(st, "C_sq", [128, 512], F32)
            rs = sbuf(st, "C_rs", [128, 512], F32)
            ps_s = [psum(st, "C_ps%d" % i) for i in range(3)]
            ps_n = [[psum(st, "C_pn%d%d" % (i, c)) for c in range(2)] for i in range(2)]
            ps_q = psum(st, "C_pq")
            gi = 0
            nqb = 0
            for h in range(4):
                q_, k_, v_ = qT[h % 2], kT[h % 2], vn[h % 2]
                kb.load(sp, q_.res, q_[:], QT_d[12 + h, :, :], [dram["QT"]])
                kb.load(sp, k_.res, k_[:], KT_d[12 + h, :, :], [dram["KT"]])
                kb.load(sp, v_.res, v_[:],
                        V_d[:, 1536 + h * 128:1536 + (h + 1) * 128].rearrange("(t p) c -> p t c", p=128),
                        [dram["V"]])
                for qb in range(8):
                    pn = ps_n[nqb % 2]
                    sa = sacc[nqb % 2]
                    nqb += 1
                    steps = [(kt, c) for kt in range(32) for c in range(2)]

                    def qk(i):
                        kt, c = steps[i]
                        pss = ps_s[(gi + i) % 3]
                        mm(pss, pss[:], k_, k_[64 * c:64 * c + 64, kt * 128:(kt + 1) * 128],
                           q_, q_[64 * c:64 * c + 64, qb * 512:(qb + 1) * 512])

                    qk(0)
                    for i, (kt, c) in enumerate(steps):
                        if i + 1 < len(steps):
                            qk(i + 1)
                        pss = ps_s[(gi + i) % 3]
                        p_t = pT[(gi + i) % 4]
                        ex(act, [pss], [p_t], lambda: nc.scalar.activation(
                            out=p_t[:], in_=pss[:], func=AF.Exp, scale=0.125))
                        mm(pn[c], pn[c][:], v_, v_[:, kt, :], p_t, p_t[:], start=(kt == 0), stop=(kt == 31))
                        if c == 0:
                            if kt == 0:
                                ex(pool, [p_t], [sa[c]], lambda: nc.gpsimd.tensor_copy(sa[c][:], p_t[:]))
                            else:
                                ex(pool, [p_t, sa[c]], [sa[c]], lambda: nc.gpsimd.tensor_tensor(
                                    out=sa[c][:], in0=sa[c][:], in1=p_t[:], op=ALU.add))
                        else:
                            if kt == 0:
                                ex(dve, [p_t], [sa[c]], lambda: nc.vector.tensor_copy(sa[c][:], p_t[:]))
                            else:
                                ex(dve, [p_t, sa[c]], [sa[c]], lambda: nc.vector.tensor_tensor(
                                    out=sa[c][:], in0=sa[c][:], in1=p_t[:], op=ALU.add))
                    gi += len(steps)
                    for c in range(2):
                        mm(ps_q, ps_q[:], ones_f, ones_f[:], sa[c], sa[c][:])
                        ex(dve, [ps_q], [rr[c]], lambda: nc.vector.reciprocal(out=rr[c][:], in_=ps_q[:]))
                        ex(dve, [pn[c], rr[c]], [aa[c]], lambda: nc.vector.tensor_tensor(
                            out=aa[c][:], in0=pn[c][:], in1=rr[c][:], op=ALU.mult))
                    ex(dve, [aa[0], aa[1], neglam], [A_], lambda: nc.vector.scalar_tensor_tensor(
                        out=A_[:], in0=aa[1][:], scalar=neglam[:, 0:1], in1=aa[0][:], op0=ALU.mult, op1=ALU.add))
                    ex(act, [A_], [sq], lambda: nc.scalar.activation(out=sq[:], in_=A_[:], func=AF.Square))
                    mm(ps_q, ps_q[:], ones_f, ones_f[:], sq, sq[:])
                    ex(act, [ps_q], [rs], lambda: nc.scalar.activation(
                        out=rs[:], in_=ps_q[:], func=AF.Sqrt, bias=1e-5, scale=1.0 / 128.0))
                    ex(dve, [rs], [rs], lambda: nc.vector.reciprocal(out=rs[:], in_=rs[:]))
                    ex(dve, [A_, rs], [A_], lambda: nc.vector.tensor_tensor(out=A_[:], in0=A_[:], in1=rs[:], op=ALU.mult))
                    ex(dve, [A_, nw], [ob], lambda: nc.vector.tensor_scalar(
                        out=ob[:, qb * 512:(qb + 1) * 512], in0=A_[:], scalar1=nw[:, 0:1], scalar2=None, op0=ALU.mult))
                kb.store(sp, ob.res, oT_d[4 + h, :, :], ob[:], [dram["oT"]])
        kb.barrier()

        with contextlib.ExitStack() as st:
            lnp = sbuf(st, "D_lnp", [128, 2 * D], F32)
            kb.load(sp, lnp.res, lnp[:], ln_d[:, 0:2 * D])
            wr = sbuf(st, "D_wr", [128, 16, NE], F32)
            kb.load(sp, wr.res, wr[:], wr_d.rearrange("(k p) e -> p k e", p=128))
            wbr = sbuf(st, "D_wbr", [128, 8, D], BF16)
            wo = sbuf(st, "D_wo", [128, 16, D], BF16)
            for g in range(2):
                kb.load(pool, wbr.res, wbr[:, 4 * g:4 * g + 4, :],
                        wbr_d[g * 512:(g + 1) * 512, :].rearrange("(k p) c -> p k c", p=128))
            for i in range(4):
                kb.load(pool, wo.res, wo[:, 4 * i:4 * i + 4, :],
                        wout_d[i * 512:(i + 1) * 512, :].rearrange("(k p) c -> p k c", p=128))
            oTb = [sbuf(st, "D_oTb%d" % i, [128, 8, 512], BF16) for i in range(2)]
            Gt = [sbuf(st, "D_G%d" % i, [128, 2 * D], BF16) for i in range(2)]
            xts = [sbuf(st, "D_xt%d" % i, [128, 1, D], F32) for i in range(2)]
            tA = [sbuf(st, "D_tA%d" % i, [128, 512], F32) for i in range(2)]
            tB = [sbuf(st, "D_tB%d" % i, [128, 512], F32) for i in range(2)]
            mg = sbuf(st, "D_mg", [128, D], BF16)
            mT = sbuf(st, "D_mT", [128, 16, 128], BF16)
            x1b = [sbuf(st, "D_x1b%d" % i, [128, D], BF16) for i in range(1)]
            x1T = sbuf(st, "D_x1T", [128, 16, 128], F32)
            junk = Buf(x1T.t, "junkview")
            junk.res = x1T.res
            junk_ap = x1T[:].rearrange("p a b -> p (a b)")
            st1 = sbuf(st, "D_st1", [128, 8], F32)
            eb = sbuf(st, "D_eb", [128, NE], F32)
            pf = [psum(st, "D_p%d" % i) for i in range(6)]
            pbf = [psum(st, "D_pb%d" % i, BF16) for i in range(2)]
            npf = 0

            def prefetch(tt):
                if tt >= NT:
                    return
                if tt % 4 == 0:
                    ob_ = oTb[(tt // 4) % 2]
                    c0_ = tt * 128
                    kb.load(sp, ob_.res, ob_[:], oT_d[:, :, c0_:c0_ + 512].rearrange("k p t -> p k t"), [dram["oT"]])
                kb.load(sp, Gt[tt % 2].res, Gt[tt % 2][:], G_d[tt * 128:(tt + 1) * 128, :], [dram["G"]])
                kb.load(sp, xts[tt % 2].res, xts[tt % 2][:, 0, :], x_d[tt * 128:(tt + 1) * 128, :])

            prefetch(0)
            for tt in range(NT):
                prefetch(tt + 1)
                ti = tt % 4
                r0 = tt * 128
                ob_ = oTb[(tt // 4) % 2]
                G_ = Gt[tt % 2]
                xt_ = xts[tt % 2]
                hv = xt_[:, 0, :]
                for db in range(4):
                    pa = pf[npf % 6]
                    pb_ = pf[(npf + 1) % 6]
                    npf += 2
                    for g, pp in ((0, pa), (1, pb_)):
                        for c in range(4):
                            mm(pp, pp[:], ob_, ob_[:, 4 * g + c, ti * 128:(ti + 1) * 128],
                               wbr, wbr[:, 4 * g + c, db * 512:(db + 1) * 512], start=(c == 0), stop=(c == 3))
                    ta, tb_ = tA[db % 2], tB[db % 2]
                    ex(dve, [G_, pa], [ta], lambda: nc.vector.tensor_tensor(
                        out=ta[:], in0=G_[:, db * 512:(db + 1) * 512], in1=pa[:], op=ALU.mult))
                    ex(dve, [G_, pb_], [tb_], lambda: nc.vector.tensor_tensor(
                        out=tb_[:], in0=G_[:, D + db * 512:D + (db + 1) * 512], in1=pb_[:], op=ALU.mult))
                    ex(pool, [ta, tb_], [mg], lambda: nc.gpsimd.tensor_tensor(
                        out=mg[:, db * 512:(db + 1) * 512], in0=ta[:], in1=tb_[:], op=ALU.add))
                for half in range(2):
                    pt = pbf[half]
                    for j in range(8):
                        k = half * 8 + j
                        tr(pt, pt[:, j * 128:(j + 1) * 128], mg, mg[:, k * 128:(k + 1) * 128], ident_b)
                    dst = mT[:, half * 8:(half + 1) * 8, :]
                    src = pt[:].rearrange("p (a b) -> p a b", a=8)
                    if half == 0:
                        ex(act, [pt], [mT], lambda: nc.scalar.copy(out=dst, in_=src))
                    else:
                        ex(dve, [pt], [mT], lambda: nc.vector.tensor_copy(dst, src))
                for obk in range(4):
                    pm = pf[npf % 6]
                    npf += 1
                    for k in range(16):
                        mm(pm, pm[:], mT, mT[:, k, :], wo, wo[:, k, obk * 512:(obk + 1) * 512],
                           start=(k == 0), stop=(k == 15))
                    ex(dve, [xt_, pm], [xt_], lambda: nc.vector.scalar_tensor_tensor(
                        out=xt_[:, 0, obk * 512:(obk + 1) * 512], in0=xt_[:, 0, obk * 512:(obk + 1) * 512],
                        scalar=ALPHA, in1=pm[:], op0=ALU.mult, op1=ALU.add))
                layer_norm(nc, kb, ex, hv, xt_, junk, junk_ap, st1, lnp, 0)
                kb.store(sp, xt_.res, x1_d[r0:r0 + 128, :], hv, [dram["x1"]])
                xb_ = x1b[0]
                ex(act, [xt_], [xb_], lambda: nc.scalar.copy(out=xb_[:], in_=hv))
                kb.store(sp, xb_.res, x1b_d[r0:r0 + 128, :], xb_[:], [dram["x1b"]])
                for kk in range(4):
                    pb = pf[npf % 6]
                    npf += 1
                    for j in range(4):
                        k = kk * 4 + j
                        tr(pb, pb[:, j * 128:(j + 1) * 128], xt_, xt_[:, 0, k * 128:(k + 1) * 128], ident_f)
                    dst = x1T[:, kk * 4:(kk + 1) * 4, :]
                    src = pb[:].rearrange("p (a b) -> p a b", a=4)
                    if kk % 2 == 0:
                        ex(act, [pb], [x1T], lambda: nc.scalar.copy(out=dst, in_=src))
                    else:
                        ex(dve, [pb], [x1T], lambda: nc.vector.tensor_copy(dst, src))
                pl = pf[npf % 6]
                npf += 1
                for k in range(16):
                    mm(pl, pl[:, 0:NE], x1T, x1T[:, k, :], wr, wr[:, k, :], start=(k == 0), stop=(k == 15))
                ex(dve, [pl], [st1], lambda: nc.vector.reduce_max(out=st1[:, 4:5], in_=pl[:, 0:NE], axis=AX.X))
                ex(dve, [st1], [st1], lambda: nc.vector.tensor_scalar_mul(out=st1[:, 5:6], in0=st1[:, 4:5], scalar1=-1.0))
                ex(act, [pl, st1], [eb, st1], lambda: nc.scalar.activation(
                    out=eb[:], in_=pl[:, 0:NE], func=AF.Exp, bias=st1[:, 5:6], scale=1.0, accum_out=st1[:, 6:7]))
                ex(dve, [st1], [st1], lambda: nc.vector.reciprocal(out=st1[:, 7:8], in_=st1[:, 6:7]))
                ex(dve, [eb, st1], [aff], lambda: nc.vector.tensor_scalar(
                    out=aff[:, tt * NE:(tt + 1) * NE], in0=eb[:], scalar1=st1[:, 7:8], scalar2=None, op0=ALU.mult))
        kb.barrier()

        with contextlib.ExitStack() as st:
            lo = sbuf(st, "E_lo", [128, NE], F32)
            hi = sbuf(st, "E_hi", [128, NE], F32)
            mid = sbuf(st, "E_mid", [128, NE], F32)
            dd = sbuf(st, "E_dd", [128, NE], F32)
            sel = sbuf(st, "E_sel", [128, NE], F32)
            cmp_ = sbuf(st, "E_cmp", [128, NT * NE], F32)
            cnt = sbuf(st, "E_cnt", [128, NE], F32)
            mkb = sbuf(st, "E_mkb", [128, NT * NE], BF16)
            tri = sbuf(st, "E_tri", [128, 128], BF16)
            eoff = sbuf(st, "E_eoff", [128, NT * NE], F32)
            tt_ = sbuf(st, "E_tt", [128, NT * NE], F32)
            cum = sbuf(st, "E_cum", [128, NT * NE], F32)
            pos = sbuf(st, "E_pos", [128, NT * NE], F32)
            val = sbuf(st, "E_val", [128, NT * NE], F32)
            pc = psum(st, "E_pc")
            pw = psum(st, "E_pw")
            ptt = psum(st, "E_pt")
            kb.load(sp, tri.res, tri[:], tri_d[:, :])
            kb.load(sp, eoff.res, eoff[:], eoff_d[:, :])
            ex(dve, [], [lo], lambda: nc.vector.memset(lo[:], 0.0))
            ex(dve, [], [hi], lambda: nc.vector.memset(hi[:], 1.0))
            aff3 = aff[:].rearrange("p (t e) -> p t e", e=NE)

            def bc(b):
                a = b[:]
                return bass.AP(a.tensor, a.offset, [list(a.ap[0]), [0, NT], list(a.ap[1])])

            for _ in range(34):
                ex(dve, [lo, hi], [mid], lambda: nc.vector.tensor_tensor(out=mid[:], in0=lo[:], in1=hi[:], op=ALU.add))
                ex(dve, [mid], [mid], lambda: nc.vector.tensor_scalar_mul(out=mid[:], in0=mid[:], scalar1=0.5))
                ex(dve, [aff, mid], [cmp_], lambda: nc.vector.tensor_tensor(
                    out=cmp_[:].rearrange("p (t e) -> p t e", e=NE), in0=aff3, in1=bc(mid), op=ALU.is_gt))
                ex(dve, [cmp_], [cnt], lambda: nc.vector.reduce_sum(
                    out=cnt[:], in_=cmp_[:].rearrange("p (t e) -> p e t", e=NE), axis=AX.X))
                mm(pc, pc[:, 0:NE], ones_f, ones_f[:], cnt, cnt[:])
                ex(dve, [pc], [sel], lambda: nc.vector.tensor_scalar(
                    out=sel[:], in0=pc[:, 0:NE], scalar1=float(CAP) - 0.5, scalar2=None, op0=ALU.is_gt))
                ex(dve, [mid, lo], [dd], lambda: nc.vector.tensor_tensor(out=dd[:], in0=mid[:], in1=lo[:], op=ALU.subtract))
                ex(dve, [dd, sel], [dd], lambda: nc.vector.tensor_tensor(out=dd[:], in0=dd[:], in1=sel[:], op=ALU.mult))
                ex(dve, [dd, lo], [lo], lambda: nc.vector.tensor_tensor(out=lo[:], in0=lo[:], in1=dd[:], op=ALU.add))
                ex(dve, [mid, hi], [dd], lambda: nc.vector.tensor_tensor(out=dd[:], in0=hi[:], in1=mid[:], op=ALU.subtract))
                ex(dve, [dd, sel], [dd], lambda: nc.vector.tensor_tensor(out=dd[:], in0=dd[:], in1=sel[:], op=ALU.mult))
                ex(dve, [dd, mid], [hi], lambda: nc.vector.tensor_tensor(out=hi[:], in0=mid[:], in1=dd[:], op=ALU.add))
            ex(dve, [aff, lo], [cmp_], lambda: nc.vector.tensor_tensor(
                out=cmp_[:].rearrange("p (t e) -> p t e", e=NE), in0=aff3, in1=bc(lo), op=ALU.is_gt))
            ex(act, [cmp_], [mkb], lambda: nc.scalar.copy(out=mkb[:], in_=cmp_[:]))
            mm(pw, pw[:], tri, tri[:], mkb, mkb[:])
            mm(ptt, ptt[:], ones_b, ones_b[:], mkb, mkb[:])
            ex(act, [ptt], [tt_], lambda: nc.scalar.copy(out=tt_[:], in_=ptt[:]))
            ex(dve, [], [cum], lambda: nc.vector.memset(cum[:, 0:NE], 0.0))
            for t in range(1, NT):
                ex(dve, [cum, tt_], [cum], lambda t=t: nc.vector.tensor_tensor(
                    out=cum[:, t * NE:(t + 1) * NE], in0=cum[:, (t - 1) * NE:t * NE],
                    in1=tt_[:, (t - 1) * NE:t * NE], op=ALU.add))
            ex(dve, [pw, cum], [pos], lambda: nc.vector.tensor_tensor(out=pos[:], in0=cum[:], in1=pw[:], op=ALU.add))
            ex(dve, [pos], [val], lambda: nc.vector.tensor_scalar(
                out=val[:], in0=pos[:], scalar1=float(CAP) - 0.5, scalar2=None, op0=ALU.is_lt))
            ex(dve, [val, cmp_], [val], lambda: nc.vector.tensor_tensor(out=val[:], in0=val[:], in1=cmp_[:], op=ALU.mult))
            ex(dve, [aff, val], [gm], lambda: nc.vector.tensor_tensor(out=gm[:], in0=aff[:], in1=val[:], op=ALU.mult))
            ex(dve, [pos, eoff], [pos], lambda: nc.vector.tensor_tensor(out=pos[:], in0=pos[:], in1=eoff[:], op=ALU.add))
            ex(dve, [pos], [pos], lambda: nc.vector.tensor_scalar_add(out=pos[:], in0=pos[:], scalar1=-BIG))
            ex(dve, [pos, val], [pos], lambda: nc.vector.tensor_tensor(out=pos[:], in0=pos[:], in1=val[:], op=ALU.mult))
            ex(dve, [pos], [pos], lambda: nc.vector.tensor_scalar_add(out=pos[:], in0=pos[:], scalar1=BIG))
            ex(dve, [pos], [idx_i], lambda: nc.vector.tensor_copy(idx_i[:], pos[:]))
            if DEBUG:
                kb.store(sp, aff.res, aff_d[:, :], aff[:])
        kb.barrier()

        with contextlib.ExitStack() as st:
            xb = [sbuf(st, "F_xb%d" % i, [128, D], BF16) for i in range(3)]
            for tt in range(NT):
                b = xb[tt % 3]
                kb.load(sp, b.res, b[:], x1b_d[tt * 128:(tt + 1) * 128, :], [dram["x1b"]])
                for e in range(NE):
                    col = tt * NE + e
                    if b.res.dout is None:
                        b.res.dout = kb.dsem()
                    kb.dma(pool, None, None, [b.res, idx_i.res], [], b.res.dout,
                           fn=lambda b=b, col=col: nc.gpsimd.indirect_dma_start(
                               out=xin_d[:, :], out_offset=bass.IndirectOffsetOnAxis(ap=idx_i[:, col:col + 1], axis=0),
                               in_=b[:], in_offset=None, bounds_check=bc_reg, oob_is_err=False))
        kb.barrier()

        with contextlib.ExitStack() as st:
            xi = [sbuf(st, "G_xi%d" % i, [128, 4, D], BF16) for i in range(2)]
            xiT = [sbuf(st, "G_xiT%d" % i, [128, 16, 512], BF16) for i in range(2)]
            hT = sbuf(st, "G_hT", [128, 16, 512], BF16)
            wpc = [sbuf(st, "G_wp%d" % i, [128, 16, 512], BF16) for i in range(5)]
            sgl = [sbuf(st, "G_sg%d" % i, [128, 512], F32) for i in range(2)]
            ost = [sbuf(st, "G_os%d" % i, [128, 512], F32) for i in range(3)]
            ptr = [psum(st, "G_pt%d" % i, BF16) for i in range(2)]
            pgu = [psum(st, "G_pg%d" % i) for i in range(4)]
            pdn = [psum(st, "G_pd%d" % i) for i in range(2)]
            nos = 0
            npd = 0

            def piece(src_d, e, cblk):
                return src_d[e * D:(e + 1) * D, cblk * 512:(cblk + 1) * 512].rearrange("(k p) c -> p k c", p=128)

            wsrcs = []
            for e in range(NE):
                for fb in range(4):
                    wsrcs.append(piece(wg_d, e, fb))
                    wsrcs.append(piece(wu_d, e, fb))
                for db in range(4):
                    wsrcs.append(piece(wd_d, e, db))
            wstream = Stream(kb, pool, wpc, wsrcs, hold=2)

            def load_x(e):
                x_ = xi[e % 2]
                kb.load(sp, x_.res, x_[:], xin_d[e * CAP:(e + 1) * CAP, :].rearrange("(s p) d -> p s d", p=128))

            def transposes(e):
                x_ = xi[e % 2]
                xt_ = xiT[e % 2]
                for kp in range(8):
                    pt = ptr[kp % 2]
                    for k2 in range(2):
                        k = kp * 2 + k2
                        for s_ in range(4):
                            tr(pt, pt[:, (k2 * 4 + s_) * 128:(k2 * 4 + s_ + 1) * 128], x_,
                               x_[:, s_, k * 128:(k + 1) * 128], ident_b)
                    dst = xt_[:, kp * 2:kp * 2 + 2, :]
                    src = pt[:].rearrange("p (a b) -> p a b", a=2)
                    if kp % 2 == 0:
                        ex(act, [pt], [xt_], lambda: nc.scalar.copy(out=dst, in_=src))
                    else:
                        ex(dve, [pt], [xt_], lambda: nc.vector.tensor_copy(dst, src))

            load_x(0)
            wstream.get(0)
            transposes(0)
            for e in range(NE):
                if e + 1 < NE:
                    load_x(e + 1)
                xt_ = xiT[e % 2]
                for fb in range(4):
                    wg = wstream.get(e * 12 + fb * 2)
                    wu = wstream.get(e * 12 + fb * 2 + 1)
                    for fi in range(4):
                        fc = fb * 4 + fi
                        pg = pgu[2 * (fc % 2)]
                        pu = pgu[2 * (fc % 2) + 1]
                        for k in range(16):
                            mm(pg, pg[:], wg, wg[:, k, fi * 128:(fi + 1) * 128], xt_, xt_[:, k, :],
                               start=(k == 0), stop=(k == 15))
                        for k in range(16):
                            mm(pu, pu[:], wu, wu[:, k, fi * 128:(fi + 1) * 128], xt_, xt_[:, k, :],
                               start=(k == 0), stop=(k == 15))
                        s_g = sgl[fc % 2]
                        ex(act, [pg], [s_g], lambda: nc.scalar.activation(out=s_g[:], in_=pg[:], func=AF.Silu))
                        ex(dve, [s_g, pu], [hT], lambda: nc.vector.tensor_tensor(
                            out=hT[:, fc, :], in0=s_g[:], in1=pu[:], op=ALU.mult))
                if e + 1 < NE:
                    transposes(e + 1)
                for db in range(4):
                    wd = wstream.get(e * 12 + 8 + db)
                    for s_ in range(4):
                        pd = pdn[npd % 2]
                        npd += 1
                        for fc in range(16):
                            mm(pd, pd[:], hT, hT[:, fc, s_ * 128:(s_ + 1) * 128], wd, wd[:, fc, :],
                               start=(fc == 0), stop=(fc == 15))
                        o_ = ost[nos % 3]
                        nos += 1
                        if s_ % 2 == 0:
                            ex(act, [pd], [o_], lambda: nc.scalar.copy(out=o_[:], in_=pd[:]))
                        else:
                            ex(dve, [pd], [o_], lambda: nc.vector.tensor_copy(o_[:], pd[:]))
                        r0 = e * CAP + s_ * 128
                        kb.store(sp, o_.res, yo_d[r0:r0 + 128, db * 512:(db + 1) * 512], o_[:], [dram["yo"]])
        kb.barrier()

        with contextlib.ExitStack() as st:
            lnp = sbuf(st, "H_lnp", [128, 2 * D], F32)
            kb.load(sp, lnp.res, lnp[:], ln_d[:, 2 * D:4 * D])
            xa = [sbuf(st, "H_xa%d" % i, [128, 1, D], F32) for i in range(2)]
            gb = [sbuf(st, "H_g%d" % i, [128, D], F32) for i in range(4)]
            junk = sbuf(st, "H_junk", [128, D], F32)
            st1 = sbuf(st, "H_st1", [128, 8], F32)
            for g_ in gb:
                ex(pool, [], [g_], lambda g_=g_: nc.gpsimd.memset(g_[:], 0.0))
            ng = 0
            for tt in range(NT):
                a_ = xa[tt % 2]
                r0 = tt * 128
                kb.load(sp, a_.res, a_[:, 0, :], x1_d[r0:r0 + 128, :], [dram["x1"]])
                ex(act, [a_], [a_], lambda a_=a_: nc.scalar.mul(out=a_[:, 0, :], in_=a_[:, 0, :], mul=ALPHA))
                for e in range(NE):
                    col = tt * NE + e
                    g_ = gb[ng % 4]
                    ng += 1
                    if g_.res.din is None:
                        g_.res.din = kb.dsem()
                    kb.dma(pool, None, None, [idx_i.res, dram["yo"]], [g_.res], g_.res.din,
                           fn=lambda g_=g_, col=col: nc.gpsimd.indirect_dma_start(
                               out=g_[:], out_offset=None, in_=yo_d[:, :],
                               in_offset=bass.IndirectOffsetOnAxis(ap=idx_i[:, col:col + 1], axis=0),
                               bounds_check=bc_reg, oob_is_err=False))
                    ex(dve, [g_, gm, a_], [a_], lambda g_=g_, a_=a_, col=col: nc.vector.scalar_tensor_tensor(
                        out=a_[:, 0, :], in0=g_[:], scalar=gm[:, col:col + 1], in1=a_[:, 0, :], op0=ALU.mult, op1=ALU.add))
                layer_norm(nc, kb, ex, a_[:, 0, :], a_, junk, junk[:], st1, lnp, 0)
                kb.store(sp, a_.res, out_d[r0:r0 + 128, :], a_[:, 0, :])
        kb.barrier()
        nc._marks = kb.marks
    return nc


def layer_norm(nc, kb, ex, hv, hb, junk, junk_ap, st1, lnp, off):
    dve, act, pool = kb.dve, kb.act, kb.pool
    ex(dve, [hb], [st1], lambda: nc.vector.reduce_sum(out=st1[:, 0:1], in_=hv, axis=AX.X))
    ex(dve, [st1], [st1], lambda: nc.vector.tensor_scalar_mul(out=st1[:, 1:2], in0=st1[:, 0:1], scalar1=-1.0 / D))
    ex(dve, [hb, st1], [hb], lambda: nc.vector.tensor_scalar(
        out=hv, in0=hv, scalar1=st1[:, 1:2], scalar2=None, op0=ALU.add))
    ex(act, [hb], [junk, st1], lambda: nc.scalar.activation(
        out=junk_ap, in_=hv, func=AF.Square, accum_out=st1[:, 2:3]))
    ex(act, [st1], [st1], lambda: nc.scalar.activation(
        out=st1[:, 3:4], in_=st1[:, 2:3], func=AF.Sqrt, bias=LN_EPS, scale=1.0 / D))
    ex(dve, [st1], [st1], lambda: nc.vector.reciprocal(out=st1[:, 3:4], in_=st1[:, 3:4]))
    ex(dve, [hb, st1, lnp], [hb], lambda: nc.vector.scalar_tensor_tensor(
        out=hv, in0=hv, scalar=st1[:, 3:4], in1=lnp[:, off:off + D], op0=ALU.mult, op1=ALU.mult))
    ex(pool, [hb, lnp], [hb], lambda: nc.gpsimd.tensor_tensor(
        out=hv, in0=hv, in1=lnp[:, off + D:off + 2 * D], op=ALU.add))


DEBUG_OUT = ()


def _consts():
    import ml_dtypes
    bf = ml_dtypes.bfloat16
    pos = np.arange(S, dtype=np.float32)
    c = {}
    for nm, half, rot, nblk in (("A", 16, 32, 4), ("B", 8, 16, 8)):
        inv = (np.float32(500000.0) ** (-2.0 * np.arange(half, dtype=np.float32) / rot)).astype(np.float32)
        ang = (pos[:, None] * inv[None, :]).astype(np.float32)
        cs, sn = np.cos(ang).astype(np.float32), np.sin(ang).astype(np.float32)
        cc = np.concatenate([cs, cs], axis=1)
        ss = np.concatenate([sn, -sn], axis=1)
        c["cc" + nm] = np.ascontiguousarray(np.tile(cc, (1, nblk)))
        c["ss" + nm] = np.ascontiguousarray(np.tile(ss, (1, nblk)))
    a = np.arange(128)[:, None]
    b = np.arange(128)[None, :]
    m = np.concatenate([(a - b >= 64), (np.abs(a - b) <= 64), (b - a >= 64)], axis=1)
    c["bandmask"] = m.astype(np.float32).astype(bf)
    c["ident_f"] = np.eye(128, dtype=np.float32)
    c["ident_b"] = np.eye(128, dtype=np.float32).astype(bf)
    c["tri_b"] = (a < b).astype(np.float32).astype(bf)
    eo = np.tile((np.arange(NE, dtype=np.float32) * CAP)[None, :], (128, NT))
    c["eoff"] = np.ascontiguousarray(eo.astype(np.float32))
    return c


_CACHE = {}


def kernel(x, w_in, lambda_q1, lambda_k1, lambda_q2, lambda_k2, diff_norm_w, w_branch, w_out,
           ln1_g, ln1_b, w_router, w_gate, w_up, w_down, ln2_g, ln2_b):
    f = lambda a: np.ascontiguousarray(np.asarray(a, dtype=np.float32))
    x = f(x)
    if "nc" not in _CACHE:
        _CACHE["nc"] = build_program()
        _CACHE["c"] = _consts()
    nc = _CACHE["nc"]
    c = _CACHE["c"]
    lam4 = np.concatenate([f(lambda_q1)[0], f(lambda_k1)[0], f(lambda_q2)[0], f(lambda_k2)[0]])[None, :]
    lnp = np.concatenate([f(ln1_g)[0], f(ln1_b)[0], f(ln2_g)[0], f(ln2_b)[0]])[None, :]
    shared = {
        "w_in": f(w_in)[0], "w_branch": f(w_branch)[0].reshape(1024, D), "w_out": f(w_out)[0],
        "w_router": f(w_router)[0], "w_gate": f(w_gate)[0].reshape(NE * D, D),
        "w_up": f(w_up)[0].reshape(NE * D, D), "w_down": f(w_down)[0].reshape(NE * D, D),
        "lam4": np.ascontiguousarray(np.broadcast_to(lam4, (128, 256))),
        "nw": np.ascontiguousarray(f(diff_norm_w)[0].reshape(128, 1)),
        "lnp": np.ascontiguousarray(np.broadcast_to(lnp, (128, 4 * D))),
    }
    shared.update(c)
    nb = x.shape[0]
    in_maps = []
    for core in range(8):
        m = dict(shared)
        m["x"] = np.ascontiguousarray(x[(core // 2) % nb])
        in_maps.append(m)
    res = run_bass_kernel_spmd(nc, in_maps, core_ids=list(range(8)))
    out = np.stack([np.asarray(res.results[2 * b]["out"], dtype=np.float32) for b in range(nb)], axis=0)
    return out
```

```python
import contextlib
import numpy as np
import concourse.bass as bass
import concourse.mybir as mybir
from concourse.bass_utils import run_bass_kernel_spmd

F32 = mybir.dt.float32
BF16 = mybir.dt.bfloat16
I32 = mybir.dt.int32
ALU = mybir.AluOpType
AF = mybir.ActivationFunctionType
AX = mybir.AxisListType

S = 4096
D = 2048
NT = S // 128
NE = 16
CAP = 512
ALPHA = 2.0 ** 0.25
LN_EPS = 1e-5
BIG = float(1 << 20)
DEBUG = False


class Ev:
    __slots__ = ("sem", "val")

    def __init__(self, sem, val):
        self.sem = sem
        self.val = val


class DSem:
    def __init__(self, h):
        self.h = h
        self.n = 0


class Res:
    def __init__(self, name):
        self.name = name
        self.w = None
        self.r = {}
        self.din = None
        self.dout = None


class Q:
    def __init__(self, eng, sem, is_pe=False):
        self.eng = eng
        self.sem = sem
        self.n = 0
        self.known = {}
        self.is_pe = is_pe

    def wait(self, ev):
        if ev is None:
            return
        if ev.sem is self.sem and self.is_pe:
            return
        k = id(ev.sem)
        if self.known.get(k, 0) >= ev.val:
            return
        self.eng.wait_ge(ev.sem, ev.val)
        self.known[k] = ev.val


class KB:
    def __init__(self, nc, es):
        self.nc = nc
        self.es = es
        self.nsem = 0
        self.pe = Q(nc.tensor, self.newsem("q_pe"), True)
        self.act = Q(nc.scalar, self.newsem("q_act"))
        self.dve = Q(nc.vector, self.newsem("q_dve"))
        self.pool = Q(nc.gpsimd, self.newsem("q_pool"))
        self.sp = Q(nc.sync, self.newsem("q_sp"))
        self.queues = [self.pe, self.act, self.dve, self.pool, self.sp]
        self.dsems = []

    def newsem(self, name):
        self.nsem += 1
        return self.es.enter_context(self.nc.semaphore(name))

    def dsem(self):
        free = getattr(self, "free_dsems", None)
        if free:
            return free.pop()
        d = DSem(self.newsem("d%d" % len(self.dsems)))
        self.dsems.append(d)
        return d

    def dma_group(self, q, fns, reads, writes, ds):
        self._pre(q, reads, writes)
        for fn in fns:
            inst = fn()
            ds.n += 16
            inst.then_inc(ds.h, 16)
        self._post(Ev(ds.h, ds.n), reads, writes)

    def _pre(self, q, reads, writes):
        for r in reads:
            q.wait(r.w)
        for w in writes:
            q.wait(w.w)
            for ev in w.r.values():
                q.wait(ev)

    def _post(self, ev, reads, writes):
        for r in reads:
            r.r[id(ev.sem)] = ev
        for w in writes:
            w.w = ev
            w.r = {}

    def op(self, q, reads, writes, fn):
        self._pre(q, reads, writes)
        inst = fn()
        q.n += 1
        inst.then_inc(q.sem, 1)
        self._post(Ev(q.sem, q.n), reads, writes)

    def dma(self, q, out, in_, reads, writes, ds, fn=None):
        self._pre(q, reads, writes)
        if fn is None:
            inst = q.eng.dma_start(out=out, in_=in_)
        else:
            inst = fn()
        ds.n += 16
        inst.then_inc(ds.h, 16)
        self._post(Ev(ds.h, ds.n), reads, writes)

    def load(self, q, res, out, in_, src=()):
        if res.din is None:
            res.din = self.dsem()
        self.dma(q, out, in_, list(src), [res], res.din)

    def store(self, q, res, out, in_, dst=()):
        if res.dout is None:
            res.dout = self.dsem()
        self.dma(q, out, in_, [res], list(dst), res.dout)

    def barrier(self):
        self.marks = getattr(self, "marks", [])
        self.marks.append({"pe": self.pe.n, "act": self.act.n, "dve": self.dve.n, "pool": self.pool.n})
        for q in self.queues:
            for p in self.queues:
                if p is not q and p.n > 0:
                    q.wait(Ev(p.sem, p.n))
            for d in self.dsems:
                if d.n > 0:
                    q.wait(Ev(d.h, d.n))
        self.free_dsems = list(self.dsems)


class Stream:
    def __init__(self, kb, q, bufs, srcs, hold=1):
        self.kb, self.q, self.bufs, self.srcs = kb, q, bufs, srcs
        self.n = 0
        self.hold = hold

    def get(self, i):
        last = min(i + len(self.bufs) - self.hold, len(self.srcs) - 1)
        while self.n <= last:
            b = self.bufs[self.n % len(self.bufs)]
            self.kb.load(self.q, b.res, b[:], self.srcs[self.n])
            self.n += 1
        return self.bufs[i % len(self.bufs)]


class Buf:
    def __init__(self, t, name):
        self.t = t
        self.res = Res(name)

    def __getitem__(self, k):
        return self.t[k]


def build_program():
    nc = bass.Bass("TRN2", target_bir_lowering=False)
    es = contextlib.ExitStack()

    def din(name, shape, dt=F32):
        return nc.dram_tensor(name, list(shape), dt, kind="ExternalInput").ap()

    def dscr(name, shape, dt):
        kind = "ExternalOutput" if (DEBUG and name in DEBUG_OUT) else "Internal"
        return nc.dram_tensor(name, list(shape), dt, kind=kind).ap()

    x_d = din("x", [S, D])
    win_d = din("w_in", [D, 10240])
    wbr_d = din("w_branch", [1024, D])
    wout_d = din("w_out", [D, D])
    wr_d = din("w_router", [D, NE])
    wg_d = din("w_gate", [NE * D, D])
    wu_d = din("w_up", [NE * D, D])
    wd_d = din("w_down", [NE * D, D])
    lam_d = din("lam4", [128, 256])
    nw_d = din("nw", [128, 1])
    ln_d = din("lnp", [128, 4 * D])
    ccA_d = din("ccA", [S, 128])
    ssA_d = din("ssA", [S, 128])
    ccB_d = din("ccB", [S, 128])
    ssB_d = din("ssB", [S, 128])
    msk_d = din("bandmask", [128, 384], BF16)
    idf_d = din("ident_f", [128, 128])
    idb_d = din("ident_b", [128, 128], BF16)
    tri_d = din("tri_b", [128, 128], BF16)
    eoff_d = din("eoff", [128, NT * NE])
    out_d = nc.dram_tensor("out", [S, D], F32, kind="ExternalOutput").ap()

    QT_d = dscr("QT_s", [16, 128, S], BF16)
    KT_d = dscr("KT_s", [16, 128, S], BF16)
    V_d = dscr("V_s", [S, D], BF16)
    oT_d = dscr("oT_s", [8, 128, S], BF16)
    x1_d = dscr("x1_s", [S, D], F32)
    x1b_d = dscr("x1b_s", [S, D], BF16)
    xin_d = dscr("xin_s", [NE * CAP, D], BF16)
    yo_d = dscr("yo_s", [NE * CAP, D], BF16)
    aff_d = dscr("aff_s", [128, NT * NE], F32)
    G_d = dscr("G_s", [S, 2 * D], BF16)

    with es:
        kb = KB(nc, es)
        pe, act, dve, pool, sp = kb.pe, kb.act, kb.dve, kb.pool, kb.sp

        def sbuf(st, name, shape, dt):
            return Buf(st.enter_context(nc.sbuf_tensor("sb_" + name, list(shape), dt)), name)

        def psum(st, name, dt=F32):
            shape = [128, 512] if dt == F32 else [128, 1024]
            return Buf(st.enter_context(nc.psum_tensor("ps_" + name, shape, dt)), name)

        def mm(out_b, out_ap, l_b, l_ap, r_b, r_ap, start=True, stop=True):
            kb.op(pe, [l_b.res, r_b.res], [out_b.res],
                  lambda: nc.tensor.matmul(out_ap, l_ap, r_ap, start=start, stop=stop))

        def tr(out_b, out_ap, in_b, in_ap, id_b):
            kb.op(pe, [in_b.res, id_b.res], [out_b.res],
                  lambda: nc.tensor.transpose(out_ap, in_ap, id_b[:]))

        def ex(q, reads, writes, fn):
            kb.op(q, [b.res for b in reads], [b.res for b in writes], fn)

        bc_reg = nc.gpsimd.alloc_register("bc_reg")
        nc.gpsimd.reg_mov(bc_reg, NE * CAP - 1)
        dram = {n: Res(n) for n in ["xT", "QT", "KT", "V", "oT", "x1", "x1b", "xin", "yo", "G"]}

        gst = es
        ident_f = sbuf(gst, "ident_f", [128, 128], F32)
        ident_b = sbuf(gst, "ident_b", [128, 128], BF16)
        ones_b = sbuf(gst, "ones_b", [128, 128], BF16)
        ones_f = sbuf(gst, "ones_f", [128, 128], F32)
        lam4 = sbuf(gst, "lam4", [128, 256], F32)
        nw = sbuf(gst, "nw", [128, 1], F32)
        neglam = sbuf(gst, "neglam", [128, 1], F32)
        ltmp = sbuf(gst, "ltmp", [128, 128], F32)
        lsum = sbuf(gst, "lsum", [128, 2], F32)
        aff = sbuf(gst, "aff", [128, NT * NE], F32)
        idx_i = sbuf(gst, "idx_i", [128, NT * NE], I32)
        gm = sbuf(gst, "gm", [128, NT * NE], F32)
        kb.load(sp, ident_f.res, ident_f[:], idf_d[:, :])
        kb.load(sp, ident_b.res, ident_b[:], idb_d[:, :])
        kb.load(sp, lam4.res, lam4[:], lam_d[:, :])
        kb.load(sp, nw.res, nw[:], nw_d[:, :])
        ex(pool, [], [ones_b], lambda: nc.gpsimd.memset(ones_b[:], 1.0))
        ex(pool, [], [ones_f], lambda: nc.gpsimd.memset(ones_f[:], 1.0))
        ex(dve, [lam4], [ltmp], lambda: nc.vector.tensor_tensor(
            out=ltmp[:].rearrange("p (a c) -> p a c", a=2),
            in0=lam4[:].rearrange("p (a b c) -> p a b c", a=2, b=2)[:, :, 0, :],
            in1=lam4[:].rearrange("p (a b c) -> p a b c", a=2, b=2)[:, :, 1, :], op=ALU.mult))
        ex(dve, [ltmp], [lsum], lambda: nc.vector.reduce_sum(
            out=lsum[:], in_=ltmp[:].rearrange("p (a c) -> p a c", a=2), axis=AX.X))
        ex(act, [lsum], [lsum], lambda: nc.scalar.activation(out=lsum[:], in_=lsum[:], func=AF.Exp))
        ex(dve, [lsum], [neglam], lambda: nc.vector.tensor_tensor(
            out=neglam[:], in0=lsum[:, 1:2], in1=lsum[:, 0:1], op=ALU.subtract))
        ex(dve, [neglam], [neglam], lambda: nc.vector.tensor_scalar_add(
            out=neglam[:], in0=neglam[:], scalar1=-0.2))
        ex(dve, [nw], [nw], lambda: nc.vector.tensor_scalar_mul(out=nw[:], in0=nw[:], scalar1=0.8))

        with contextlib.ExitStack() as st:
            xT = sbuf(st, "A_xT", [128, 16, 2048], BF16)
            tabs = [sbuf(st, "A_tab%d" % i, [128, 16, 128], F32) for i in range(4)]
            xt = [sbuf(st, "A_xt%d" % i, [128, D], F32) for i in range(2)]
            wp = [sbuf(st, "A_wp%d" % i, [128, 16, 512], BF16) for i in range(2)]
            zt = [sbuf(st, "A_zt%d" % i, [128, 512], F32) for i in range(2)]
            uu = [sbuf(st, "A_uu%d" % i, [128, 128], F32) for i in range(2)]
            vv = [sbuf(st, "A_vv%d" % i, [128, 128], F32) for i in range(2)]
            zb = [sbuf(st, "A_zb%d" % i, [128, 512], BF16) for i in range(4)]
            qst = [sbuf(st, "A_qst%d" % i, [128, 4, 2048], BF16) for i in range(2)]
            pmm = [psum(st, "A_pm%d" % i) for i in range(4)]
            ptr = [psum(st, "A_pt%d" % i, BF16) for i in range(2)]
            tab_src = [ccA_d, ssA_d, ccB_d, ssB_d]
            nxt = 0
            for hf in range(2):
                t0 = hf * 2048
                for i in range(4):
                    kb.load(sp, tabs[i].res, tabs[i][:],
                            tab_src[i][t0:t0 + 2048, :].rearrange("(t p) c -> p t c", p=128))
                for tl in range(16):
                    xs = xt[tl % 2]
                    r0 = t0 + tl * 128
                    kb.load(sp, xs.res, xs[:], x_d[r0:r0 + 128, :])
                    for kk in range(4):
                        pb = pmm[kk]
                        for j in range(4):
                            k = kk * 4 + j
                            tr(pb, pb[:, j * 128:(j + 1) * 128], xs, xs[:, k * 128:(k + 1) * 128], ident_f)
                        dst = xT[:, kk * 4:(kk + 1) * 4, tl * 128:(tl + 1) * 128]
                        src = pb[:].rearrange("p (a b) -> p a b", a=4)
                        if kk % 2 == 0:
                            ex(act, [pb], [xT], lambda dst=dst, src=src: nc.scalar.copy(out=dst, in_=src))
                        else:
                            ex(dve, [pb], [xT], lambda dst=dst, src=src: nc.vector.tensor_copy(dst, src))
                NCB = 20

                def col_of(cb):
                    return cb * 512 if cb < 12 else 6144 + (cb - 12) * 512

                wsrcs = [win_d[:, col_of(cb):col_of(cb) + 512].rearrange("(k p) c -> p k c", p=128)
                         for cb in range(NCB)]
                wstream = Stream(kb, pool, wp, wsrcs)
                for cb in range(NCB):
                    w = wstream.get(cb)
                    is_v = cb in (6, 7, 8, 11)
                    is_g = cb >= 12
                    is_b = cb in (9, 10)
                    qs = qst[cb % 2]
                    pend = []

                    def flush_one():
                        z_b, tl_ = pend.pop(0)
                        pt = ptr[tl_ % 2]
                        for h in range(4):
                            tr(pt, pt[:, h * 128:(h + 1) * 128], z_b, z_b[:, h * 128:(h + 1) * 128], ident_b)
                        dst = qs[:, :, tl_ * 128:(tl_ + 1) * 128]
                        src = pt[:, 0:512].rearrange("p (a b) -> p a b", a=4)
                        ex(dve, [pt], [qs], lambda: nc.vector.tensor_copy(dst, src))

                    for tl in range(16):
                        pb = pmm[nxt % 4]
                        nxt += 1
                        for k in range(16):
                            mm(pb, pb[:], xT, xT[:, k, tl * 128:(tl + 1) * 128], w, w[:, k, :],
                               start=(k == 0), stop=(k == 15))
                        z_b = zb[tl % 4]
                        r0 = t0 + tl * 128
                        if is_v:
                            ex(act, [pb], [z_b], lambda: nc.scalar.copy(out=z_b[:], in_=pb[:]))
                            c0 = (cb - 6) * 512 if cb < 9 else 1536
                            kb.store(sp, z_b.res, V_d[r0:r0 + 128, c0:c0 + 512], z_b[:], [dram["V"]])
                            continue
                        if is_g:
                            ex(act, [pb], [z_b], lambda: nc.scalar.activation(out=z_b[:], in_=pb[:], func=AF.Sigmoid))
                            c0 = (cb - 12) * 512
                            kb.store(sp, z_b.res, G_d[r0:r0 + 128, c0:c0 + 512], z_b[:], [dram["G"]])
                            continue
                        z_t = zt[tl % 2]
                        u_t = uu[tl % 2]
                        v_t = vv[tl % 2]
                        ex(act, [pb], [z_t], lambda: nc.scalar.copy(out=z_t[:], in_=pb[:]))
                        if not is_b:
                            nb, bw, hh_ = 4, 128, 16
                            cc, ss = tabs[0], tabs[1]
                        else:
                            nb, bw, hh_ = 8, 64, 8
                            cc, ss = tabs[2], tabs[3]
                        z3 = z_t[:].rearrange("p (a b) -> p a b", a=nb)
                        zb3 = z_b[:].rearrange("p (a b) -> p a b", a=nb)
                        u3 = u_t[:].rearrange("p (a b) -> p a b", a=nb)
                        v3 = v_t[:].rearrange("p (a b) -> p a b", a=nb)
                        c3 = cc[:, tl, :].rearrange("p (a b) -> p a b", a=nb)
                        s3 = ss[:, tl, :].rearrange("p (a b) -> p a b", a=nb)
                        rw = 2 * hh_
                        ex(dve, [z_t, cc], [u_t], lambda: nc.vector.tensor_tensor(
                            out=u3, in0=z3[:, :, 0:rw], in1=c3, op=ALU.mult))
                        ex(dve, [z_t, ss], [v_t], lambda: nc.vector.tensor_tensor(
                            out=v3, in0=z3[:, :, 0:rw], in1=s3, op=ALU.mult))
                        ex(pool, [u_t, v_t], [z_b], lambda: nc.gpsimd.tensor_tensor(
                            out=zb3[:, :, 0:hh_], in0=u3[:, :, 0:hh_], in1=v3[:, :, hh_:rw], op=ALU.add))
                        ex(pool, [u_t, v_t], [z_b], lambda: nc.gpsimd.tensor_tensor(
                            out=zb3[:, :, hh_:rw], in0=u3[:, :, hh_:rw], in1=v3[:, :, 0:hh_], op=ALU.add))
                        ex(act, [z_t], [z_b], lambda: nc.scalar.copy(out=zb3[:, :, rw:bw], in_=z3[:, :, rw:bw]))
                        pend.append((z_b, tl))
                        if len(pend) > 2:
                            flush_one()
                    while pend:
                        flush_one()
                    if not (is_v or is_g):
                        if cb < 3:
                            tgt, key, h0 = QT_d, "QT", cb * 4
                        elif cb < 6:
                            tgt, key, h0 = KT_d, "KT", (cb - 3) * 4
                        elif cb == 9:
                            tgt, key, h0 = QT_d, "QT", 12
                        else:
                            tgt, key, h0 = KT_d, "KT", 12
                        for h in range(4):
                            kb.store(sp, qs.res, tgt[h0 + h, :, t0:t0 + 2048], qs[:, h, :], [dram[key]])
        kb.barrier()

        with contextlib.ExitStack() as st:
            msk = sbuf(st, "B_msk", [128, 384], BF16)
            kb.load(sp, msk.res, msk[:], msk_d[:, :])
            qT = [sbuf(st, "B_q%d" % i, [128, S], BF16) for i in range(2)]
            kT = [sbuf(st, "B_k%d" % i, [128, S], BF16) for i in range(2)]
            vs = [sbuf(st, "B_v%d" % i, [128, 32, 128], BF16) for i in range(2)]
            accn = sbuf(st, "B_accn", [128, S], F32)
            accd = sbuf(st, "B_accd", [128, S], F32)
            ob = sbuf(st, "B_ob", [128, S], BF16)
            pT = [sbuf(st, "B_pT%d" % i, [128, 384], BF16) for i in range(3)]
            ps_s = [psum(st, "B_ps%d" % i) for i in range(3)]
            ps_n = [psum(st, "B_pn%d" % i) for i in range(2)]
            ps_d = [psum(st, "B_pd%d" % i) for i in range(2)]
            it = 0
            blk = 0
            for hh in range(4):
                for g, r in enumerate((1, 4, 16)):
                    hd = g * 4 + hh
                    L = S // r
                    nj = L // 128
                    q_, k_, v_ = qT[it % 2], kT[it % 2], vs[it % 2]
                    it += 1
                    kb.load(sp, q_.res, q_[:], QT_d[hd, :, :], [dram["QT"]])
                    kb.load(sp, k_.res, k_[:], KT_d[hd, :, :], [dram["KT"]])
                    for ph in range(r):
                        kb.load(sp, v_.res, v_[:, ph * nj:(ph + 1) * nj, :],
                                V_d[ph:S:r, hd * 128:(hd + 1) * 128].rearrange("(j a) c -> a j c", a=128),
                                [dram["V"]])
                    nblk = r * nj
                    for b0 in range(0, nblk, 4):
                        pn = ps_n[(b0 // 4) % 2]
                        pd = ps_d[(b0 // 4) % 2]
                        for bi in range(b0, b0 + 4):
                            ph, j = bi // nj, bi % nj
                            kts = [kt for kt in (j - 1, j, j + 1) if 0 <= kt < nj]
                            lo = (kts[0] - j + 1) * 128
                            hi = (kts[-1] - j + 2) * 128
                            pss = ps_s[blk % 3]
                            p_t = pT[blk % 3]
                            blk += 1
                            qsl = q_[:, ph + r * 128 * j: ph + r * 128 * j + r * 127 + 1: r]
                            for kt in kts:
                                off = (kt - j + 1) * 128
                                ksl = k_[:, ph + r * 128 * kt: ph + r * 128 * kt + r * 127 + 1: r]
                                mm(pss, pss[:, off:off + 128], k_, ksl, q_, qsl)
                            ex(act, [pss], [p_t], lambda: nc.scalar.activation(
                                out=p_t[:, lo:hi], in_=pss[:, lo:hi], func=AF.Exp, scale=128.0 ** -0.5))
                            ex(dve, [p_t, msk], [p_t], lambda: nc.vector.tensor_tensor(
                                out=p_t[:, lo:hi], in0=p_t[:, lo:hi], in1=msk[:, lo:hi], op=ALU.mult))
                            c0 = (bi - b0) * 128
                            for n_, kt in enumerate(kts):
                                off = (kt - j + 1) * 128
                                mm(pn, pn[:, c0:c0 + 128], v_, v_[:, ph * nj + kt, :], p_t, p_t[:, off:off + 128],
                                   start=(n_ == 0), stop=(n_ == len(kts) - 1))
                            for n_, kt in enumerate(kts):
                                off = (kt - j + 1) * 128
                                mm(pd, pd[:, c0:c0 + 128], ones_b, ones_b[:], p_t, p_t[:, off:off + 128],
                                   start=(n_ == 0), stop=(n_ == len(kts) - 1))
                        ph0, j0 = b0 // nj, b0 % nj
                        outs = []
                        for acc in (accn, accd):
                            av = acc[:].rearrange("p (m r) -> p r m", r=r)
                            if r <= 4:
                                outs.append((av[:, ph0, j0 * 128:j0 * 128 + 512], None))
                            else:
                                outs.append((av[:, ph0:ph0 + 2, :], 2))
                        for (dst, f), pb_, acc, q in ((outs[0], pn, accn, act), (outs[1], pd, accd, dve)):
                            src = pb_[:] if f is None else pb_[:].rearrange("p (f m) -> p f m", f=f)
                            if g == 0:
                                if q is act:
                                    ex(act, [pb_], [acc], lambda dst=dst, src=src: nc.scalar.copy(out=dst, in_=src))
                                else:
                                    ex(dve, [pb_], [acc], lambda dst=dst, src=src: nc.vector.tensor_copy(dst, src))
                            else:
                                ex(dve, [pb_, acc], [acc], lambda dst=dst, src=src: nc.vector.tensor_tensor(
                                    out=dst, in0=dst, in1=src, op=ALU.add))
                ex(dve, [accd], [accd], lambda: nc.vector.reciprocal(out=accd[:], in_=accd[:]))
                ex(dve, [accn, accd], [ob], lambda: nc.vector.tensor_tensor(
                    out=ob[:], in0=accn[:], in1=accd[:], op=ALU.mult))
                kb.store(sp, ob.res, oT_d[hh, :, :], ob[:], [dram["oT"]])
        kb.barrier()

        with contextlib.ExitStack() as st:
            qT = [sbuf(st, "C_q%d" % i, [128, S], BF16) for i in range(2)]
            kT = [sbuf(st, "C_k%d" % i, [128, S], BF16) for i in range(2)]
            vn = [sbuf(st, "C_v%d" % i, [128, 32, 128], BF16) for i in range(2)]
            ob = sbuf(st, "C_ob", [128, S], BF16)
            pT = [sbuf(st, "C_pT%d" % i, [128, 512], BF16) for i in range(4)]
            sacc = [[sbuf(st, "C_sa%d%d" % (i, c), [128, 512], F32) for c in range(2)] for i in range(2)]
            rr = [sbuf(st, "C_rr%d" % i, [128, 512], F32) for i in range(2)]
            aa = [sbuf(st, "C_aa%d" % i, [128, 512], F32) for i in range(2)]
            A_ = sbuf(st, "C_A", [128, 512], F32)
            sq = sbuf(st, "C_sq", [128, 512], F32)
            rs = sbuf(st, "C_rs", [128, 512], F32)
            ps_s = [psum(st, "C_ps%d" % i) for i in range(3)]
            ps_n = [[psum(st, "C_pn%d%d" % (i, c)) for c in range(2)] for i in range(1)]
            ps_d = psum(st, "C_pd")
            ps_q = psum(st, "C_pq")
            gi = 0
            nqb = 0
            for h in range(4):
                q_, k_, v_ = qT[h % 2], kT[h % 2], vn[h % 2]
                kb.load(sp, q_.res, q_[:], QT_d[12 + h, :, :], [dram["QT"]])
                kb.load(sp, k_.res, k_[:], KT_d[12 + h, :, :], [dram["KT"]])
                kb.load(sp, v_.res, v_[:],
                        V_d[:, 1536 + h * 128:1536 + (h + 1) * 128].rearrange("(t p) c -> p t c", p=128),
                        [dram["V"]])
                for qb in range(8):
                    pn = ps_n[0]
                    sa = sacc[nqb % 2]
                    nqb += 1
                    steps = [(kt, c) for kt in range(32) for c in range(2)]

                    def qk(i):
                        kt, c = steps[i]
                        pss = ps_s[(gi + i) % 3]
                        mm(pss, pss[:], k_, k_[64 * c:64 * c + 64, kt * 128:(kt + 1) * 128],
                           q_, q_[64 * c:64 * c + 64, qb * 512:(qb + 1) * 512])

                    qk(0)
                    for i, (kt, c) in enumerate(steps):
                        if i + 1 < len(steps):
                            qk(i + 1)
                        pss = ps_s[(gi + i) % 3]
                        p_t = pT[(gi + i) % 4]
                        ex(act, [pss], [p_t], lambda: nc.scalar.activation(
                            out=p_t[:], in_=pss[:], func=AF.Exp, scale=0.125))
                        mm(pn[c], pn[c][:], v_, v_[:, kt, :], p_t, p_t[:], start=(kt == 0), stop=(kt == 31))
                        if c == 0:
                            mm(ps_d, ps_d[:], ones_b, ones_b[:], p_t, p_t[:], start=(kt == 0), stop=(kt == 31))
                        else:
                            if kt == 0:
                                ex(dve, [p_t], [sa[c]], lambda: nc.vector.tensor_copy(sa[c][:], p_t[:]))
                            else:
                                ex(dve, [p_t, sa[c]], [sa[c]], lambda: nc.vector.tensor_tensor(
                                    out=sa[c][:], in0=sa[c][:], in1=p_t[:], op=ALU.add))
                    gi += len(steps)
                    for c in range(2):
                        if c == 0:
                            pden = ps_d
                        else:
                            mm(ps_q, ps_q[:], ones_f, ones_f[:], sa[c], sa[c][:])
                            pden = ps_q
                        ex(dve, [pden], [rr[c]], lambda: nc.vector.reciprocal(out=rr[c][:], in_=pden[:]))
                        ex(dve, [pn[c], rr[c]], [aa[c]], lambda: nc.vector.tensor_tensor(
                            out=aa[c][:], in0=pn[c][:], in1=rr[c][:], op=ALU.mult))
                    ex(dve, [aa[0], aa[1], neglam], [A_], lambda: nc.vector.scalar_tensor_tensor(
                        out=A_[:], in0=aa[1][:], scalar=neglam[:, 0:1], in1=aa[0][:], op0=ALU.mult, op1=ALU.add))
                    ex(act, [A_], [sq], lambda: nc.scalar.activation(out=sq[:], in_=A_[:], func=AF.Square))
                    mm(ps_q, ps_q[:], ones_f, ones_f[:], sq, sq[:])
                    ex(act, [ps_q], [rs], lambda: nc.scalar.activation(
                        out=rs[:], in_=ps_q[:], func=AF.Sqrt, bias=1e-5, scale=1.0 / 128.0))
                    ex(dve, [rs], [rs], lambda: nc.vector.reciprocal(out=rs[:], in_=rs[:]))
                    ex(dve, [A_, rs], [A_], lambda: nc.vector.tensor_tensor(out=A_[:], in0=A_[:], in1=rs[:], op=ALU.mult))
                    ex(dve, [A_, nw], [ob], lambda: nc.vector.tensor_scalar(
                        out=ob[:, qb * 512:(qb + 1) * 512], in0=A_[:], scalar1=nw[:, 0:1], scalar2=None, op0=ALU.mult))
                kb.store(sp, ob.res, oT_d[4 + h, :, :], ob[:], [dram["oT"]])
        kb.barrier()

        with contextlib.ExitStack() as st:
            lnp = sbuf(st, "D_lnp", [128, 2 * D], F32)
            kb.load(sp, lnp.res, lnp[:], ln_d[:, 0:2 * D])
            wr = sbuf(st, "D_wr", [128, 16, NE], F32)
            kb.load(sp, wr.res, wr[:], wr_d.rearrange("(k p) e -> p k e", p=128))
            wbr = sbuf(st, "D_wbr", [128, 8, D], BF16)
            wo = sbuf(st, "D_wo", [128, 16, D], BF16)
            for g in range(2):
                kb.load(pool, wbr.res, wbr[:, 4 * g:4 * g + 4, :],
                        wbr_d[g * 512:(g + 1) * 512, :].rearrange("(k p) c -> p k c", p=128))
            for i in range(4):
                kb.load(pool, wo.res, wo[:, 4 * i:4 * i + 4, :],
                        wout_d[i * 512:(i + 1) * 512, :].rearrange("(k p) c -> p k c", p=128))
            oTb = [sbuf(st, "D_oTb%d" % i, [128, 8, 512], BF16) for i in range(2)]
            Gt = [sbuf(st, "D_G%d" % i, [128, 2 * D], BF16) for i in range(2)]
            xts = [sbuf(st, "D_xt%d" % i, [128, 1, D], F32) for i in range(3)]
            tA = [sbuf(st, "D_tA%d" % i, [128, 512], F32) for i in range(1)]
            tB = [sbuf(st, "D_tB%d" % i, [128, 512], F32) for i in range(1)]
            mg = sbuf(st, "D_mg", [128, D], BF16)
            mT = sbuf(st, "D_mT", [128, 16, 128], BF16)
            x1b = [sbuf(st, "D_x1b%d" % i, [128, D], BF16) for i in range(1)]
            x1T = sbuf(st, "D_x1T", [128, 16, 128], F32)
            junk = Buf(x1T.t, "junkview")
            junk.res = x1T.res
            junk_ap = x1T[:].rearrange("p a b -> p (a b)")
            st1 = sbuf(st, "D_st1", [128, 8], F32)
            eb = sbuf(st, "D_eb", [128, NE], F32)
            pf = [psum(st, "D_p%d" % i) for i in range(6)]
            pbf = [psum(st, "D_pb%d" % i, BF16) for i in range(2)]
            npf = 0

            def prefetch(tt):
                if tt >= NT:
                    return
                if tt % 4 == 0:
                    ob_ = oTb[(tt // 4) % 2]
                    c0_ = tt * 128
                    kb.load(sp, ob_.res, ob_[:], oT_d[:, :, c0_:c0_ + 512].rearrange("k p t -> p k t"), [dram["oT"]])
                kb.load(sp, Gt[tt % 2].res, Gt[tt % 2][:], G_d[tt * 128:(tt + 1) * 128, :], [dram["G"]])
                kb.load(sp, xts[tt % 3].res, xts[tt % 3][:, 0, :], x_d[tt * 128:(tt + 1) * 128, :])

            def stage1(tt):
                nonlocal npf
                ti = tt % 4
                r0 = tt * 128
                ob_ = oTb[(tt // 4) % 2]
                G_ = Gt[tt % 2]
                xt_ = xts[tt % 3]
                hv = xt_[:, 0, :]
                for db in range(4):
                    pa = pf[npf % 6]
                    pb_ = pf[(npf + 1) % 6]
                    npf += 2
                    for g, pp in ((0, pa), (1, pb_)):
                        for c in range(4):
                            mm(pp, pp[:], ob_, ob_[:, 4 * g + c, ti * 128:(ti + 1) * 128],
                               wbr, wbr[:, 4 * g + c, db * 512:(db + 1) * 512], start=(c == 0), stop=(c == 3))
                    ta, tb_ = tA[0], tB[0]
                    ex(dve, [G_, pa], [ta], lambda: nc.vector.tensor_tensor(
                        out=ta[:], in0=G_[:, db * 512:(db + 1) * 512], in1=pa[:], op=ALU.mult))
                    ex(dve, [G_, pb_], [tb_], lambda: nc.vector.tensor_tensor(
                        out=tb_[:], in0=G_[:, D + db * 512:D + (db + 1) * 512], in1=pb_[:], op=ALU.mult))
                    ex(dve, [ta, tb_], [mg], lambda: nc.vector.tensor_tensor(
                        out=mg[:, db * 512:(db + 1) * 512], in0=ta[:], in1=tb_[:], op=ALU.add))
                for half in range(2):
                    pt = pbf[half]
                    for j in range(8):
                        k = half * 8 + j
                        tr(pt, pt[:, j * 128:(j + 1) * 128], mg, mg[:, k * 128:(k + 1) * 128], ident_b)
                    dst = mT[:, half * 8:(half + 1) * 8, :]
                    src = pt[:].rearrange("p (a b) -> p a b", a=8)
                    if half == 0:
                        ex(act, [pt], [mT], lambda: nc.scalar.copy(out=dst, in_=src))
                    else:
                        ex(dve, [pt], [mT], lambda: nc.vector.tensor_copy(dst, src))
                for obk in range(4):
                    pm = pf[npf % 6]
                    npf += 1
                    for k in range(16):
                        mm(pm, pm[:], mT, mT[:, k, :], wo, wo[:, k, obk * 512:(obk + 1) * 512],
                           start=(k == 0), stop=(k == 15))
                    ex(dve, [xt_, pm], [xt_], lambda: nc.vector.scalar_tensor_tensor(
                        out=xt_[:, 0, obk * 512:(obk + 1) * 512], in0=xt_[:, 0, obk * 512:(obk + 1) * 512],
                        scalar=ALPHA, in1=pm[:], op0=ALU.mult, op1=ALU.add))

            def stage2(tt):
                nonlocal npf
                r0 = tt * 128
                xt_ = xts[tt % 3]
                hv = xt_[:, 0, :]
                layer_norm(nc, kb, ex, hv, xt_, junk, junk_ap, st1, lnp, 0)
                kb.store(sp, xt_.res, x1_d[r0:r0 + 128, :], hv, [dram["x1"]])
                xb_ = x1b[0]
                ex(act, [xt_], [xb_], lambda: nc.scalar.copy(out=xb_[:], in_=hv))
                kb.store(sp, xb_.res, x1b_d[r0:r0 + 128, :], xb_[:], [dram["x1b"]])
                for kk in range(4):
                    pb = pf[npf % 6]
                    npf += 1
                    for j in range(4):
                        k = kk * 4 + j
                        tr(pb, pb[:, j * 128:(j + 1) * 128], xt_, xt_[:, 0, k * 128:(k + 1) * 128], ident_f)
                    dst = x1T[:, kk * 4:(kk + 1) * 4, :]
                    src = pb[:].rearrange("p (a b) -> p a b", a=4)
                    if kk % 2 == 0:
                        ex(act, [pb], [x1T], lambda: nc.scalar.copy(out=dst, in_=src))
                    else:
                        ex(dve, [pb], [x1T], lambda: nc.vector.tensor_copy(dst, src))
                pl = pf[npf % 6]
                npf += 1
                for k in range(16):
                    mm(pl, pl[:, 0:NE], x1T, x1T[:, k, :], wr, wr[:, k, :], start=(k == 0), stop=(k == 15))
                ex(dve, [pl], [st1], lambda: nc.vector.reduce_max(out=st1[:, 4:5], in_=pl[:, 0:NE], axis=AX.X))
                ex(dve, [st1], [st1], lambda: nc.vector.tensor_scalar_mul(out=st1[:, 5:6], in0=st1[:, 4:5], scalar1=-1.0))
                ex(act, [pl, st1], [eb, st1], lambda: nc.scalar.activation(
                    out=eb[:], in_=pl[:, 0:NE], func=AF.Exp, bias=st1[:, 5:6], scale=1.0, accum_out=st1[:, 6:7]))
                ex(dve, [st1], [st1], lambda: nc.vector.reciprocal(out=st1[:, 7:8], in_=st1[:, 6:7]))
                ex(dve, [eb, st1], [aff], lambda: nc.vector.tensor_scalar(
                    out=aff[:, tt * NE:(tt + 1) * NE], in0=eb[:], scalar1=st1[:, 7:8], scalar2=None, op0=ALU.mult))

            prefetch(0)
            prefetch(1)
            stage1(0)
            for tt in range(NT):
                prefetch(tt + 2)
                if tt + 1 < NT:
                    stage1(tt + 1)
                stage2(tt)
        kb.barrier()

        with contextlib.ExitStack() as st:
            lo = sbuf(st, "E_lo", [128, NE], F32)
            hi = sbuf(st, "E_hi", [128, NE], F32)
            mid = sbuf(st, "E_mid", [128, NE], F32)
            dd = sbuf(st, "E_dd", [128, NE], F32)
            sel = sbuf(st, "E_sel", [128, NE], F32)
            cmp_ = sbuf(st, "E_cmp", [128, NT * NE], F32)
            cnt = sbuf(st, "E_cnt", [128, NE], F32)
            mkb = sbuf(st, "E_mkb", [128, NT * NE], BF16)
            tri = sbuf(st, "E_tri", [128, 128], BF16)
            eoff = sbuf(st, "E_eoff", [128, NT * NE], F32)
            tt_ = sbuf(st, "E_tt", [128, NT * NE], F32)
            cum = sbuf(st, "E_cum", [128, NT * NE], F32)
            pos = sbuf(st, "E_pos", [128, NT * NE], F32)
            val = sbuf(st, "E_val", [128, NT * NE], F32)
            pc = psum(st, "E_pc")
            pw = psum(st, "E_pw")
            ptt = psum(st, "E_pt")
            kb.load(sp, tri.res, tri[:], tri_d[:, :])
            kb.load(sp, eoff.res, eoff[:], eoff_d[:, :])
            ex(dve, [], [lo], lambda: nc.vector.memset(lo[:], 0.0))
            ex(dve, [], [hi], lambda: nc.vector.memset(hi[:], 1.0))
            aff3 = aff[:].rearrange("p (t e) -> p t e", e=NE)

            def bc(b):
                a = b[:]
                return bass.AP(a.tensor, a.offset, [list(a.ap[0]), [0, NT], list(a.ap[1])])

            for _ in range(34):
                ex(dve, [lo, hi], [mid], lambda: nc.vector.tensor_tensor(out=mid[:], in0=lo[:], in1=hi[:], op=ALU.add))
                ex(dve, [mid], [mid], lambda: nc.vector.tensor_scalar_mul(out=mid[:], in0=mid[:], scalar1=0.5))
                ex(dve, [aff, mid], [cmp_], lambda: nc.vector.tensor_tensor(
                    out=cmp_[:].rearrange("p (t e) -> p t e", e=NE), in0=aff3, in1=bc(mid), op=ALU.is_gt))
                ex(dve, [cmp_], [cnt], lambda: nc.vector.reduce_sum(
                    out=cnt[:], in_=cmp_[:].rearrange("p (t e) -> p e t", e=NE), axis=AX.X))
                mm(pc, pc[:, 0:NE], ones_f, ones_f[:], cnt, cnt[:])
                ex(dve, [pc], [sel], lambda: nc.vector.tensor_scalar(
                    out=sel[:], in0=pc[:, 0:NE], scalar1=float(CAP) - 0.5, scalar2=None, op0=ALU.is_gt))
                ex(dve, [mid, lo], [dd], lambda: nc.vector.tensor_tensor(out=dd[:], in0=mid[:], in1=lo[:], op=ALU.subtract))
                ex(dve, [dd, sel], [dd], lambda: nc.vector.tensor_tensor(out=dd[:], in0=dd[:], in1=sel[:], op=ALU.mult))
                ex(dve, [dd, lo], [lo], lambda: nc.vector.tensor_tensor(out=lo[:], in0=lo[:], in1=dd[:], op=ALU.add))
                ex(dve, [mid, hi], [dd], lambda: nc.vector.tensor_tensor(out=dd[:], in0=hi[:], in1=mid[:], op=ALU.subtract))
                ex(dve, [dd, sel], [dd], lambda: nc.vector.tensor_tensor(out=dd[:], in0=dd[:], in1=sel[:], op=ALU.mult))
                ex(dve, [dd, mid], [hi], lambda: nc.vector.tensor_tensor(out=hi[:], in0=mid[:], in1=dd[:], op=ALU.add))
            ex(dve, [aff, lo], [cmp_], lambda: nc.vector.tensor_tensor(
                out=cmp_[:].rearrange("p (t e) -> p t e", e=NE), in0=aff3, in1=bc(lo), op=ALU.is_gt))
            ex(act, [cmp_], [mkb], lambda: nc.scalar.copy(out=mkb[:], in_=cmp_[:]))
            mm(pw, pw[:], tri, tri[:], mkb, mkb[:])
            mm(ptt, ptt[:], ones_b, ones_b[:], mkb, mkb[:])
            ex(act, [ptt], [tt_], lambda: nc.scalar.copy(out=tt_[:], in_=ptt[:]))
            ex(dve, [], [cum], lambda: nc.vector.memset(cum[:, 0:NE], 0.0))
            for t in range(1, NT):
                ex(dve, [cum, tt_], [cum], lambda t=t: nc.vector.tensor_tensor(
                    out=cum[:, t * NE:(t + 1) * NE], in0=cum[:, (t - 1) * NE:t * NE],
                    in1=tt_[:, (t - 1) * NE:t * NE], op=ALU.add))
            ex(dve, [pw, cum], [pos], lambda: nc.vector.tensor_tensor(out=pos[:], in0=cum[:], in1=pw[:], op=ALU.add))
            ex(dve, [pos], [val], lambda: nc.vector.tensor_scalar(
                out=val[:], in0=pos[:], scalar1=float(CAP) - 0.5, scalar2=None, op0=ALU.is_lt))
            ex(dve, [val, cmp_], [val], lambda: nc.vector.tensor_tensor(out=val[:], in0=val[:], in1=cmp_[:], op=ALU.mult))
            ex(dve, [aff, val], [gm], lambda: nc.vector.tensor_tensor(out=gm[:], in0=aff[:], in1=val[:], op=ALU.mult))
            ex(dve, [pos, eoff], [pos], lambda: nc.vector.tensor_tensor(out=pos[:], in0=pos[:], in1=eoff[:], op=ALU.add))
            ex(dve, [pos], [pos], lambda: nc.vector.tensor_scalar_add(out=pos[:], in0=pos[:], scalar1=-BIG))
            ex(dve, [pos, val], [pos], lambda: nc.vector.tensor_tensor(out=pos[:], in0=pos[:], in1=val[:], op=ALU.mult))
            ex(dve, [pos], [pos], lambda: nc.vector.tensor_scalar_add(out=pos[:], in0=pos[:], scalar1=BIG))
            ex(dve, [pos], [idx_i], lambda: nc.vector.tensor_copy(idx_i[:], pos[:]))
            if DEBUG:
                kb.store(sp, aff.res, aff_d[:, :], aff[:])
        kb.barrier()

        with contextlib.ExitStack() as st:
            xb = [sbuf(st, "F_xb%d" % i, [128, D], BF16) for i in range(3)]
            for tt in range(NT):
                b = xb[tt % 3]
                kb.load(sp, b.res, b[:], x1b_d[tt * 128:(tt + 1) * 128, :], [dram["x1b"]])
                for e in range(NE):
                    col = tt * NE + e
                    if b.res.dout is None:
                        b.res.dout = kb.dsem()
                    kb.dma(pool, None, None, [b.res, idx_i.res], [], b.res.dout,
                           fn=lambda b=b, col=col: nc.gpsimd.indirect_dma_start(
                               out=xin_d[:, :], out_offset=bass.IndirectOffsetOnAxis(ap=idx_i[:, col:col + 1], axis=0),
                               in_=b[:], in_offset=None, bounds_check=bc_reg, oob_is_err=False))
        kb.barrier()

        with contextlib.ExitStack() as st:
            xi = [sbuf(st, "G_xi%d" % i, [128, 4, D], BF16) for i in range(2)]
            xiT = [sbuf(st, "G_xiT%d" % i, [128, 16, 512], BF16) for i in range(2)]
            hT = sbuf(st, "G_hT", [128, 16, 512], BF16)
            wpc = [sbuf(st, "G_wp%d" % i, [128, 16, 512], BF16) for i in range(5)]
            sgl = [sbuf(st, "G_sg%d" % i, [128, 512], F32) for i in range(2)]
            ost = [sbuf(st, "G_os%d" % i, [128, 512], BF16) for i in range(3)]
            ptr = [psum(st, "G_pt%d" % i, BF16) for i in range(2)]
            pgu = [psum(st, "G_pg%d" % i) for i in range(4)]
            pdn = [psum(st, "G_pd%d" % i) for i in range(2)]
            nos = 0
            npd = 0

            def piece(src_d, e, cblk):
                return src_d[e * D:(e + 1) * D, cblk * 512:(cblk + 1) * 512].rearrange("(k p) c -> p k c", p=128)

            wsrcs = []
            for e in range(NE):
                for fb in range(4):
                    wsrcs.append(piece(wg_d, e, fb))
                    wsrcs.append(piece(wu_d, e, fb))
                for db in range(4):
                    wsrcs.append(piece(wd_d, e, db))
            wstream = Stream(kb, pool, wpc, wsrcs, hold=2)

            def load_x(e):
                x_ = xi[e % 2]
                kb.load(sp, x_.res, x_[:], xin_d[e * CAP:(e + 1) * CAP, :].rearrange("(s p) d -> p s d", p=128))

            def transposes(e):
                x_ = xi[e % 2]
                xt_ = xiT[e % 2]
                for kp in range(8):
                    pt = ptr[kp % 2]
                    for k2 in range(2):
                        k = kp * 2 + k2
                        for s_ in range(4):
                            tr(pt, pt[:, (k2 * 4 + s_) * 128:(k2 * 4 + s_ + 1) * 128], x_,
                               x_[:, s_, k * 128:(k + 1) * 128], ident_b)
                    dst = xt_[:, kp * 2:kp * 2 + 2, :]
                    src = pt[:].rearrange("p (a b) -> p a b", a=2)
                    if kp % 2 == 0:
                        ex(act, [pt], [xt_], lambda: nc.scalar.copy(out=dst, in_=src))
                    else:
                        ex(dve, [pt], [xt_], lambda: nc.vector.tensor_copy(dst, src))

            load_x(0)
            wstream.get(0)
            transposes(0)
            for e in range(NE):
                if e + 1 < NE:
                    load_x(e + 1)
                xt_ = xiT[e % 2]
                for fb in range(4):
                    wg = wstream.get(e * 12 + fb * 2)
                    wu = wstream.get(e * 12 + fb * 2 + 1)
                    for fi in range(4):
                        fc = fb * 4 + fi
                        pg = pgu[2 * (fc % 2)]
                        pu = pgu[2 * (fc % 2) + 1]
                        for k in range(16):
                            mm(pg, pg[:], wg, wg[:, k, fi * 128:(fi + 1) * 128], xt_, xt_[:, k, :],
                               start=(k == 0), stop=(k == 15))
                        for k in range(16):
                            mm(pu, pu[:], wu, wu[:, k, fi * 128:(fi + 1) * 128], xt_, xt_[:, k, :],
                               start=(k == 0), stop=(k == 15))
                        s_g = sgl[fc % 2]
                        ex(act, [pg], [s_g], lambda: nc.scalar.activation(out=s_g[:], in_=pg[:], func=AF.Silu))
                        ex(dve, [s_g, pu], [hT], lambda: nc.vector.tensor_tensor(
                            out=hT[:, fc, :], in0=s_g[:], in1=pu[:], op=ALU.mult))
                if e + 1 < NE:
                    transposes(e + 1)
                for db in range(4):
                    wd = wstream.get(e * 12 + 8 + db)
                    for s_ in range(4):
                        pd = pdn[npd % 2]
                        npd += 1
                        for fc in range(16):
                            mm(pd, pd[:], hT, hT[:, fc, s_ * 128:(s_ + 1) * 128], wd, wd[:, fc, :],
                               start=(fc == 0), stop=(fc == 15))
                        o_ = ost[nos % 3]
                        nos += 1
                        if s_ % 2 == 0:
                            ex(act, [pd], [o_], lambda: nc.scalar.copy(out=o_[:], in_=pd[:]))
                        else:
                            ex(dve, [pd], [o_], lambda: nc.vector.tensor_copy(o_[:], pd[:]))
                        r0 = e * CAP + s_ * 128
                        kb.store(sp, o_.res, yo_d[r0:r0 + 128, db * 512:(db + 1) * 512], o_[:], [dram["yo"]])
        kb.barrier()

        with contextlib.ExitStack() as st:
            lnp = sbuf(st, "H_lnp", [128, 2 * D], F32)
            kb.load(sp, lnp.res, lnp[:], ln_d[:, 2 * D:4 * D])
            NG = 32
            xa = [sbuf(st, "H_xa%d" % i, [128, 1, D], F32) for i in range(2)]
            gb = [sbuf(st, "H_g%d" % i, [128, D], BF16) for i in range(NG)]
            dg = [sbuf(st, "H_dg%d" % i, [128, 128], BF16) for i in range(4)]
            junk = sbuf(st, "H_junk", [128, D], F32)
            st1 = sbuf(st, "H_st1", [128, 8], F32)
            pacc = [[psum(st, "H_p%d%d" % (i, c)) for c in range(4)] for i in range(2)]
            gset = [Res("gset0"), Res("gset1")]
            for i_, g_ in enumerate(gb):
                g_.res = gset[i_ // NE]
                ex(dve, [], [g_], lambda: nc.vector.memset(g_[:], 0.0))

            def gathers(tt):
                rs = gset[tt % 2]
                if rs.din is None:
                    rs.din = kb.dsem()
                lst = []
                fns = []
                for e in range(NE):
                    col = tt * NE + e
                    g_ = gb[(tt % 2) * NE + e]
                    g_.res = rs
                    fns.append(lambda g_=g_, col=col: nc.gpsimd.indirect_dma_start(
                        out=g_[:], out_offset=None, in_=yo_d[:, :],
                        in_offset=bass.IndirectOffsetOnAxis(ap=idx_i[:, col:col + 1], axis=0),
                        bounds_check=bc_reg, oob_is_err=False))
                    lst.append(g_)
                kb.dma_group(pool, fns, [idx_i.res, dram["yo"]], [rs], rs.din)
                return lst

            nd = 0
            gnext = gathers(0)
            for tt in range(NT):
                a_ = xa[tt % 2]
                r0 = tt * 128
                kb.load(sp, a_.res, a_[:, 0, :], x1_d[r0:r0 + 128, :], [dram["x1"]])
                glist = gnext
                if tt + 1 < NT:
                    gnext = gathers(tt + 1)
                pa = pacc[tt % 2]
                for e in range(NE):
                    col = tt * NE + e
                    d_ = dg[nd % 4]
                    nd += 1
                    ex(dve, [ident_b, gm], [d_], lambda: nc.vector.tensor_scalar(
                        out=d_[:], in0=ident_b[:], scalar1=gm[:, col:col + 1], scalar2=None, op0=ALU.mult))
                    g_ = glist[e]
                    for cb in range(4):
                        mm(pa[cb], pa[cb][:], d_, d_[:], g_, g_[:, cb * 512:(cb + 1) * 512],
                           start=(e == 0), stop=(e == NE - 1))
                for cb in range(4):
                    ex(dve, [a_, pa[cb]], [a_], lambda: nc.vector.scalar_tensor_tensor(
                        out=a_[:, 0, cb * 512:(cb + 1) * 512], in0=a_[:, 0, cb * 512:(cb + 1) * 512],
                        scalar=ALPHA, in1=pa[cb][:], op0=ALU.mult, op1=ALU.add))
                layer_norm(nc, kb, ex, a_[:, 0, :], a_, junk, junk[:], st1, lnp, 0)
                kb.store(sp, a_.res, out_d[r0:r0 + 128, :], a_[:, 0, :])
        kb.barrier()
        nc._marks = kb.marks
    return nc


def layer_norm(nc, kb, ex, hv, hb, junk, junk_ap, st1, lnp, off):
    dve, act, pool = kb.dve, kb.act, kb.pool
    ex(dve, [hb], [st1], lambda: nc.vector.reduce_sum(out=st1[:, 0:1], in_=hv, axis=AX.X))
    ex(dve, [st1], [st1], lambda: nc.vector.tensor_scalar_mul(out=st1[:, 1:2], in0=st1[:, 0:1], scalar1=-1.0 / D))
    ex(dve, [hb, st1], [hb], lambda: nc.vector.tensor_scalar(
        out=hv, in0=hv, scalar1=st1[:, 1:2], scalar2=None, op0=ALU.add))
    ex(act, [hb], [junk, st1], lambda: nc.scalar.activation(
        out=junk_ap, in_=hv, func=AF.Square, accum_out=st1[:, 2:3]))
    ex(act, [st1], [st1], lambda: nc.scalar.activation(
        out=st1[:, 3:4], in_=st1[:, 2:3], func=AF.Sqrt, bias=LN_EPS, scale=1.0 / D))
    ex(dve, [st1], [st1], lambda: nc.vector.reciprocal(out=st1[:, 3:4], in_=st1[:, 3:4]))
    ex(dve, [hb, st1, lnp], [hb], lambda: nc.vector.scalar_tensor_tensor(
        out=hv, in0=hv, scalar=st1[:, 3:4], in1=lnp[:, off:off + D], op0=ALU.mult, op1=ALU.mult))
    ex(dve, [hb, lnp], [hb], lambda: nc.vector.tensor_tensor(
        out=hv, in0=hv, in1=lnp[:, off + D:off + 2 * D], op=ALU.add))


DEBUG_OUT = ()


def _consts():
    import ml_dtypes
    bf = ml_dtypes.bfloat16
    pos = np.arange(S, dtype=np.float32)
    c = {}
    for nm, half, rot, nblk in (("A", 16, 32, 4), ("B", 8, 16, 8)):
        inv = (np.float32(500000.0) ** (-2.0 * np.arange(half, dtype=np.float32) / rot)).astype(np.float32)
        ang = (pos[:, None] * inv[None, :]).astype(np.float32)
        cs, sn = np.cos(ang).astype(np.float32), np.sin(ang).astype(np.float32)
        cc = np.concatenate([cs, cs], axis=1)
        ss = np.concatenate([sn, -sn], axis=1)
        c["cc" + nm] = np.ascontiguousarray(np.tile(cc, (1, nblk)))
        c["ss" + nm] = np.ascontiguousarray(np.tile(ss, (1, nblk)))
    a = np.arange(128)[:, None]
    b = np.arange(128)[None, :]
    m = np.concatenate([(a - b >= 64), (np.abs(a - b) <= 64), (b - a >= 64)], axis=1)
    c["bandmask"] = m.astype(np.float32).astype(bf)
    c["ident_f"] = np.eye(128, dtype=np.float32)
    c["ident_b"] = np.eye(128, dtype=np.float32).astype(bf)
    c["tri_b"] = (a < b).astype(np.float32).astype(bf)
    eo = np.tile((np.arange(NE, dtype=np.float32) * CAP)[None, :], (128, NT))
    c["eoff"] = np.ascontiguousarray(eo.astype(np.float32))
    return c


_CACHE = {}


def kernel(x, w_in, lambda_q1, lambda_k1, lambda_q2, lambda_k2, diff_norm_w, w_branch, w_out,
           ln1_g, ln1_b, w_router, w_gate, w_up, w_down, ln2_g, ln2_b):
    f = lambda a: np.ascontiguousarray(np.asarray(a, dtype=np.float32))
    x = f(x)
    if "nc" not in _CACHE:
        _CACHE["nc"] = build_program()
        _CACHE["c"] = _consts()
    nc = _CACHE["nc"]
    c = _CACHE["c"]
    lam4 = np.concatenate([f(lambda_q1)[0], f(lambda_k1)[0], f(lambda_q2)[0], f(lambda_k2)[0]])[None, :]
    lnp = np.concatenate([f(ln1_g)[0], f(ln1_b)[0], f(ln2_g)[0], f(ln2_b)[0]])[None, :]
    shared = {
        "w_in": f(w_in)[0], "w_branch": f(w_branch)[0].reshape(1024, D), "w_out": f(w_out)[0],
        "w_router": f(w_router)[0], "w_gate": f(w_gate)[0].reshape(NE * D, D),
        "w_up": f(w_up)[0].reshape(NE * D, D), "w_down": f(w_down)[0].reshape(NE * D, D),
        "lam4": np.ascontiguousarray(np.broadcast_to(lam4, (128, 256))),
        "nw": np.ascontiguousarray(f(diff_norm_w)[0].reshape(128, 1)),
        "lnp": np.ascontiguousarray(np.broadcast_to(lnp, (128, 4 * D))),
    }
    shared.update(c)
    nb = x.shape[0]
    in_maps = []
    for core in range(8):
        m = dict(shared)
        m["x"] = np.ascontiguousarray(x[(core // 2) % nb])
        in_maps.append(m)
    res = run_bass_kernel_spmd(nc, in_maps, core_ids=list(range(8)))
    out = np.stack([np.asarray(res.results[2 * b]["out"], dtype=np.float32) for b in range(nb)], axis=0)
    return out
```

```python
import contextlib
import numpy as np
import concourse.bass as bass
import concourse.mybir as mybir
from concourse.bass_utils import run_bass_kernel_spmd

F32 = mybir.dt.float32
BF16 = mybir.dt.bfloat16
I32 = mybir.dt.int32
ALU = mybir.AluOpType
AF = mybir.ActivationFunctionType
AX = mybir.AxisListType

S = 4096
D = 2048
NT = S // 128
NE = 16
CAP = 512
ALPHA = 2.0 ** 0.25
LN_EPS = 1e-5
BIG = float(1 << 20)
DEBUG = False
FUSE_WAIT = True


class Ev:
    __slots__ = ("sem", "val")

    def __init__(self, sem, val):
        self.sem = sem
        self.val = val


class DSem:
    def __init__(self, h):
        self.h = h
        self.n = 0


class Res:
    def __init__(self, name):
        self.name = name
        self.w = None
        self.r = {}
        self.din = None
        self.dout = None


class Q:
    def __init__(self, eng, sem, is_pe=False):
        self.eng = eng
        self.sem = sem
        self.n = 0
        self.known = {}
        self.is_pe = is_pe

    def wait(self, ev):
        if ev is None:
            return
        if ev.sem is self.sem and self.is_pe:
            return
        k = id(ev.sem)
        if self.known.get(k, 0) >= ev.val:
            return
        self.eng.wait_ge(ev.sem, ev.val)
        self.known[k] = ev.val


class KB:
    def __init__(self, nc, es):
        self.nc = nc
        self.es = es
        self.nsem = 0
        self.pe = Q(nc.tensor, self.newsem("q_pe"), True)
        self.act = Q(nc.scalar, self.newsem("q_act"))
        self.dve = Q(nc.vector, self.newsem("q_dve"))
        self.pool = Q(nc.gpsimd, self.newsem("q_pool"))
        self.sp = Q(nc.sync, self.newsem("q_sp"))
        self.queues = [self.pe, self.act, self.dve, self.pool, self.sp]
        self.dsems = []

    def newsem(self, name):
        self.nsem += 1
        return self.es.enter_context(self.nc.semaphore(name))

    def dsem(self):
        free = getattr(self, "free_dsems", None)
        if free:
            return free.pop()
        d = DSem(self.newsem("d%d" % len(self.dsems)))
        self.dsems.append(d)
        return d

    def dma_group(self, q, fns, reads, writes, ds):
        self._pre(q, reads, writes)
        for fn in fns:
            inst = fn()
            ds.n += 16
            inst.then_inc(ds.h, 16)
        self._post(Ev(ds.h, ds.n), reads, writes)

    def _need(self, q, reads, writes):
        need = {}

        def add(ev):
            if ev is None:
                return
            if ev.sem is q.sem and q.is_pe:
                return
            k = id(ev.sem)
            if q.known.get(k, 0) >= ev.val:
                return
            if k not in need or need[k].val < ev.val:
                need[k] = ev

        for r in reads:
            add(r.w)
        for w in writes:
            add(w.w)
            for ev in w.r.values():
                add(ev)
        return list(need.values())

    def _pre(self, q, reads, writes, keep_last=False):
        evs = self._need(q, reads, writes)
        last = None
        if keep_last and evs:
            last = evs.pop()
        for ev in evs:
            q.eng.wait_ge(ev.sem, ev.val)
            q.known[id(ev.sem)] = ev.val
        return last

    def _post(self, ev, reads, writes):
        for r in reads:
            r.r[id(ev.sem)] = ev
        for w in writes:
            w.w = ev
            w.r = {}

    def op(self, q, reads, writes, fn):
        last = self._pre(q, reads, writes, keep_last=FUSE_WAIT)
        inst = fn()
        if last is not None:
            inst._wait_ge(last.sem, last.val)
            q.known[id(last.sem)] = last.val
        q.n += 1
        inst.then_inc(q.sem, 1)
        self._post(Ev(q.sem, q.n), reads, writes)

    def dma(self, q, out, in_, reads, writes, ds, fn=None):
        self._pre(q, reads, writes)
        if fn is None:
            inst = q.eng.dma_start(out=out, in_=in_)
        else:
            inst = fn()
        ds.n += 16
        inst.then_inc(ds.h, 16)
        self._post(Ev(ds.h, ds.n), reads, writes)

    def load(self, q, res, out, in_, src=()):
        if res.din is None:
            res.din = self.dsem()
        self.dma(q, out, in_, list(src), [res], res.din)

    def store(self, q, res, out, in_, dst=()):
        if res.dout is None:
            res.dout = self.dsem()
        self.dma(q, out, in_, [res], list(dst), res.dout)

    def barrier(self):
        self.marks = getattr(self, "marks", [])
        self.marks.append({"pe": self.pe.n, "act": self.act.n, "dve": self.dve.n, "pool": self.pool.n})
        for q in self.queues:
            for p in self.queues:
                if p is not q and p.n > 0:
                    q.wait(Ev(p.sem, p.n))
            for d in self.dsems:
                if d.n > 0:
                    q.wait(Ev(d.h, d.n))
        self.free_dsems = list(self.dsems)


class Stream:
    def __init__(self, kb, q, bufs, srcs, hold=1):
        self.kb, self.q, self.bufs, self.srcs = kb, q, bufs, srcs
        self.n = 0
        self.hold = hold

    def get(self, i):
        last = min(i + len(self.bufs) - self.hold, len(self.srcs) - 1)
        while self.n <= last:
            b = self.bufs[self.n % len(self.bufs)]
            self.kb.load(self.q, b.res, b[:], self.srcs[self.n])
            self.n += 1
        return self.bufs[i % len(self.bufs)]


class Buf:
    def __init__(self, t, name):
        self.t = t
        self.res = Res(name)

    def __getitem__(self, k):
        return self.t[k]


def build_program():
    nc = bass.Bass("TRN2", target_bir_lowering=False)
    es = contextlib.ExitStack()

    def din(name, shape, dt=F32):
        return nc.dram_tensor(name, list(shape), dt, kind="ExternalInput").ap()

    def dscr(name, shape, dt):
        kind = "ExternalOutput" if (DEBUG and name in DEBUG_OUT) else "Internal"
        return nc.dram_tensor(name, list(shape), dt, kind=kind).ap()

    x_d = din("x", [S, D])
    win_d = din("w_in", [D, 10240])
    wbr_d = din("w_branch", [1024, D])
    wout_d = din("w_out", [D, D])
    wr_d = din("w_router", [D, NE])
    wg_d = din("w_gate", [NE * D, D])
    wu_d = din("w_up", [NE * D, D])
    wd_d = din("w_down", [NE * D, D])
    lam_d = din("lam4", [128, 256])
    nw_d = din("nw", [128, 1])
    ln_d = din("lnp", [128, 4 * D])
    ccA_d = din("ccA", [S, 128])
    ssA_d = din("ssA", [S, 128])
    ccB_d = din("ccB", [S, 128])
    ssB_d = din("ssB", [S, 128])
    msk_d = din("bandmask", [128, 384], BF16)
    idf_d = din("ident_f", [128, 128])
    idb_d = din("ident_b", [128, 128], BF16)
    tri_d = din("tri_b", [128, 128], BF16)
    eoff_d = din("eoff", [128, NT * NE])
    out_d = nc.dram_tensor("out", [S, D], F32, kind="ExternalOutput").ap()

    QT_d = dscr("QT_s", [16, 128, S], BF16)
    KT_d = dscr("KT_s", [16, 128, S], BF16)
    V_d = dscr("V_s", [S, D], BF16)
    oT_d = dscr("oT_s", [8, 128, S], BF16)
    x1_d = dscr("x1_s", [S, D], F32)
    x1b_d = dscr("x1b_s", [S, D], BF16)
    xin_d = dscr("xin_s", [NE * CAP, D], BF16)
    yo_d = dscr("yo_s", [NE * CAP, D], BF16)
    aff_d = dscr("aff_s", [128, NT * NE], F32)
    G_d = dscr("G_s", [S, 2 * D], BF16)

    with es:
        kb = KB(nc, es)
        pe, act, dve, pool, sp = kb.pe, kb.act, kb.dve, kb.pool, kb.sp

        def sbuf(st, name, shape, dt):
            return Buf(st.enter_context(nc.sbuf_tensor("sb_" + name, list(shape), dt)), name)

        def psum(st, name, dt=F32):
            shape = [128, 512] if dt == F32 else [128, 1024]
            return Buf(st.enter_context(nc.psum_tensor("ps_" + name, shape, dt)), name)

        def mm(out_b, out_ap, l_b, l_ap, r_b, r_ap, start=True, stop=True):
            kb.op(pe, [l_b.res, r_b.res], [out_b.res],
                  lambda: nc.tensor.matmul(out_ap, l_ap, r_ap, start=start, stop=stop))

        def tr(out_b, out_ap, in_b, in_ap, id_b):
            kb.op(pe, [in_b.res, id_b.res], [out_b.res],
                  lambda: nc.tensor.transpose(out_ap, in_ap, id_b[:]))

        def ex(q, reads, writes, fn):
            kb.op(q, [b.res for b in reads], [b.res for b in writes], fn)

        bc_reg = nc.gpsimd.alloc_register("bc_reg")
        nc.gpsimd.reg_mov(bc_reg, NE * CAP - 1)
        dram = {n: Res(n) for n in ["xT", "QT", "KT", "V", "oT", "x1", "x1b", "xin", "yo", "G"]}

        gst = es
        ident_f = sbuf(gst, "ident_f", [128, 128], F32)
        ident_b = sbuf(gst, "ident_b", [128, 128], BF16)
        ones_b = sbuf(gst, "ones_b", [128, 128], BF16)
        ones_f = sbuf(gst, "ones_f", [128, 128], F32)
        lam4 = sbuf(gst, "lam4", [128, 256], F32)
        nw = sbuf(gst, "nw", [128, 1], F32)
        neglam = sbuf(gst, "neglam", [128, 1], F32)
        ltmp = sbuf(gst, "ltmp", [128, 128], F32)
        lsum = sbuf(gst, "lsum", [128, 2], F32)
        aff = sbuf(gst, "aff", [128, NT * NE], F32)
        idx_i = sbuf(gst, "idx_i", [128, NT * NE], I32)
        gm = sbuf(gst, "gm", [128, NT * NE], F32)
        kb.load(sp, ident_f.res, ident_f[:], idf_d[:, :])
        kb.load(sp, ident_b.res, ident_b[:], idb_d[:, :])
        kb.load(sp, lam4.res, lam4[:], lam_d[:, :])
        kb.load(sp, nw.res, nw[:], nw_d[:, :])
        ex(pool, [], [ones_b], lambda: nc.gpsimd.memset(ones_b[:], 1.0))
        ex(pool, [], [ones_f], lambda: nc.gpsimd.memset(ones_f[:], 1.0))
        ex(dve, [lam4], [ltmp], lambda: nc.vector.tensor_tensor(
            out=ltmp[:].rearrange("p (a c) -> p a c", a=2),
            in0=lam4[:].rearrange("p (a b c) -> p a b c", a=2, b=2)[:, :, 0, :],
            in1=lam4[:].rearrange("p (a b c) -> p a b c", a=2, b=2)[:, :, 1, :], op=ALU.mult))
        ex(dve, [ltmp], [lsum], lambda: nc.vector.reduce_sum(
            out=lsum[:], in_=ltmp[:].rearrange("p (a c) -> p a c", a=2), axis=AX.X))
        ex(act, [lsum], [lsum], lambda: nc.scalar.activation(out=lsum[:], in_=lsum[:], func=AF.Exp))
        ex(dve, [lsum], [neglam], lambda: nc.vector.tensor_tensor(
            out=neglam[:], in0=lsum[:, 1:2], in1=lsum[:, 0:1], op=ALU.subtract))
        ex(dve, [neglam], [neglam], lambda: nc.vector.tensor_scalar_add(
            out=neglam[:], in0=neglam[:], scalar1=-0.2))
        ex(dve, [nw], [nw], lambda: nc.vector.tensor_scalar_mul(out=nw[:], in0=nw[:], scalar1=0.8))

        with contextlib.ExitStack() as st:
            xT = sbuf(st, "A_xT", [128, 16, 2048], BF16)
            tabs = [sbuf(st, "A_tab%d" % i, [128, 16, 128], F32) for i in range(4)]
            xt = [sbuf(st, "A_xt%d" % i, [128, D], F32) for i in range(2)]
            wp = [sbuf(st, "A_wp%d" % i, [128, 16, 512], BF16) for i in range(2)]
            zt = [sbuf(st, "A_zt%d" % i, [128, 512], F32) for i in range(2)]
            uu = [sbuf(st, "A_uu%d" % i, [128, 128], F32) for i in range(2)]
            vv = [sbuf(st, "A_vv%d" % i, [128, 128], F32) for i in range(2)]
            zb = [sbuf(st, "A_zb%d" % i, [128, 512], BF16) for i in range(4)]
            qst = [sbuf(st, "A_qst%d" % i, [128, 4, 2048], BF16) for i in range(2)]
            pmm = [psum(st, "A_pm%d" % i) for i in range(4)]
            ptr = [psum(st, "A_pt%d" % i, BF16) for i in range(2)]
            tab_src = [ccA_d, ssA_d, ccB_d, ssB_d]
            nxt = 0
            for hf in range(2):
                t0 = hf * 2048
                for i in range(4):
                    kb.load(sp, tabs[i].res, tabs[i][:],
                            tab_src[i][t0:t0 + 2048, :].rearrange("(t p) c -> p t c", p=128))
                for tl in range(16):
                    xs = xt[tl % 2]
                    r0 = t0 + tl * 128
                    kb.load(sp, xs.res, xs[:], x_d[r0:r0 + 128, :])
                    for kk in range(4):
                        pb = pmm[kk]
                        for j in range(4):
                            k = kk * 4 + j
                            tr(pb, pb[:, j * 128:(j + 1) * 128], xs, xs[:, k * 128:(k + 1) * 128], ident_f)
                        dst = xT[:, kk * 4:(kk + 1) * 4, tl * 128:(tl + 1) * 128]
                        src = pb[:].rearrange("p (a b) -> p a b", a=4)
                        if kk % 2 == 0:
                            ex(act, [pb], [xT], lambda dst=dst, src=src: nc.scalar.copy(out=dst, in_=src))
                        else:
                            ex(dve, [pb], [xT], lambda dst=dst, src=src: nc.vector.tensor_copy(dst, src))
                NCB = 20

                def col_of(cb):
                    return cb * 512 if cb < 12 else 6144 + (cb - 12) * 512

                wsrcs = [win_d[:, col_of(cb):col_of(cb) + 512].rearrange("(k p) c -> p k c", p=128)
                         for cb in range(NCB)]
                wstream = Stream(kb, pool, wp, wsrcs)
                for cb in range(NCB):
                    w = wstream.get(cb)
                    is_v = cb in (6, 7, 8, 11)
                    is_g = cb >= 12
                    is_b = cb in (9, 10)
                    qs = qst[cb % 2]
                    pend = []

                    def flush_one():
                        z_b, tl_ = pend.pop(0)
                        pt = ptr[tl_ % 2]
                        for h in range(4):
                            tr(pt, pt[:, h * 128:(h + 1) * 128], z_b, z_b[:, h * 128:(h + 1) * 128], ident_b)
                        dst = qs[:, :, tl_ * 128:(tl_ + 1) * 128]
                        src = pt[:, 0:512].rearrange("p (a b) -> p a b", a=4)
                        ex(dve, [pt], [qs], lambda: nc.vector.tensor_copy(dst, src))

                    for tl in range(16):
                        pb = pmm[nxt % 4]
                        nxt += 1
                        for k in range(16):
                            mm(pb, pb[:], xT, xT[:, k, tl * 128:(tl + 1) * 128], w, w[:, k, :],
                               start=(k == 0), stop=(k == 15))
                        z_b = zb[tl % 4]
                        r0 = t0 + tl * 128
                        if is_v:
                            ex(act, [pb], [z_b], lambda: nc.scalar.copy(out=z_b[:], in_=pb[:]))
                            c0 = (cb - 6) * 512 if cb < 9 else 1536
                            kb.store(sp, z_b.res, V_d[r0:r0 + 128, c0:c0 + 512], z_b[:], [dram["V"]])
                            continue
                        if is_g:
                            ex(act, [pb], [z_b], lambda: nc.scalar.activation(out=z_b[:], in_=pb[:], func=AF.Sigmoid))
                            c0 = (cb - 12) * 512
                            kb.store(sp, z_b.res, G_d[r0:r0 + 128, c0:c0 + 512], z_b[:], [dram["G"]])
                            continue
                        z_t = zt[tl % 2]
                        u_t = uu[tl % 2]
                        v_t = vv[tl % 2]
                        ex(act, [pb], [z_t], lambda: nc.scalar.copy(out=z_t[:], in_=pb[:]))
                        if not is_b:
                            nb, bw, hh_ = 4, 128, 16
                            cc, ss = tabs[0], tabs[1]
                        else:
                            nb, bw, hh_ = 8, 64, 8
                            cc, ss = tabs[2], tabs[3]
                        z3 = z_t[:].rearrange("p (a b) -> p a b", a=nb)
                        zb3 = z_b[:].rearrange("p (a b) -> p a b", a=nb)
                        u3 = u_t[:].rearrange("p (a b) -> p a b", a=nb)
                        v3 = v_t[:].rearrange("p (a b) -> p a b", a=nb)
                        c3 = cc[:, tl, :].rearrange("p (a b) -> p a b", a=nb)
                        s3 = ss[:, tl, :].rearrange("p (a b) -> p a b", a=nb)
                        rw = 2 * hh_
                        ex(dve, [z_t, cc], [u_t], lambda: nc.vector.tensor_tensor(
                            out=u3, in0=z3[:, :, 0:rw], in1=c3, op=ALU.mult))
                        ex(dve, [z_t, ss], [v_t], lambda: nc.vector.tensor_tensor(
                            out=v3, in0=z3[:, :, 0:rw], in1=s3, op=ALU.mult))
                        ex(pool, [u_t, v_t], [z_b], lambda: nc.gpsimd.tensor_tensor(
                            out=zb3[:, :, 0:hh_], in0=u3[:, :, 0:hh_], in1=v3[:, :, hh_:rw], op=ALU.add))
                        ex(pool, [u_t, v_t], [z_b], lambda: nc.gpsimd.tensor_tensor(
                            out=zb3[:, :, hh_:rw], in0=u3[:, :, hh_:rw], in1=v3[:, :, 0:hh_], op=ALU.add))
                        ex(act, [z_t], [z_b], lambda: nc.scalar.copy(out=zb3[:, :, rw:bw], in_=z3[:, :, rw:bw]))
                        pend.append((z_b, tl))
                        if len(pend) > 2:
                            flush_one()
                    while pend:
                        flush_one()
                    if not (is_v or is_g):
                        if cb < 3:
                            tgt, key, h0 = QT_d, "QT", cb * 4
                        elif cb < 6:
                            tgt, key, h0 = KT_d, "KT", (cb - 3) * 4
                        elif cb == 9:
                            tgt, key, h0 = QT_d, "QT", 12
                        else:
                            tgt, key, h0 = KT_d, "KT", 12
                        for h in range(4):
                            kb.store(sp, qs.res, tgt[h0 + h, :, t0:t0 + 2048], qs[:, h, :], [dram[key]])
        kb.barrier()

        with contextlib.ExitStack() as st:
            msk = sbuf(st, "B_msk", [128, 384], BF16)
            kb.load(sp, msk.res, msk[:], msk_d[:, :])
            qT = [sbuf(st, "B_q%d" % i, [128, S], BF16) for i in range(2)]
            kT = [sbuf(st, "B_k%d" % i, [128, S], BF16) for i in range(2)]
            vs = [sbuf(st, "B_v%d" % i, [128, 32, 128], BF16) for i in range(2)]
            accn = sbuf(st, "B_accn", [128, S], F32)
            accd = sbuf(st, "B_accd", [128, S], F32)
            ob = sbuf(st, "B_ob", [128, S], BF16)
            pT = [sbuf(st, "B_pT%d" % i, [128, 384], BF16) for i in range(3)]
            ps_s = [psum(st, "B_ps%d" % i) for i in range(3)]
            ps_n = [psum(st, "B_pn%d" % i) for i in range(2)]
            ps_d = [psum(st, "B_pd%d" % i) for i in range(2)]
            it = 0
            blk = 0
            for hh in range(4):
                for g, r in enumerate((1, 4, 16)):
                    hd = g * 4 + hh
                    L = S // r
                    nj = L // 128
                    q_, k_, v_ = qT[it % 2], kT[it % 2], vs[it % 2]
                    it += 1
                    kb.load(sp, q_.res, q_[:], QT_d[hd, :, :], [dram["QT"]])
                    kb.load(sp, k_.res, k_[:], KT_d[hd, :, :], [dram["KT"]])
                    for ph in range(r):
                        kb.load(sp, v_.res, v_[:, ph * nj:(ph + 1) * nj, :],
                                V_d[ph:S:r, hd * 128:(hd + 1) * 128].rearrange("(j a) c -> a j c", a=128),
                                [dram["V"]])
                    nblk = r * nj
                    for b0 in range(0, nblk, 4):
                        pn = ps_n[(b0 // 4) % 2]
                        pd = ps_d[(b0 // 4) % 2]
                        for bi in range(b0, b0 + 4):
                            ph, j = bi // nj, bi % nj
                            kts = [kt for kt in (j - 1, j, j + 1) if 0 <= kt < nj]
                            lo = (kts[0] - j + 1) * 128
                            hi = (kts[-1] - j + 2) * 128
                            pss = ps_s[blk % 3]
                            p_t = pT[blk % 3]
                            blk += 1
                            qsl = q_[:, ph + r * 128 * j: ph + r * 128 * j + r * 127 + 1: r]
                            for kt in kts:
                                off = (kt - j + 1) * 128
                                ksl = k_[:, ph + r * 128 * kt: ph + r * 128 * kt + r * 127 + 1: r]
                                mm(pss, pss[:, off:off + 128], k_, ksl, q_, qsl)
                            ex(act, [pss], [p_t], lambda: nc.scalar.activation(
                                out=p_t[:, lo:hi], in_=pss[:, lo:hi], func=AF.Exp, scale=128.0 ** -0.5))
                            ex(dve, [p_t, msk], [p_t], lambda: nc.vector.tensor_tensor(
                                out=p_t[:, lo:hi], in0=p_t[:, lo:hi], in1=msk[:, lo:hi], op=ALU.mult))
                            c0 = (bi - b0) * 128
                            for n_, kt in enumerate(kts):
                                off = (kt - j + 1) * 128
                                mm(pn, pn[:, c0:c0 + 128], v_, v_[:, ph * nj + kt, :], p_t, p_t[:, off:off + 128],
                                   start=(n_ == 0), stop=(n_ == len(kts) - 1))
                            for n_, kt in enumerate(kts):
                                off = (kt - j + 1) * 128
                                mm(pd, pd[:, c0:c0 + 128], ones_b, ones_b[:], p_t, p_t[:, off:off + 128],
                                   start=(n_ == 0), stop=(n_ == len(kts) - 1))
                        ph0, j0 = b0 // nj, b0 % nj
                        outs = []
                        for acc in (accn, accd):
                            av = acc[:].rearrange("p (m r) -> p r m", r=r)
                            if r <= 4:
                                outs.append((av[:, ph0, j0 * 128:j0 * 128 + 512], None))
                            else:
                                outs.append((av[:, ph0:ph0 + 2, :], 2))
                        for (dst, f), pb_, acc, q in ((outs[0], pn, accn, act), (outs[1], pd, accd, dve)):
                            src = pb_[:] if f is None else pb_[:].rearrange("p (f m) -> p f m", f=f)
                            if g == 0:
                                if q is act:
                                    ex(act, [pb_], [acc], lambda dst=dst, src=src: nc.scalar.copy(out=dst, in_=src))
                                else:
                                    ex(dve, [pb_], [acc], lambda dst=dst, src=src: nc.vector.tensor_copy(dst, src))
                            else:
                                ex(dve, [pb_, acc], [acc], lambda dst=dst, src=src: nc.vector.tensor_tensor(
                                    out=dst, in0=dst, in1=src, op=ALU.add))
                ex(dve, [accd], [accd], lambda: nc.vector.reciprocal(out=accd[:], in_=accd[:]))
                ex(dve, [accn, accd], [ob], lambda: nc.vector.tensor_tensor(
                    out=ob[:], in0=accn[:], in1=accd[:], op=ALU.mult))
                kb.store(sp, ob.res, oT_d[hh, :, :], ob[:], [dram["oT"]])
        kb.barrier()

        with contextlib.ExitStack() as st:
            qT = [sbuf(st, "C_q%d" % i, [128, S], BF16) for i in range(2)]
            kT = [sbuf(st, "C_k%d" % i, [128, S], BF16) for i in range(2)]
            vn = [sbuf(st, "C_v%d" % i, [128, 32, 128], BF16) for i in range(2)]
            ob = sbuf(st, "C_ob", [128, S], BF16)
            pT = [sbuf(st, "C_pT%d" % i, [128, 512], BF16) for i in range(4)]
            sacc = [[sbuf(st, "C_sa%d%d" % (i, c), [128, 512], F32) for c in range(2)] for i in range(2)]
            rr = [sbuf(st, "C_rr%d" % i, [128, 512], F32) for i in range(2)]
            aa = [sbuf(st, "C_aa%d" % i, [128, 512], F32) for i in range(2)]
            A_ = sbuf(st, "C_A", [128, 512], F32)
            sq = sbuf(st, "C_sq", [128, 512], F32)
            rs = sbuf(st, "C_rs", [128, 512], F32)
            ps_s = [psum(st, "C_ps%d" % i) for i in range(3)]
            ps_n = [[psum(st, "C_pn%d%d" % (i, c)) for c in range(2)] for i in range(1)]
            ps_d = psum(st, "C_pd")
            ps_q = psum(st, "C_pq")
            gi = 0
            nqb = 0
            for h in range(4):
                q_, k_, v_ = qT[h % 2], kT[h % 2], vn[h % 2]
                kb.load(sp, q_.res, q_[:], QT_d[12 + h, :, :], [dram["QT"]])
                kb.load(sp, k_.res, k_[:], KT_d[12 + h, :, :], [dram["KT"]])
                kb.load(sp, v_.res, v_[:],
                        V_d[:, 1536 + h * 128:1536 + (h + 1) * 128].rearrange("(t p) c -> p t c", p=128),
                        [dram["V"]])
                for qb in range(8):
                    pn = ps_n[0]
                    sa = sacc[nqb % 2]
                    nqb += 1
                    steps = [(kt, c) for kt in range(32) for c in range(2)]

                    def qk(i):
                        kt, c = steps[i]
                        pss = ps_s[(gi + i) % 3]
                        mm(pss, pss[:], k_, k_[64 * c:64 * c + 64, kt * 128:(kt + 1) * 128],
                           q_, q_[64 * c:64 * c + 64, qb * 512:(qb + 1) * 512])

                    qk(0)
                    for i, (kt, c) in enumerate(steps):
                        if i + 1 < len(steps):
                            qk(i + 1)
                        pss = ps_s[(gi + i) % 3]
                        p_t = pT[(gi + i) % 4]
                        ex(act, [pss], [p_t], lambda: nc.scalar.activation(
                            out=p_t[:], in_=pss[:], func=AF.Exp, scale=0.125))
                        mm(pn[c], pn[c][:], v_, v_[:, kt, :], p_t, p_t[:], start=(kt == 0), stop=(kt == 31))
                        if True:
                            if kt == 0:
                                ex(dve, [p_t], [sa[c]], lambda: nc.vector.tensor_copy(sa[c][:], p_t[:]))
                            else:
                                ex(dve, [p_t, sa[c]], [sa[c]], lambda: nc.vector.tensor_tensor(
                                    out=sa[c][:], in0=sa[c][:], in1=p_t[:], op=ALU.add))
                    gi += len(steps)
                    for c in range(2):
                        mm(ps_q, ps_q[:], ones_f, ones_f[:], sa[c], sa[c][:])
                        pden = ps_q
                        ex(dve, [pden], [rr[c]], lambda: nc.vector.reciprocal(out=rr[c][:], in_=pden[:]))
                        ex(dve, [pn[c], rr[c]], [aa[c]], lambda: nc.vector.tensor_tensor(
                            out=aa[c][:], in0=pn[c][:], in1=rr[c][:], op=ALU.mult))
                    ex(dve, [aa[0], aa[1], neglam], [A_], lambda: nc.vector.scalar_tensor_tensor(
                        out=A_[:], in0=aa[1][:], scalar=neglam[:, 0:1], in1=aa[0][:], op0=ALU.mult, op1=ALU.add))
                    ex(act, [A_], [sq], lambda: nc.scalar.activation(out=sq[:], in_=A_[:], func=AF.Square))
                    mm(ps_q, ps_q[:], ones_f, ones_f[:], sq, sq[:])
                    ex(act, [ps_q], [rs], lambda: nc.scalar.activation(
                        out=rs[:], in_=ps_q[:], func=AF.Sqrt, bias=1e-5, scale=1.0 / 128.0))
                    ex(dve, [rs], [rs], lambda: nc.vector.reciprocal(out=rs[:], in_=rs[:]))
                    ex(dve, [A_, rs], [A_], lambda: nc.vector.tensor_tensor(out=A_[:], in0=A_[:], in1=rs[:], op=ALU.mult))
                    ex(dve, [A_, nw], [ob], lambda: nc.vector.tensor_scalar(
                        out=ob[:, qb * 512:(qb + 1) * 512], in0=A_[:], scalar1=nw[:, 0:1], scalar2=None, op0=ALU.mult))
                kb.store(sp, ob.res, oT_d[4 + h, :, :], ob[:], [dram["oT"]])
        kb.barrier()

        with contextlib.ExitStack() as st:
            lnp = sbuf(st, "D_lnp", [128, 2 * D], F32)
            kb.load(sp, lnp.res, lnp[:], ln_d[:, 0:2 * D])
            wr = sbuf(st, "D_wr", [128, 16, NE], F32)
            kb.load(sp, wr.res, wr[:], wr_d.rearrange("(k p) e -> p k e", p=128))
            wbr = sbuf(st, "D_wbr", [128, 8, D], BF16)
            wo = sbuf(st, "D_wo", [128, 16, D], BF16)
            for g in range(2):
                kb.load(pool, wbr.res, wbr[:, 4 * g:4 * g + 4, :],
                        wbr_d[g * 512:(g + 1) * 512, :].rearrange("(k p) c -> p k c", p=128))
            for i in range(4):
                kb.load(pool, wo.res, wo[:, 4 * i:4 * i + 4, :],
                        wout_d[i * 512:(i + 1) * 512, :].rearrange("(k p) c -> p k c", p=128))
            oTb = [sbuf(st, "D_oTb%d" % i, [128, 8, 512], BF16) for i in range(2)]
            Gt = [sbuf(st, "D_G%d" % i, [128, 2 * D], BF16) for i in range(2)]
            xts = [sbuf(st, "D_xt%d" % i, [128, 1, D], F32) for i in range(3)]
            tA = [sbuf(st, "D_tA%d" % i, [128, 512], F32) for i in range(1)]
            tB = [sbuf(st, "D_tB%d" % i, [128, 512], F32) for i in range(1)]
            mg = sbuf(st, "D_mg", [128, D], BF16)
            mT = sbuf(st, "D_mT", [128, 16, 128], BF16)
            x1b = [sbuf(st, "D_x1b%d" % i, [128, D], BF16) for i in range(1)]
            x1T = sbuf(st, "D_x1T", [128, 16, 128], F32)
            junk = Buf(x1T.t, "junkview")
            junk.res = x1T.res
            junk_ap = x1T[:].rearrange("p a b -> p (a b)")
            st1 = sbuf(st, "D_st1", [128, 8], F32)
            eb = sbuf(st, "D_eb", [128, NE], F32)
            pf = [psum(st, "D_p%d" % i) for i in range(6)]
            pbf = [psum(st, "D_pb%d" % i, BF16) for i in range(2)]
            npf = 0

            def prefetch(tt):
                if tt >= NT:
                    return
                if tt % 4 == 0:
                    ob_ = oTb[(tt // 4) % 2]
                    c0_ = tt * 128
                    kb.load(sp, ob_.res, ob_[:], oT_d[:, :, c0_:c0_ + 512].rearrange("k p t -> p k t"), [dram["oT"]])
                kb.load(sp, Gt[tt % 2].res, Gt[tt % 2][:], G_d[tt * 128:(tt + 1) * 128, :], [dram["G"]])
                kb.load(sp, xts[tt % 3].res, xts[tt % 3][:, 0, :], x_d[tt * 128:(tt + 1) * 128, :])

            def stage1(tt):
                nonlocal npf
                ti = tt % 4
                r0 = tt * 128
                ob_ = oTb[(tt // 4) % 2]
                G_ = Gt[tt % 2]
                xt_ = xts[tt % 3]
                hv = xt_[:, 0, :]
                for db in range(4):
                    pa = pf[npf % 6]
                    pb_ = pf[(npf + 1) % 6]
                    npf += 2
                    for g, pp in ((0, pa), (1, pb_)):
                        for c in range(4):
                            mm(pp, pp[:], ob_, ob_[:, 4 * g + c, ti * 128:(ti + 1) * 128],
                               wbr, wbr[:, 4 * g + c, db * 512:(db + 1) * 512], start=(c == 0), stop=(c == 3))
                    ta, tb_ = tA[0], tB[0]
                    ex(dve, [G_, pa], [ta], lambda: nc.vector.tensor_tensor(
                        out=ta[:], in0=G_[:, db * 512:(db + 1) * 512], in1=pa[:], op=ALU.mult))
                    ex(dve, [G_, pb_], [tb_], lambda: nc.vector.tensor_tensor(
                        out=tb_[:], in0=G_[:, D + db * 512:D + (db + 1) * 512], in1=pb_[:], op=ALU.mult))
                    ex(dve, [ta, tb_], [mg], lambda: nc.vector.tensor_tensor(
                        out=mg[:, db * 512:(db + 1) * 512], in0=ta[:], in1=tb_[:], op=ALU.add))
                for half in range(2):
                    pt = pbf[half]
                    for j in range(8):
                        k = half * 8 + j
                        tr(pt, pt[:, j * 128:(j + 1) * 128], mg, mg[:, k * 128:(k + 1) * 128], ident_b)
                    dst = mT[:, half * 8:(half + 1) * 8, :]
                    src = pt[:].rearrange("p (a b) -> p a b", a=8)
                    if half == 0:
                        ex(act, [pt], [mT], lambda: nc.scalar.copy(out=dst, in_=src))
                    else:
                        ex(dve, [pt], [mT], lambda: nc.vector.tensor_copy(dst, src))
                for obk in range(4):
                    pm = pf[npf % 6]
                    npf += 1
                    for k in range(16):
                        mm(pm, pm[:], mT, mT[:, k, :], wo, wo[:, k, obk * 512:(obk + 1) * 512],
                           start=(k == 0), stop=(k == 15))
                    ex(dve, [xt_, pm], [xt_], lambda: nc.vector.scalar_tensor_tensor(
                        out=xt_[:, 0, obk * 512:(obk + 1) * 512], in0=xt_[:, 0, obk * 512:(obk + 1) * 512],
                        scalar=ALPHA, in1=pm[:], op0=ALU.mult, op1=ALU.add))

            def stage2(tt):
                nonlocal npf
                r0 = tt * 128
                xt_ = xts[tt % 3]
                hv = xt_[:, 0, :]
                layer_norm(nc, kb, ex, hv, xt_, junk, junk_ap, st1, lnp, 0)
                kb.store(sp, xt_.res, x1_d[r0:r0 + 128, :], hv, [dram["x1"]])
                xb_ = x1b[0]
                ex(act, [xt_], [xb_], lambda: nc.scalar.copy(out=xb_[:], in_=hv))
                kb.store(sp, xb_.res, x1b_d[r0:r0 + 128, :], xb_[:], [dram["x1b"]])
                for kk in range(4):
                    pb = pf[npf % 6]
                    npf += 1
                    for j in range(4):
                        k = kk * 4 + j
                        tr(pb, pb[:, j * 128:(j + 1) * 128], xt_, xt_[:, 0, k * 128:(k + 1) * 128], ident_f)
                    dst = x1T[:, kk * 4:(kk + 1) * 4, :]
                    src = pb[:].rearrange("p (a b) -> p a b", a=4)
                    if kk % 2 == 0:
                        ex(act, [pb], [x1T], lambda: nc.scalar.copy(out=dst, in_=src))
                    else:
                        ex(dve, [pb], [x1T], lambda: nc.vector.tensor_copy(dst, src))
                pl = pf[npf % 6]
                npf += 1
                for k in range(16):
                    mm(pl, pl[:, 0:NE], x1T, x1T[:, k, :], wr, wr[:, k, :], start=(k == 0), stop=(k == 15))
                ex(dve, [pl], [st1], lambda: nc.vector.reduce_max(out=st1[:, 4:5], in_=pl[:, 0:NE], axis=AX.X))
                ex(dve, [st1], [st1], lambda: nc.vector.tensor_scalar_mul(out=st1[:, 5:6], in0=st1[:, 4:5], scalar1=-1.0))
                ex(act, [pl, st1], [eb, st1], lambda: nc.scalar.activation(
                    out=eb[:], in_=pl[:, 0:NE], func=AF.Exp, bias=st1[:, 5:6], scale=1.0, accum_out=st1[:, 6:7]))
                ex(dve, [st1], [st1], lambda: nc.vector.reciprocal(out=st1[:, 7:8], in_=st1[:, 6:7]))
                ex(dve, [eb, st1], [aff], lambda: nc.vector.tensor_scalar(
                    out=aff[:, tt * NE:(tt + 1) * NE], in0=eb[:], scalar1=st1[:, 7:8], scalar2=None, op0=ALU.mult))

            prefetch(0)
            prefetch(1)
            stage1(0)
            for tt in range(NT):
                prefetch(tt + 2)
                if tt + 1 < NT:
                    stage1(tt + 1)
                stage2(tt)
        kb.barrier()

        with contextlib.ExitStack() as st:
            lo = sbuf(st, "E_lo", [128, NE], F32)
            hi = sbuf(st, "E_hi", [128, NE], F32)
            mid = sbuf(st, "E_mid", [128, NE], F32)
            dd = sbuf(st, "E_dd", [128, NE], F32)
            sel = sbuf(st, "E_sel", [128, NE], F32)
            cmp_ = sbuf(st, "E_cmp", [128, NT * NE], F32)
            cnt = sbuf(st, "E_cnt", [128, NE], F32)
            mkb = sbuf(st, "E_mkb", [128, NT * NE], BF16)
            tri = sbuf(st, "E_tri", [128, 128], BF16)
            eoff = sbuf(st, "E_eoff", [128, NT * NE], F32)
            tt_ = sbuf(st, "E_tt", [128, NT * NE], F32)
            cum = sbuf(st, "E_cum", [128, NT * NE], F32)
            pos = sbuf(st, "E_pos", [128, NT * NE], F32)
            val = sbuf(st, "E_val", [128, NT * NE], F32)
            pc = psum(st, "E_pc")
            pw = psum(st, "E_pw")
            ptt = psum(st, "E_pt")
            kb.load(sp, tri.res, tri[:], tri_d[:, :])
            kb.load(sp, eoff.res, eoff[:], eoff_d[:, :])
            ex(dve, [], [lo], lambda: nc.vector.memset(lo[:], 0.0))
            ex(dve, [], [hi], lambda: nc.vector.memset(hi[:], 1.0))
            aff3 = aff[:].rearrange("p (t e) -> p t e", e=NE)

            def bc(b):
                a = b[:]
                return bass.AP(a.tensor, a.offset, [list(a.ap[0]), [0, NT], list(a.ap[1])])

            for _ in range(34):
                ex(dve, [lo, hi], [mid], lambda: nc.vector.tensor_tensor(out=mid[:], in0=lo[:], in1=hi[:], op=ALU.add))
                ex(dve, [mid], [mid], lambda: nc.vector.tensor_scalar_mul(out=mid[:], in0=mid[:], scalar1=0.5))
                ex(dve, [aff, mid], [cmp_], lambda: nc.vector.tensor_tensor(
                    out=cmp_[:].rearrange("p (t e) -> p t e", e=NE), in0=aff3, in1=bc(mid), op=ALU.is_gt))
                ex(dve, [cmp_], [cnt], lambda: nc.vector.reduce_sum(
                    out=cnt[:], in_=cmp_[:].rearrange("p (t e) -> p e t", e=NE), axis=AX.X))
                mm(pc, pc[:, 0:NE], ones_f, ones_f[:], cnt, cnt[:])
                ex(dve, [pc], [sel], lambda: nc.vector.tensor_scalar(
                    out=sel[:], in0=pc[:, 0:NE], scalar1=float(CAP) - 0.5, scalar2=None, op0=ALU.is_gt))
                ex(dve, [mid, lo], [dd], lambda: nc.vector.tensor_tensor(out=dd[:], in0=mid[:], in1=lo[:], op=ALU.subtract))
                ex(dve, [dd, sel], [dd], lambda: nc.vector.tensor_tensor(out=dd[:], in0=dd[:], in1=sel[:], op=ALU.mult))
                ex(dve, [dd, lo], [lo], lambda: nc.vector.tensor_tensor(out=lo[:], in0=lo[:], in1=dd[:], op=ALU.add))
                ex(dve, [mid, hi], [dd], lambda: nc.vector.tensor_tensor(out=dd[:], in0=hi[:], in1=mid[:], op=ALU.subtract))
                ex(dve, [dd, sel], [dd], lambda: nc.vector.tensor_tensor(out=dd[:], in0=dd[:], in1=sel[:], op=ALU.mult))
                ex(dve, [dd, mid], [hi], lambda: nc.vector.tensor_tensor(out=hi[:], in0=mid[:], in1=dd[:], op=ALU.add))
            ex(dve, [aff, lo], [cmp_], lambda: nc.vector.tensor_tensor(
                out=cmp_[:].rearrange("p (t e) -> p t e", e=NE), in0=aff3, in1=bc(lo), op=ALU.is_gt))
            ex(act, [cmp_], [mkb], lambda: nc.scalar.copy(out=mkb[:], in_=cmp_[:]))
            mm(pw, pw[:], tri, tri[:], mkb, mkb[:])
            mm(ptt, ptt[:], ones_b, ones_b[:], mkb, mkb[:])
            ex(act, [ptt], [tt_], lambda: nc.scalar.copy(out=tt_[:], in_=ptt[:]))
            ex(dve, [], [cum], lambda: nc.vector.memset(cum[:, 0:NE], 0.0))
            for t in range(1, NT):
                ex(dve, [cum, tt_], [cum], lambda t=t: nc.vector.tensor_tensor(
                    out=cum[:, t * NE:(t + 1) * NE], in0=cum[:, (t - 1) * NE:t * NE],
                    in1=tt_[:, (t - 1) * NE:t * NE], op=ALU.add))
            ex(dve, [pw, cum], [pos], lambda: nc.vector.tensor_tensor(out=pos[:], in0=cum[:], in1=pw[:], op=ALU.add))
            ex(dve, [pos], [val], lambda: nc.vector.tensor_scalar(
                out=val[:], in0=pos[:], scalar1=float(CAP) - 0.5, scalar2=None, op0=ALU.is_lt))
            ex(dve, [val, cmp_], [val], lambda: nc.vector.tensor_tensor(out=val[:], in0=val[:], in1=cmp_[:], op=ALU.mult))
            ex(dve, [aff, val], [gm], lambda: nc.vector.tensor_tensor(out=gm[:], in0=aff[:], in1=val[:], op=ALU.mult))
            ex(dve, [pos, eoff], [pos], lambda: nc.vector.tensor_tensor(out=pos[:], in0=pos[:], in1=eoff[:], op=ALU.add))
            ex(dve, [pos], [pos], lambda: nc.vector.tensor_scalar_add(out=pos[:], in0=pos[:], scalar1=-BIG))
            ex(dve, [pos, val], [pos], lambda: nc.vector.tensor_tensor(out=pos[:], in0=pos[:], in1=val[:], op=ALU.mult))
            ex(dve, [pos], [pos], lambda: nc.vector.tensor_scalar_add(out=pos[:], in0=pos[:], scalar1=BIG))
            ex(dve, [pos], [idx_i], lambda: nc.vector.tensor_copy(idx_i[:], pos[:]))
            if DEBUG:
                kb.store(sp, aff.res, aff_d[:, :], aff[:])
        kb.barrier()

        with contextlib.ExitStack() as st:
            xb = [sbuf(st, "F_xb%d" % i, [128, D], BF16) for i in range(3)]
            for tt in range(NT):
                b = xb[tt % 3]
                kb.load(sp, b.res, b[:], x1b_d[tt * 128:(tt + 1) * 128, :], [dram["x1b"]])
                for e in range(NE):
                    col = tt * NE + e
                    if b.res.dout is None:
                        b.res.dout = kb.dsem()
                    kb.dma(pool, None, None, [b.res, idx_i.res], [], b.res.dout,
                           fn=lambda b=b, col=col: nc.gpsimd.indirect_dma_start(
                               out=xin_d[:, :], out_offset=bass.IndirectOffsetOnAxis(ap=idx_i[:, col:col + 1], axis=0),
                               in_=b[:], in_offset=None, bounds_check=bc_reg, oob_is_err=False))
        kb.barrier()

        with contextlib.ExitStack() as st:
            xi = [sbuf(st, "G_xi%d" % i, [128, 4, D], BF16) for i in range(2)]
            xiT = [sbuf(st, "G_xiT%d" % i, [128, 16, 512], BF16) for i in range(2)]
            hT = sbuf(st, "G_hT", [128, 16, 512], BF16)
            wpc = [sbuf(st, "G_wp%d" % i, [128, 16, 512], BF16) for i in range(5)]
            sgl = [sbuf(st, "G_sg%d" % i, [128, 512], F32) for i in range(2)]
            ost = [sbuf(st, "G_os%d" % i, [128, 512], BF16) for i in range(3)]
            ptr = [psum(st, "G_pt%d" % i, BF16) for i in range(2)]
            pgu = [psum(st, "G_pg%d" % i) for i in range(4)]
            pdn = [psum(st, "G_pd%d" % i) for i in range(2)]
            nos = 0
            npd = 0

            def piece(src_d, e, cblk):
                return src_d[e * D:(e + 1) * D, cblk * 512:(cblk + 1) * 512].rearrange("(k p) c -> p k c", p=128)

            wsrcs = []
            for e in range(NE):
                for fb in range(4):
                    wsrcs.append(piece(wg_d, e, fb))
                    wsrcs.append(piece(wu_d, e, fb))
                for db in range(4):
                    wsrcs.append(piece(wd_d, e, db))
            wstream = Stream(kb, pool, wpc, wsrcs, hold=2)

            def load_x(e):
                x_ = xi[e % 2]
                kb.load(sp, x_.res, x_[:], xin_d[e * CAP:(e + 1) * CAP, :].rearrange("(s p) d -> p s d", p=128))

            def transposes(e):
                x_ = xi[e % 2]
                xt_ = xiT[e % 2]
                for kp in range(8):
                    pt = ptr[kp % 2]
                    for k2 in range(2):
                        k = kp * 2 + k2
                        for s_ in range(4):
                            tr(pt, pt[:, (k2 * 4 + s_) * 128:(k2 * 4 + s_ + 1) * 128], x_,
                               x_[:, s_, k * 128:(k + 1) * 128], ident_b)
                    dst = xt_[:, kp * 2:kp * 2 + 2, :]
                    src = pt[:].rearrange("p (a b) -> p a b", a=2)
                    if kp % 2 == 0:
                        ex(act, [pt], [xt_], lambda: nc.scalar.copy(out=dst, in_=src))
                    else:
                        ex(dve, [pt], [xt_], lambda: nc.vector.tensor_copy(dst, src))

            load_x(0)
            wstream.get(0)
            transposes(0)
            for e in range(NE):
                if e + 1 < NE:
                    load_x(e + 1)
                xt_ = xiT[e % 2]
                for fb in range(4):
                    wg = wstream.get(e * 12 + fb * 2)
                    wu = wstream.get(e * 12 + fb * 2 + 1)
                    for fi in range(4):
                        fc = fb * 4 + fi
                        pg = pgu[2 * (fc % 2)]
                        pu = pgu[2 * (fc % 2) + 1]
                        for k in range(16):
                            mm(pg, pg[:], wg, wg[:, k, fi * 128:(fi + 1) * 128], xt_, xt_[:, k, :],
                               start=(k == 0), stop=(k == 15))
                        for k in range(16):
                            mm(pu, pu[:], wu, wu[:, k, fi * 128:(fi + 1) * 128], xt_, xt_[:, k, :],
                               start=(k == 0), stop=(k == 15))
                        s_g = sgl[fc % 2]
                        ex(act, [pg], [s_g], lambda: nc.scalar.activation(out=s_g[:], in_=pg[:], func=AF.Silu))
                        ex(dve, [s_g, pu], [hT], lambda: nc.vector.tensor_tensor(
                            out=hT[:, fc, :], in0=s_g[:], in1=pu[:], op=ALU.mult))
                if e + 1 < NE:
                    transposes(e + 1)
                for db in range(4):
                    wd = wstream.get(e * 12 + 8 + db)
                    for s_ in range(4):
                        pd = pdn[npd % 2]
                        npd += 1
                        for fc in range(16):
                            mm(pd, pd[:], hT, hT[:, fc, s_ * 128:(s_ + 1) * 128], wd, wd[:, fc, :],
                               start=(fc == 0), stop=(fc == 15))
                        o_ = ost[nos % 3]
                        nos += 1
                        if s_ % 2 == 0:
                            ex(act, [pd], [o_], lambda: nc.scalar.copy(out=o_[:], in_=pd[:]))
                        else:
                            ex(dve, [pd], [o_], lambda: nc.vector.tensor_copy(o_[:], pd[:]))
                        r0 = e * CAP + s_ * 128
                        kb.store(sp, o_.res, yo_d[r0:r0 + 128, db * 512:(db + 1) * 512], o_[:], [dram["yo"]])
        kb.barrier()

        with contextlib.ExitStack() as st:
            lnp = sbuf(st, "H_lnp", [128, 2 * D], F32)
            kb.load(sp, lnp.res, lnp[:], ln_d[:, 2 * D:4 * D])
            NG = 32
            xa = [sbuf(st, "H_xa%d" % i, [128, 1, D], F32) for i in range(2)]
            gb = [sbuf(st, "H_g%d" % i, [128, D], BF16) for i in range(NG)]
            dg = [sbuf(st, "H_dg%d" % i, [128, 128], BF16) for i in range(4)]
            junk = sbuf(st, "H_junk", [128, D], F32)
            st1 = sbuf(st, "H_st1", [128, 8], F32)
            pacc = [[psum(st, "H_p%d%d" % (i, c)) for c in range(4)] for i in range(2)]
            gset = [Res("gset0"), Res("gset1")]
            for i_, g_ in enumerate(gb):
                g_.res = gset[i_ // NE]
                ex(dve, [], [g_], lambda: nc.vector.memset(g_[:], 0.0))

            def gathers(tt):
                rs = gset[tt % 2]
                if rs.din is None:
                    rs.din = kb.dsem()
                lst = []
                fns = []
                for e in range(NE):
                    col = tt * NE + e
                    g_ = gb[(tt % 2) * NE + e]
                    g_.res = rs
                    fns.append(lambda g_=g_, col=col: nc.gpsimd.indirect_dma_start(
                        out=g_[:], out_offset=None, in_=yo_d[:, :],
                        in_offset=bass.IndirectOffsetOnAxis(ap=idx_i[:, col:col + 1], axis=0),
                        bounds_check=bc_reg, oob_is_err=False))
                    lst.append(g_)
                kb.dma_group(pool, fns, [idx_i.res, dram["yo"]], [rs], rs.din)
                return lst

            nd = 0
            gnext = gathers(0)
            for tt in range(NT):
                a_ = xa[tt % 2]
                r0 = tt * 128
                kb.load(sp, a_.res, a_[:, 0, :], x1_d[r0:r0 + 128, :], [dram["x1"]])
                glist = gnext
                if tt + 1 < NT:
                    gnext = gathers(tt + 1)
                pa = pacc[tt % 2]
                for e in range(NE):
                    col = tt * NE + e
                    d_ = dg[nd % 4]
                    nd += 1
                    ex(dve, [ident_b, gm], [d_], lambda: nc.vector.tensor_scalar(
                        out=d_[:], in0=ident_b[:], scalar1=gm[:, col:col + 1], scalar2=None, op0=ALU.mult))
                    g_ = glist[e]
                    for cb in range(4):
                        mm(pa[cb], pa[cb][:], d_, d_[:], g_, g_[:, cb * 512:(cb + 1) * 512],
                           start=(e == 0), stop=(e == NE - 1))
                for cb in range(4):
                    ex(dve, [a_, pa[cb]], [a_], lambda: nc.vector.scalar_tensor_tensor(
                        out=a_[:, 0, cb * 512:(cb + 1) * 512], in0=a_[:, 0, cb * 512:(cb + 1) * 512],
                        scalar=ALPHA, in1=pa[cb][:], op0=ALU.mult, op1=ALU.add))
                layer_norm(nc, kb, ex, a_[:, 0, :], a_, junk, junk[:], st1, lnp, 0)
                kb.store(sp, a_.res, out_d[r0:r0 + 128, :], a_[:, 0, :])
        kb.barrier()
        nc._marks = kb.marks
    return nc


def layer_norm(nc, kb, ex, hv, hb, junk, junk_ap, st1, lnp, off):
    dve, act, pool = kb.dve, kb.act, kb.pool
    ex(dve, [hb], [st1], lambda: nc.vector.reduce_sum(out=st1[:, 0:1], in_=hv, axis=AX.X))
    ex(dve, [st1], [st1], lambda: nc.vector.tensor_scalar_mul(out=st1[:, 1:2], in0=st1[:, 0:1], scalar1=-1.0 / D))
    ex(dve, [hb, st1], [hb], lambda: nc.vector.tensor_scalar(
        out=hv, in0=hv, scalar1=st1[:, 1:2], scalar2=None, op0=ALU.add))
    ex(act, [hb], [junk, st1], lambda: nc.scalar.activation(
        out=junk_ap, in_=hv, func=AF.Square, accum_out=st1[:, 2:3]))
    ex(act, [st1], [st1], lambda: nc.scalar.activation(
        out=st1[:, 3:4], in_=st1[:, 2:3], func=AF.Sqrt, bias=LN_EPS, scale=1.0 / D))
    ex(dve, [st1], [st1], lambda: nc.vector.reciprocal(out=st1[:, 3:4], in_=st1[:, 3:4]))
    ex(dve, [hb, st1, lnp], [hb], lambda: nc.vector.scalar_tensor_tensor(
        out=hv, in0=hv, scalar=st1[:, 3:4], in1=lnp[:, off:off + D], op0=ALU.mult, op1=ALU.mult))
    ex(dve, [hb, lnp], [hb], lambda: nc.vector.tensor_tensor(
        out=hv, in0=hv, in1=lnp[:, off + D:off + 2 * D], op=ALU.add))


DEBUG_OUT = ()


def _consts():
    import ml_dtypes
    bf = ml_dtypes.bfloat16
    pos = np.arange(S, dtype=np.float32)
    c = {}
    for nm, half, rot, nblk in (("A", 16, 32, 4), ("B", 8, 16, 8)):
        inv = (np.float32(500000.0) ** (-2.0 * np.arange(half, dtype=np.float32) / rot)).astype(np.float32)
        ang = (pos[:, None] * inv[None, :]).astype(np.float32)
        cs, sn = np.cos(ang).astype(np.float32), np.sin(ang).astype(np.float32)
        cc = np.concatenate([cs, cs], axis=1)
        ss = np.concatenate([sn, -sn], axis=1)
        c["cc" + nm] = np.ascontiguousarray(np.tile(cc, (1, nblk)))
        c["ss" + nm] = np.ascontiguousarray(np.tile(ss, (1, nblk)))
    a = np.arange(128)[:, None]
    b = np.arange(128)[None, :]
    m = np.concatenate([(a - b >= 64), (np.abs(a - b) <= 64), (b - a >= 64)], axis=1)
    c["bandmask"] = m.astype(np.float32).astype(bf)
    c["ident_f"] = np.eye(128, dtype=np.float32)
    c["ident_b"] = np.eye(128, dtype=np.float32).astype(bf)
    c["tri_b"] = (a < b).astype(np.float32).astype(bf)
    eo = np.tile((np.arange(NE, dtype=np.float32) * CAP)[None, :], (128, NT))
    c["eoff"] = np.ascontiguousarray(eo.astype(np.float32))
    return c


_CACHE = {}


def kernel(x, w_in, lambda_q1, lambda_k1, lambda_q2, lambda_k2, diff_norm_w, w_branch, w_out,
           ln1_g, ln1_b, w_router, w_gate, w_up, w_down, ln2_g, ln2_b):
    f = lambda a: np.ascontiguousarray(np.asarray(a, dtype=np.float32))
    x = f(x)
    if "nc" not in _CACHE:
        _CACHE["nc"] = build_program()
        _CACHE["c"] = _consts()
    nc = _CACHE["nc"]
    c = _CACHE["c"]
    lam4 = np.concatenate([f(lambda_q1)[0], f(lambda_k1)[0], f(lambda_q2)[0], f(lambda_k2)[0]])[None, :]
    lnp = np.concatenate([f(ln1_g)[0], f(ln1_b)[0], f(ln2_g)[0], f(ln2_b)[0]])[None, :]
    shared = {
        "w_in": f(w_in)[0], "w_branch": f(w_branch)[0].reshape(1024, D), "w_out": f(w_out)[0],
        "w_router": f(w_router)[0], "w_gate": f(w_gate)[0].reshape(NE * D, D),
        "w_up": f(w_up)[0].reshape(NE * D, D), "w_down": f(w_down)[0].reshape(NE * D, D),
        "lam4": np.ascontiguousarray(np.broadcast_to(lam4, (128, 256))),
        "nw": np.ascontiguousarray(f(diff_norm_w)[0].reshape(128, 1)),
        "lnp": np.ascontiguousarray(np.broadcast_to(lnp, (128, 4 * D))),
    }
    shared.update(c)
    nb = x.shape[0]
    in_maps = []
    for core in range(8):
        m = dict(shared)
        m["x"] = np.ascontiguousarray(x[(core // 2) % nb])
        in_maps.append(m)
    res = run_bass_kernel_spmd(nc, in_maps, core_ids=list(range(8)))
    out = np.stack([np.asarray(res.results[2 * b]["out"], dtype=np.float32) for b in range(nb)], axis=0)
    return out
```

```python
import contextlib
import numpy as np
import concourse.bass as bass
import concourse.mybir as mybir
from concourse.bass_utils import run_bass_kernel_spmd

F32 = mybir.dt.float32
BF16 = mybir.dt.bfloat16
I32 = mybir.dt.int32
ALU = mybir.AluOpType
AF = mybir.ActivationFunctionType
AX = mybir.AxisListType

S = 4096
D = 2048
NT = S // 128
NE = 16
CAP = 512
ALPHA = 2.0 ** 0.25
LN_EPS = 1e-5
BIG = float(1 << 20)
DEBUG = False
FUSE_WAIT = True


class Ev:
    __slots__ = ("sem", "val")

    def __init__(self, sem, val):
        self.sem = sem
        self.val = val


class DSem:
    def __init__(self, h):
        self.h = h
        self.n = 0


class Res:
    def __init__(self, name):
        self.name = name
        self.w = None
        self.r = {}
        self.din = None
        self.dout = None


class Q:
    def __init__(self, eng, sem, is_pe=False):
        self.eng = eng
        self.sem = sem
        self.n = 0
        self.known = {}
        self.is_pe = is_pe

    def wait(self, ev):
        if ev is None:
            return
        if ev.sem is self.sem and self.is_pe:
            return
        k = id(ev.sem)
        if self.known.get(k, 0) >= ev.val:
            return
        self.eng.wait_ge(ev.sem, ev.val)
        self.known[k] = ev.val


class KB:
    def __init__(self, nc, es):
        self.nc = nc
        self.es = es
        self.nsem = 0
        self.pe = Q(nc.tensor, self.newsem("q_pe"), True)
        self.act = Q(nc.scalar, self.newsem("q_act"))
        self.dve = Q(nc.vector, self.newsem("q_dve"))
        self.pool = Q(nc.gpsimd, self.newsem("q_pool"))
        self.sp = Q(nc.sync, self.newsem("q_sp"))
        self.queues = [self.pe, self.act, self.dve, self.pool, self.sp]
        self.dsems = []

    def newsem(self, name):
        self.nsem += 1
        return self.es.enter_context(self.nc.semaphore(name))

    def dsem(self):
        free = getattr(self, "free_dsems", None)
        if free:
            return free.pop()
        d = DSem(self.newsem("d%d" % len(self.dsems)))
        self.dsems.append(d)
        return d

    def dma_group(self, q, fns, reads, writes, ds):
        self._pre(q, reads, writes)
        for fn in fns:
            inst = fn()
            ds.n += 16
            inst.then_inc(ds.h, 16)
        self._post(Ev(ds.h, ds.n), reads, writes)

    def _need(self, q, reads, writes):
        need = {}

        def add(ev):
            if ev is None:
                return
            if ev.sem is q.sem and q.is_pe:
                return
            k = id(ev.sem)
            if q.known.get(k, 0) >= ev.val:
                return
            if k not in need or need[k].val < ev.val:
                need[k] = ev

        for r in reads:
            add(r.w)
        for w in writes:
            add(w.w)
            for ev in w.r.values():
                add(ev)
        return list(need.values())

    def _pre(self, q, reads, writes, keep_last=False):
        evs = self._need(q, reads, writes)
        last = None
        if keep_last and evs:
            last = evs.pop()
        for ev in evs:
            q.eng.wait_ge(ev.sem, ev.val)
            q.known[id(ev.sem)] = ev.val
        return last

    def _post(self, ev, reads, writes):
        for r in reads:
            r.r[id(ev.sem)] = ev
        for w in writes:
            w.w = ev
            w.r = {}

    def op(self, q, reads, writes, fn):
        last = self._pre(q, reads, writes, keep_last=FUSE_WAIT)
        inst = fn()
        if last is not None:
            inst._wait_ge(last.sem, last.val)
            q.known[id(last.sem)] = last.val
        q.n += 1
        inst.then_inc(q.sem, 1)
        self._post(Ev(q.sem, q.n), reads, writes)

    def dma(self, q, out, in_, reads, writes, ds, fn=None):
        self._pre(q, reads, writes)
        if fn is None:
            inst = q.eng.dma_start(out=out, in_=in_)
        else:
            inst = fn()
        ds.n += 16
        inst.then_inc(ds.h, 16)
        self._post(Ev(ds.h, ds.n), reads, writes)

    def load(self, q, res, out, in_, src=()):
        if res.din is None:
            res.din = self.dsem()
        self.dma(q, out, in_, list(src), [res], res.din)

    def store(self, q, res, out, in_, dst=()):
        if res.dout is None:
            res.dout = self.dsem()
        self.dma(q, out, in_, [res], list(dst), res.dout)

    def barrier(self):
        self.marks = getattr(self, "marks", [])
        self.marks.append({"pe": self.pe.n, "act": self.act.n, "dve": self.dve.n, "pool": self.pool.n})
        for q in self.queues:
            for p in self.queues:
                if p is not q and p.n > 0:
                    q.wait(Ev(p.sem, p.n))
            for d in self.dsems:
                if d.n > 0:
                    q.wait(Ev(d.h, d.n))
        self.free_dsems = list(self.dsems)


class Stream:
    def __init__(self, kb, q, bufs, srcs, hold=1):
        self.kb, self.q, self.bufs, self.srcs = kb, q, bufs, srcs
        self.n = 0
        self.hold = hold

    def get(self, i):
        last = min(i + len(self.bufs) - self.hold, len(self.srcs) - 1)
        while self.n <= last:
            b = self.bufs[self.n % len(self.bufs)]
            self.kb.load(self.q, b.res, b[:], self.srcs[self.n])
            self.n += 1
        return self.bufs[i % len(self.bufs)]


class Buf:
    def __init__(self, t, name):
        self.t = t
        self.res = Res(name)

    def __getitem__(self, k):
        return self.t[k]


def build_program():
    nc = bass.Bass("TRN2", target_bir_lowering=False)
    es = contextlib.ExitStack()

    def din(name, shape, dt=F32):
        return nc.dram_tensor(name, list(shape), dt, kind="ExternalInput").ap()

    def dscr(name, shape, dt):
        kind = "ExternalOutput" if (DEBUG and name in DEBUG_OUT) else "Internal"
        return nc.dram_tensor(name, list(shape), dt, kind=kind).ap()

    x_d = din("x", [S, D])
    win_d = din("w_in", [D, 10240])
    wbr_d = din("w_branch", [1024, D])
    wout_d = din("w_out", [D, D])
    wr_d = din("w_router", [D, NE])
    wg_d = din("w_gate", [NE * D, D])
    wu_d = din("w_up", [NE * D, D])
    wd_d = din("w_down", [NE * D, D])
    lam_d = din("lam4", [128, 256])
    nw_d = din("nw", [128, 1])
    ln_d = din("lnp", [128, 4 * D])
    ccA_d = din("ccA", [S, 128])
    ssA_d = din("ssA", [S, 128])
    ccB_d = din("ccB", [S, 128])
    ssB_d = din("ssB", [S, 128])
    msk_d = din("bandmask", [128, 384], BF16)
    idf_d = din("ident_f", [128, 128])
    idb_d = din("ident_b", [128, 128], BF16)
    tri_d = din("tri_b", [128, 128], BF16)
    eoff_d = din("eoff", [128, NT * NE])
    out_d = nc.dram_tensor("out", [S, D], F32, kind="ExternalOutput").ap()

    QT_d = dscr("QT_s", [16, 128, S], BF16)
    KT_d = dscr("KT_s", [16, 128, S], BF16)
    V_d = dscr("V_s", [S, D], BF16)
    oT_d = dscr("oT_s", [8, 128, S], BF16)
    x1_d = dscr("x1_s", [S, D], F32)
    x1b_d = dscr("x1b_s", [S, D], BF16)
    xin_d = dscr("xin_s", [NE * CAP, D], BF16)
    yo_d = dscr("yo_s", [NE * CAP, D], BF16)
    aff_d = dscr("aff_s", [128, NT * NE], F32)
    G_d = dscr("G_s", [S, 2 * D], BF16)

    with es:
        kb = KB(nc, es)
        pe, act, dve, pool, sp = kb.pe, kb.act, kb.dve, kb.pool, kb.sp

        def sbuf(st, name, shape, dt):
            return Buf(st.enter_context(nc.sbuf_tensor("sb_" + name, list(shape), dt)), name)

        def psum(st, name, dt=F32):
            shape = [128, 512] if dt == F32 else [128, 1024]
            return Buf(st.enter_context(nc.psum_tensor("ps_" + name, shape, dt)), name)

        def mm(out_b, out_ap, l_b, l_ap, r_b, r_ap, start=True, stop=True):
            kb.op(pe, [l_b.res, r_b.res], [out_b.res],
                  lambda: nc.tensor.matmul(out_ap, l_ap, r_ap, start=start, stop=stop))

        def tr(out_b, out_ap, in_b, in_ap, id_b):
            kb.op(pe, [in_b.res, id_b.res], [out_b.res],
                  lambda: nc.tensor.transpose(out_ap, in_ap, id_b[:]))

        def ex(q, reads, writes, fn):
            kb.op(q, [b.res for b in reads], [b.res for b in writes], fn)

        bc_reg = nc.gpsimd.alloc_register("bc_reg")
        nc.gpsimd.reg_mov(bc_reg, NE * CAP - 1)
        dram = {n: Res(n) for n in ["xT", "QT", "KT", "V", "oT", "x1", "x1b", "xin", "yo", "G"]}

        gst = es
        ident_f = sbuf(gst, "ident_f", [128, 128], F32)
        ident_b = sbuf(gst, "ident_b", [128, 128], BF16)
        ones_b = sbuf(gst, "ones_b", [128, 128], BF16)
        ones_f = sbuf(gst, "ones_f", [128, 128], F32)
        lam4 = sbuf(gst, "lam4", [128, 256], F32)
        nw = sbuf(gst, "nw", [128, 1], F32)
        neglam = sbuf(gst, "neglam", [128, 1], F32)
        ltmp = sbuf(gst, "ltmp", [128, 128], F32)
        lsum = sbuf(gst, "lsum", [128, 2], F32)
        aff = sbuf(gst, "aff", [128, NT * NE], F32)
        idx_i = sbuf(gst, "idx_i", [128, NT * NE], I32)
        gm = sbuf(gst, "gm", [128, NT * NE], F32)
        kb.load(sp, ident_f.res, ident_f[:], idf_d[:, :])
        kb.load(sp, ident_b.res, ident_b[:], idb_d[:, :])
        kb.load(sp, lam4.res, lam4[:], lam_d[:, :])
        kb.load(sp, nw.res, nw[:], nw_d[:, :])
        ex(pool, [], [ones_b], lambda: nc.gpsimd.memset(ones_b[:], 1.0))
        ex(pool, [], [ones_f], lambda: nc.gpsimd.memset(ones_f[:], 1.0))
        ex(dve, [lam4], [ltmp], lambda: nc.vector.tensor_tensor(
            out=ltmp[:].rearrange("p (a c) -> p a c", a=2),
            in0=lam4[:].rearrange("p (a b c) -> p a b c", a=2, b=2)[:, :, 0, :],
            in1=lam4[:].rearrange("p (a b c) -> p a b c", a=2, b=2)[:, :, 1, :], op=ALU.mult))
        ex(dve, [ltmp], [lsum], lambda: nc.vector.reduce_sum(
            out=lsum[:], in_=ltmp[:].rearrange("p (a c) -> p a c", a=2), axis=AX.X))
        ex(act, [lsum], [lsum], lambda: nc.scalar.activation(out=lsum[:], in_=lsum[:], func=AF.Exp))
        ex(dve, [lsum], [neglam], lambda: nc.vector.tensor_tensor(
            out=neglam[:], in0=lsum[:, 1:2], in1=lsum[:, 0:1], op=ALU.subtract))
        ex(dve, [neglam], [neglam], lambda: nc.vector.tensor_scalar_add(
            out=neglam[:], in0=neglam[:], scalar1=-0.2))
        ex(dve, [nw], [nw], lambda: nc.vector.tensor_scalar_mul(out=nw[:], in0=nw[:], scalar1=0.8))

        with contextlib.ExitStack() as st:
            xT = sbuf(st, "A_xT", [128, 16, 2048], BF16)
            tabs = [sbuf(st, "A_tab%d" % i, [128, 16, 128], F32) for i in range(4)]
            xt = [sbuf(st, "A_xt%d" % i, [128, D], F32) for i in range(2)]
            wp = [sbuf(st, "A_wp%d" % i, [128, 16, 512], BF16) for i in range(2)]
            zt = [sbuf(st, "A_zt%d" % i, [128, 512], F32) for i in range(2)]
            uu = [sbuf(st, "A_uu%d" % i, [128, 128], F32) for i in range(2)]
            vv = [sbuf(st, "A_vv%d" % i, [128, 128], F32) for i in range(2)]
            zb = [sbuf(st, "A_zb%d" % i, [128, 512], BF16) for i in range(4)]
            qst = [sbuf(st, "A_qst%d" % i, [128, 4, 2048], BF16) for i in range(2)]
            pmm = [psum(st, "A_pm%d" % i) for i in range(4)]
            ptr = [psum(st, "A_pt%d" % i, BF16) for i in range(2)]
            tab_src = [ccA_d, ssA_d, ccB_d, ssB_d]
            nxt = 0
            for hf in range(2):
                t0 = hf * 2048
                for i in range(4):
                    kb.load(sp, tabs[i].res, tabs[i][:],
                            tab_src[i][t0:t0 + 2048, :].rearrange("(t p) c -> p t c", p=128))
                for tl in range(16):
                    xs = xt[tl % 2]
                    r0 = t0 + tl * 128
                    kb.load(sp, xs.res, xs[:], x_d[r0:r0 + 128, :])
                    for kk in range(4):
                        pb = pmm[kk]
                        for j in range(4):
                            k = kk * 4 + j
                            tr(pb, pb[:, j * 128:(j + 1) * 128], xs, xs[:, k * 128:(k + 1) * 128], ident_f)
                        dst = xT[:, kk * 4:(kk + 1) * 4, tl * 128:(tl + 1) * 128]
                        src = pb[:].rearrange("p (a b) -> p a b", a=4)
                        if kk % 2 == 0:
                            ex(act, [pb], [xT], lambda dst=dst, src=src: nc.scalar.copy(out=dst, in_=src))
                        else:
                            ex(dve, [pb], [xT], lambda dst=dst, src=src: nc.vector.tensor_copy(dst, src))
                NCB = 20

                def col_of(cb):
                    return cb * 512 if cb < 12 else 6144 + (cb - 12) * 512

                wsrcs = [win_d[:, col_of(cb):col_of(cb) + 512].rearrange("(k p) c -> p k c", p=128)
                         for cb in range(NCB)]
                wstream = Stream(kb, pool, wp, wsrcs)
                for cb in range(NCB):
                    w = wstream.get(cb)
                    is_v = cb in (6, 7, 8, 11)
                    is_g = cb >= 12
                    is_b = cb in (9, 10)
                    qs = qst[cb % 2]
                    pend = []

                    def flush_one():
                        z_b, tl_ = pend.pop(0)
                        pt = ptr[tl_ % 2]
                        for h in range(4):
                            tr(pt, pt[:, h * 128:(h + 1) * 128], z_b, z_b[:, h * 128:(h + 1) * 128], ident_b)
                        dst = qs[:, :, tl_ * 128:(tl_ + 1) * 128]
                        src = pt[:, 0:512].rearrange("p (a b) -> p a b", a=4)
                        ex(dve, [pt], [qs], lambda: nc.vector.tensor_copy(dst, src))

                    for tl in range(16):
                        pb = pmm[nxt % 4]
                        nxt += 1
                        for k in range(16):
                            mm(pb, pb[:], xT, xT[:, k, tl * 128:(tl + 1) * 128], w, w[:, k, :],
                               start=(k == 0), stop=(k == 15))
                        z_b = zb[tl % 4]
                        r0 = t0 + tl * 128
                        if is_v:
                            ex(act, [pb], [z_b], lambda: nc.scalar.copy(out=z_b[:], in_=pb[:]))
                            c0 = (cb - 6) * 512 if cb < 9 else 1536
                            kb.store(sp, z_b.res, V_d[r0:r0 + 128, c0:c0 + 512], z_b[:], [dram["V"]])
                            continue
                        if is_g:
                            ex(act, [pb], [z_b], lambda: nc.scalar.activation(out=z_b[:], in_=pb[:], func=AF.Sigmoid))
                            c0 = (cb - 12) * 512
                            kb.store(sp, z_b.res, G_d[r0:r0 + 128, c0:c0 + 512], z_b[:], [dram["G"]])
                            continue
                        z_t = zt[tl % 2]
                        u_t = uu[tl % 2]
                        v_t = vv[tl % 2]
                        ex(act, [pb], [z_t], lambda: nc.scalar.copy(out=z_t[:], in_=pb[:]))
                        if not is_b:
                            nb, bw, hh_ = 4, 128, 16
                            cc, ss = tabs[0], tabs[1]
                        else:
                            nb, bw, hh_ = 8, 64, 8
                            cc, ss = tabs[2], tabs[3]
                        z3 = z_t[:].rearrange("p (a b) -> p a b", a=nb)
                        zb3 = z_b[:].rearrange("p (a b) -> p a b", a=nb)
                        u3 = u_t[:].rearrange("p (a b) -> p a b", a=nb)
                        v3 = v_t[:].rearrange("p (a b) -> p a b", a=nb)
                        c3 = cc[:, tl, :].rearrange("p (a b) -> p a b", a=nb)
                        s3 = ss[:, tl, :].rearrange("p (a b) -> p a b", a=nb)
                        rw = 2 * hh_
                        ex(dve, [z_t, cc], [u_t], lambda: nc.vector.tensor_tensor(
                            out=u3, in0=z3[:, :, 0:rw], in1=c3, op=ALU.mult))
                        ex(dve, [z_t, ss], [v_t], lambda: nc.vector.tensor_tensor(
                            out=v3, in0=z3[:, :, 0:rw], in1=s3, op=ALU.mult))
                        ex(pool, [u_t, v_t], [z_b], lambda: nc.gpsimd.tensor_tensor(
                            out=zb3[:, :, 0:hh_], in0=u3[:, :, 0:hh_], in1=v3[:, :, hh_:rw], op=ALU.add))
                        ex(pool, [u_t, v_t], [z_b], lambda: nc.gpsimd.tensor_tensor(
                            out=zb3[:, :, hh_:rw], in0=u3[:, :, hh_:rw], in1=v3[:, :, 0:hh_], op=ALU.add))
                        ex(act, [z_t], [z_b], lambda: nc.scalar.copy(out=zb3[:, :, rw:bw], in_=z3[:, :, rw:bw]))
                        pend.append((z_b, tl))
                        if len(pend) > 2:
                            flush_one()
                    while pend:
                        flush_one()
                    if not (is_v or is_g):
                        if cb < 3:
                            tgt, key, h0 = QT_d, "QT", cb * 4
                        elif cb < 6:
                            tgt, key, h0 = KT_d, "KT", (cb - 3) * 4
                        elif cb == 9:
                            tgt, key, h0 = QT_d, "QT", 12
                        else:
                            tgt, key, h0 = KT_d, "KT", 12
                        for h in range(4):
                            kb.store(sp, qs.res, tgt[h0 + h, :, t0:t0 + 2048], qs[:, h, :], [dram[key]])
        kb.barrier()

        with contextlib.ExitStack() as st:
            msk = sbuf(st, "B_msk", [128, 384], BF16)
            kb.load(sp, msk.res, msk[:], msk_d[:, :])
            qT = [sbuf(st, "B_q%d" % i, [128, S], BF16) for i in range(2)]
            kT = [sbuf(st, "B_k%d" % i, [128, S], BF16) for i in range(2)]
            vs = [sbuf(st, "B_v%d" % i, [128, 32, 128], BF16) for i in range(2)]
            accn = sbuf(st, "B_accn", [128, S], F32)
            accd = sbuf(st, "B_accd", [128, S], F32)
            ob = sbuf(st, "B_ob", [128, S], BF16)
            pT = [sbuf(st, "B_pT%d" % i, [128, 384], BF16) for i in range(3)]
            ps_s = [psum(st, "B_ps%d" % i) for i in range(3)]
            ps_n = [psum(st, "B_pn%d" % i) for i in range(2)]
            ps_d = [psum(st, "B_pd%d" % i) for i in range(2)]
            it = 0
            blk = 0
            for hh in range(4):
                for g, r in enumerate((1, 4, 16)):
                    hd = g * 4 + hh
                    L = S // r
                    nj = L // 128
                    q_, k_, v_ = qT[it % 2], kT[it % 2], vs[it % 2]
                    it += 1
                    kb.load(sp, q_.res, q_[:], QT_d[hd, :, :], [dram["QT"]])
                    kb.load(sp, k_.res, k_[:], KT_d[hd, :, :], [dram["KT"]])
                    for ph in range(r):
                        kb.load(sp, v_.res, v_[:, ph * nj:(ph + 1) * nj, :],
                                V_d[ph:S:r, hd * 128:(hd + 1) * 128].rearrange("(j a) c -> a j c", a=128),
                                [dram["V"]])
                    nblk = r * nj
                    for b0 in range(0, nblk, 4):
                        pn = ps_n[(b0 // 4) % 2]
                        pd = ps_d[(b0 // 4) % 2]
                        for bi in range(b0, b0 + 4):
                            ph, j = bi // nj, bi % nj
                            kts = [kt for kt in (j - 1, j, j + 1) if 0 <= kt < nj]
                            lo = (kts[0] - j + 1) * 128
                            hi = (kts[-1] - j + 2) * 128
                            pss = ps_s[blk % 3]
                            p_t = pT[blk % 3]
                            blk += 1
                            qsl = q_[:, ph + r * 128 * j: ph + r * 128 * j + r * 127 + 1: r]
                            for kt in kts:
                                off = (kt - j + 1) * 128
                                ksl = k_[:, ph + r * 128 * kt: ph + r * 128 * kt + r * 127 + 1: r]
                                mm(pss, pss[:, off:off + 128], k_, ksl, q_, qsl)
                            ex(act, [pss], [p_t], lambda: nc.scalar.activation(
                                out=p_t[:, lo:hi], in_=pss[:, lo:hi], func=AF.Exp, scale=128.0 ** -0.5))
                            ex(dve, [p_t, msk], [p_t], lambda: nc.vector.tensor_tensor(
                                out=p_t[:, lo:hi], in0=p_t[:, lo:hi], in1=msk[:, lo:hi], op=ALU.mult))
                            c0 = (bi - b0) * 128
                            for n_, kt in enumerate(kts):
                                off = (kt - j + 1) * 128
                                mm(pn, pn[:, c0:c0 + 128], v_, v_[:, ph * nj + kt, :], p_t, p_t[:, off:off + 128],
                                   start=(n_ == 0), stop=(n_ == len(kts) - 1))
                            for n_, kt in enumerate(kts):
                                off = (kt - j + 1) * 128
                                mm(pd, pd[:, c0:c0 + 128], ones_b, ones_b[:], p_t, p_t[:, off:off + 128],
                                   start=(n_ == 0), stop=(n_ == len(kts) - 1))
                        ph0, j0 = b0 // nj, b0 % nj
                        outs = []
                        for acc in (accn, accd):
                            av = acc[:].rearrange("p (m r) -> p r m", r=r)
                            if r <= 4:
                                outs.append((av[:, ph0, j0 * 128:j0 * 128 + 512], None))
                            else:
                                outs.append((av[:, ph0:ph0 + 2, :], 2))
                        for (dst, f), pb_, acc, q in ((outs[0], pn, accn, act), (outs[1], pd, accd, dve)):
                            src = pb_[:] if f is None else pb_[:].rearrange("p (f m) -> p f m", f=f)
                            if g == 0:
                                if q is act:
                                    ex(act, [pb_], [acc], lambda dst=dst, src=src: nc.scalar.copy(out=dst, in_=src))
                                else:
                                    ex(dve, [pb_], [acc], lambda dst=dst, src=src: nc.vector.tensor_copy(dst, src))
                            else:
                                ex(dve, [pb_, acc], [acc], lambda dst=dst, src=src: nc.vector.tensor_tensor(
                                    out=dst, in0=dst, in1=src, op=ALU.add))
                ex(dve, [accd], [accd], lambda: nc.vector.reciprocal(out=accd[:], in_=accd[:]))
                ex(dve, [accn, accd], [ob], lambda: nc.vector.tensor_tensor(
                    out=ob[:], in0=accn[:], in1=accd[:], op=ALU.mult))
                kb.store(sp, ob.res, oT_d[hh, :, :], ob[:], [dram["oT"]])
        kb.barrier()

        with contextlib.ExitStack() as st:
            qT = [sbuf(st, "C_q%d" % i, [128, S], BF16) for i in range(2)]
            kT = [sbuf(st, "C_k%d" % i, [128, S], BF16) for i in range(2)]
            vn = [sbuf(st, "C_v%d" % i, [128, 32, 128], BF16) for i in range(2)]
            ob = sbuf(st, "C_ob", [128, S], BF16)
            pT = [sbuf(st, "C_pT%d" % i, [128, 512], BF16) for i in range(4)]
            sacc = [[sbuf(st, "C_sa%d%d" % (i, c), [128, 512], F32) for c in range(2)] for i in range(2)]
            rr = [sbuf(st, "C_rr%d" % i, [128, 512], F32) for i in range(2)]
            aa = [sbuf(st, "C_aa%d" % i, [128, 512], F32) for i in range(2)]
            A_ = sbuf(st, "C_A", [128, 512], F32)
            sq = sbuf(st, "C_sq", [128, 512], F32)
            rs = sbuf(st, "C_rs", [128, 512], F32)
            ps_s = [psum(st, "C_ps%d" % i) for i in range(3)]
            ps_n = [[psum(st, "C_pn%d%d" % (i, c)) for c in range(2)] for i in range(1)]
            ps_d = psum(st, "C_pd")
            ps_q = psum(st, "C_pq")
            gi = 0
            nqb = 0
            for h in range(4):
                q_, k_, v_ = qT[h % 2], kT[h % 2], vn[h % 2]
                kb.load(sp, q_.res, q_[:], QT_d[12 + h, :, :], [dram["QT"]])
                kb.load(sp, k_.res, k_[:], KT_d[12 + h, :, :], [dram["KT"]])
                kb.load(sp, v_.res, v_[:],
                        V_d[:, 1536 + h * 128:1536 + (h + 1) * 128].rearrange("(t p) c -> p t c", p=128),
                        [dram["V"]])
                for qb in range(8):
                    pn = ps_n[0]
                    sa = sacc[nqb % 2]
                    nqb += 1
                    steps = [(kt, c) for kt in range(32) for c in range(2)]

                    def qk(i):
                        kt, c = steps[i]
                        pss = ps_s[(gi + i) % 3]
                        mm(pss, pss[:], k_, k_[64 * c:64 * c + 64, kt * 128:(kt + 1) * 128],
                           q_, q_[64 * c:64 * c + 64, qb * 512:(qb + 1) * 512])

                    qk(0)
                    for i, (kt, c) in enumerate(steps):
                        if i + 1 < len(steps):
                            qk(i + 1)
                        pss = ps_s[(gi + i) % 3]
                        p_t = pT[(gi + i) % 4]
                        ex(act, [pss], [p_t], lambda: nc.scalar.activation(
                            out=p_t[:], in_=pss[:], func=AF.Exp, scale=0.125))
                        mm(pn[c], pn[c][:], v_, v_[:, kt, :], p_t, p_t[:], start=(kt == 0), stop=(kt == 31))
                        if True:
                            if kt == 0:
                                ex(dve, [p_t], [sa[c]], lambda: nc.vector.tensor_copy(sa[c][:], p_t[:]))
                            else:
                                ex(dve, [p_t, sa[c]], [sa[c]], lambda: nc.vector.tensor_tensor(
                                    out=sa[c][:], in0=sa[c][:], in1=p_t[:], op=ALU.add))
                    gi += len(steps)
                    for c in range(2):
                        mm(ps_q, ps_q[:], ones_f, ones_f[:], sa[c], sa[c][:])
                        pden = ps_q
                        ex(dve, [pden], [rr[c]], lambda: nc.vector.reciprocal(out=rr[c][:], in_=pden[:]))
                        ex(dve, [pn[c], rr[c]], [aa[c]], lambda: nc.vector.tensor_tensor(
                            out=aa[c][:], in0=pn[c][:], in1=rr[c][:], op=ALU.mult))
                    ex(dve, [aa[0], aa[1], neglam], [A_], lambda: nc.vector.scalar_tensor_tensor(
                        out=A_[:], in0=aa[1][:], scalar=neglam[:, 0:1], in1=aa[0][:], op0=ALU.mult, op1=ALU.add))
                    ex(dve, [A_], [sq], lambda: nc.vector.tensor_tensor(out=sq[:], in0=A_[:], in1=A_[:], op=ALU.mult))
                    mm(ps_q, ps_q[:], ones_f, ones_f[:], sq, sq[:])
                    ex(dve, [ps_q], [rs], lambda: nc.vector.tensor_scalar(
                        out=rs[:], in0=ps_q[:], scalar1=1.0 / 128.0, scalar2=1e-5, op0=ALU.mult, op1=ALU.add))
                    ex(act, [rs], [rs], lambda: nc.scalar.activation(out=rs[:], in_=rs[:], func=AF.Ln))
                    ex(act, [rs], [rs], lambda: nc.scalar.activation(out=rs[:], in_=rs[:], func=AF.Exp, scale=-0.5))
                    ex(dve, [A_, rs], [A_], lambda: nc.vector.tensor_tensor(out=A_[:], in0=A_[:], in1=rs[:], op=ALU.mult))
                    ex(dve, [A_, nw], [ob], lambda: nc.vector.tensor_scalar(
                        out=ob[:, qb * 512:(qb + 1) * 512], in0=A_[:], scalar1=nw[:, 0:1], scalar2=None, op0=ALU.mult))
                kb.store(sp, ob.res, oT_d[4 + h, :, :], ob[:], [dram["oT"]])
        kb.barrier()

        with contextlib.ExitStack() as st:
            lnp = sbuf(st, "D_lnp", [128, 2 * D], F32)
            kb.load(sp, lnp.res, lnp[:], ln_d[:, 0:2 * D])
            wr = sbuf(st, "D_wr", [128, 16, NE], F32)
            kb.load(sp, wr.res, wr[:], wr_d.rearrange("(k p) e -> p k e", p=128))
            wbr = sbuf(st, "D_wbr", [128, 8, D], BF16)
            wo = sbuf(st, "D_wo", [128, 16, D], BF16)
            for g in range(2):
                kb.load(pool, wbr.res, wbr[:, 4 * g:4 * g + 4, :],
                        wbr_d[g * 512:(g + 1) * 512, :].rearrange("(k p) c -> p k c", p=128))
            for i in range(4):
                kb.load(pool, wo.res, wo[:, 4 * i:4 * i + 4, :],
                        wout_d[i * 512:(i + 1) * 512, :].rearrange("(k p) c -> p k c", p=128))
            oTb = [sbuf(st, "D_oTb%d" % i, [128, 8, 512], BF16) for i in range(2)]
            Gt = [sbuf(st, "D_G%d" % i, [128, 2 * D], BF16) for i in range(2)]
            xts = [sbuf(st, "D_xt%d" % i, [128, 1, D], F32) for i in range(3)]
            tA = [sbuf(st, "D_tA%d" % i, [128, 512], F32) for i in range(1)]
            tB = [sbuf(st, "D_tB%d" % i, [128, 512], F32) for i in range(1)]
            mg = sbuf(st, "D_mg", [128, D], BF16)
            mT = sbuf(st, "D_mT", [128, 16, 128], BF16)
            x1b = [sbuf(st, "D_x1b%d" % i, [128, D], BF16) for i in range(1)]
            x1T = sbuf(st, "D_x1T", [128, 16, 128], F32)
            junk = Buf(x1T.t, "junkview")
            junk.res = x1T.res
            junk_ap = x1T[:].rearrange("p a b -> p (a b)")
            st1 = sbuf(st, "D_st1", [128, 8], F32)
            eb = sbuf(st, "D_eb", [128, NE], F32)
            pf = [psum(st, "D_p%d" % i) for i in range(6)]
            pbf = [psum(st, "D_pb%d" % i, BF16) for i in range(2)]
            npf = 0

            def prefetch(tt):
                if tt >= NT:
                    return
                if tt % 4 == 0:
                    ob_ = oTb[(tt // 4) % 2]
                    c0_ = tt * 128
                    kb.load(sp, ob_.res, ob_[:], oT_d[:, :, c0_:c0_ + 512].rearrange("k p t -> p k t"), [dram["oT"]])
                kb.load(sp, Gt[tt % 2].res, Gt[tt % 2][:], G_d[tt * 128:(tt + 1) * 128, :], [dram["G"]])
                kb.load(sp, xts[tt % 3].res, xts[tt % 3][:, 0, :], x_d[tt * 128:(tt + 1) * 128, :])

            def stage1(tt):
                nonlocal npf
                ti = tt % 4
                r0 = tt * 128
                ob_ = oTb[(tt // 4) % 2]
                G_ = Gt[tt % 2]
                xt_ = xts[tt % 3]
                hv = xt_[:, 0, :]
                for db in range(4):
                    pa = pf[npf % 6]
                    pb_ = pf[(npf + 1) % 6]
                    npf += 2
                    for g, pp in ((0, pa), (1, pb_)):
                        for c in range(4):
                            mm(pp, pp[:], ob_, ob_[:, 4 * g + c, ti * 128:(ti + 1) * 128],
                               wbr, wbr[:, 4 * g + c, db * 512:(db + 1) * 512], start=(c == 0), stop=(c == 3))
                    ta, tb_ = tA[0], tB[0]
                    ex(dve, [G_, pa], [ta], lambda: nc.vector.tensor_tensor(
                        out=ta[:], in0=G_[:, db * 512:(db + 1) * 512], in1=pa[:], op=ALU.mult))
                    ex(dve, [G_, pb_], [tb_], lambda: nc.vector.tensor_tensor(
                        out=tb_[:], in0=G_[:, D + db * 512:D + (db + 1) * 512], in1=pb_[:], op=ALU.mult))
                    ex(dve, [ta, tb_], [mg], lambda: nc.vector.tensor_tensor(
                        out=mg[:, db * 512:(db + 1) * 512], in0=ta[:], in1=tb_[:], op=ALU.add))
                for half in range(2):
                    pt = pbf[half]
                    for j in range(8):
                        k = half * 8 + j
                        tr(pt, pt[:, j * 128:(j + 1) * 128], mg, mg[:, k * 128:(k + 1) * 128], ident_b)
                    dst = mT[:, half * 8:(half + 1) * 8, :]
                    src = pt[:].rearrange("p (a b) -> p a b", a=8)
                    if half == 0:
                        ex(act, [pt], [mT], lambda: nc.scalar.copy(out=dst, in_=src))
                    else:
                        ex(dve, [pt], [mT], lambda: nc.vector.tensor_copy(dst, src))
                for obk in range(4):
                    pm = pf[npf % 6]
                    npf += 1
                    for k in range(16):
                        mm(pm, pm[:], mT, mT[:, k, :], wo, wo[:, k, obk * 512:(obk + 1) * 512],
                           start=(k == 0), stop=(k == 15))
                    ex(dve, [xt_, pm], [xt_], lambda: nc.vector.scalar_tensor_tensor(
                        out=xt_[:, 0, obk * 512:(obk + 1) * 512], in0=xt_[:, 0, obk * 512:(obk + 1) * 512],
                        scalar=ALPHA, in1=pm[:], op0=ALU.mult, op1=ALU.add))

            def stage2(tt):
                nonlocal npf
                r0 = tt * 128
                xt_ = xts[tt % 3]
                hv = xt_[:, 0, :]
                layer_norm(nc, kb, ex, hv, xt_, junk, junk_ap, st1, lnp, 0)
                kb.store(sp, xt_.res, x1_d[r0:r0 + 128, :], hv, [dram["x1"]])
                xb_ = x1b[0]
                ex(act, [xt_], [xb_], lambda: nc.scalar.copy(out=xb_[:], in_=hv))
                kb.store(sp, xb_.res, x1b_d[r0:r0 + 128, :], xb_[:], [dram["x1b"]])

            def stage2b(tt):
                nonlocal npf
                xt_ = xts[tt % 3]
                for kk in range(4):
                    pb = pf[npf % 6]
                    npf += 1
                    for j in range(4):
                        k = kk * 4 + j
                        tr(pb, pb[:, j * 128:(j + 1) * 128], xt_, xt_[:, 0, k * 128:(k + 1) * 128], ident_f)
                    dst = x1T[:, kk * 4:(kk + 1) * 4, :]
                    src = pb[:].rearrange("p (a b) -> p a b", a=4)
                    if kk % 2 == 0:
                        ex(act, [pb], [x1T], lambda: nc.scalar.copy(out=dst, in_=src))
                    else:
                        ex(dve, [pb], [x1T], lambda: nc.vector.tensor_copy(dst, src))
                pl = pf[npf % 6]
                npf += 1
                for k in range(16):
                    mm(pl, pl[:, 0:NE], x1T, x1T[:, k, :], wr, wr[:, k, :], start=(k == 0), stop=(k == 15))
                ex(dve, [pl], [st1], lambda: nc.vector.reduce_max(out=st1[:, 4:5], in_=pl[:, 0:NE], axis=AX.X))
                ex(dve, [st1], [st1], lambda: nc.vector.tensor_scalar_mul(out=st1[:, 5:6], in0=st1[:, 4:5], scalar1=-1.0))
                ex(act, [pl, st1], [eb, st1], lambda: nc.scalar.activation(
                    out=eb[:], in_=pl[:, 0:NE], func=AF.Exp, bias=st1[:, 5:6], scale=1.0, accum_out=st1[:, 6:7]))
                ex(dve, [st1], [st1], lambda: nc.vector.reciprocal(out=st1[:, 7:8], in_=st1[:, 6:7]))
                ex(dve, [eb, st1], [aff], lambda: nc.vector.tensor_scalar(
                    out=aff[:, tt * NE:(tt + 1) * NE], in0=eb[:], scalar1=st1[:, 7:8], scalar2=None, op0=ALU.mult))

            prefetch(0)
            prefetch(1)
            stage1(0)
            for tt in range(NT):
                prefetch(tt + 2)
                stage2(tt)
                if tt + 1 < NT:
                    stage1(tt + 1)
                stage2b(tt)
        kb.barrier()

        with contextlib.ExitStack() as st:
            lo = sbuf(st, "E_lo", [128, NE], F32)
            hi = sbuf(st, "E_hi", [128, NE], F32)
            mid = sbuf(st, "E_mid", [128, NE], F32)
            dd = sbuf(st, "E_dd", [128, NE], F32)
            sel = sbuf(st, "E_sel", [128, NE], F32)
            cmp_ = sbuf(st, "E_cmp", [128, NT * NE], F32)
            cnt = sbuf(st, "E_cnt", [128, NE], F32)
            mkb = sbuf(st, "E_mkb", [128, NT * NE], BF16)
            tri = sbuf(st, "E_tri", [128, 128], BF16)
            eoff = sbuf(st, "E_eoff", [128, NT * NE], F32)
            tt_ = sbuf(st, "E_tt", [128, NT * NE], F32)
            cum = sbuf(st, "E_cum", [128, NT * NE], F32)
            pos = sbuf(st, "E_pos", [128, NT * NE], F32)
            val = sbuf(st, "E_val", [128, NT * NE], F32)
            pc = psum(st, "E_pc")
            pw = psum(st, "E_pw")
            ptt = psum(st, "E_pt")
            kb.load(sp, tri.res, tri[:], tri_d[:, :])
            kb.load(sp, eoff.res, eoff[:], eoff_d[:, :])
            ex(dve, [], [lo], lambda: nc.vector.memset(lo[:], 0.0))
            ex(dve, [], [hi], lambda: nc.vector.memset(hi[:], 1.0))
            aff3 = aff[:].rearrange("p (t e) -> p t e", e=NE)

            def bc(b):
                a = b[:]
                return bass.AP(a.tensor, a.offset, [list(a.ap[0]), [0, NT], list(a.ap[1])])

            for _ in range(34):
                ex(dve, [lo, hi], [mid], lambda: nc.vector.tensor_tensor(out=mid[:], in0=lo[:], in1=hi[:], op=ALU.add))
                ex(dve, [mid], [mid], lambda: nc.vector.tensor_scalar_mul(out=mid[:], in0=mid[:], scalar1=0.5))
                ex(dve, [aff, mid], [cmp_], lambda: nc.vector.tensor_tensor(
                    out=cmp_[:].rearrange("p (t e) -> p t e", e=NE), in0=aff3, in1=bc(mid), op=ALU.is_gt))
                ex(dve, [cmp_], [cnt], lambda: nc.vector.reduce_sum(
                    out=cnt[:], in_=cmp_[:].rearrange("p (t e) -> p e t", e=NE), axis=AX.X))
                mm(pc, pc[:, 0:NE], ones_f, ones_f[:], cnt, cnt[:])
                ex(dve, [pc], [sel], lambda: nc.vector.tensor_scalar(
                    out=sel[:], in0=pc[:, 0:NE], scalar1=float(CAP) - 0.5, scalar2=None, op0=ALU.is_gt))
                ex(dve, [mid, lo], [dd], lambda: nc.vector.tensor_tensor(out=dd[:], in0=mid[:], in1=lo[:], op=ALU.subtract))
                ex(dve, [dd, sel], [dd], lambda: nc.vector.tensor_tensor(out=dd[:], in0=dd[:], in1=sel[:], op=ALU.mult))
                ex(dve, [dd, lo], [lo], lambda: nc.vector.tensor_tensor(out=lo[:], in0=lo[:], in1=dd[:], op=ALU.add))
                ex(dve, [mid, hi], [dd], lambda: nc.vector.tensor_tensor(out=dd[:], in0=hi[:], in1=mid[:], op=ALU.subtract))
                ex(dve, [dd, sel], [dd], lambda: nc.vector.tensor_tensor(out=dd[:], in0=dd[:], in1=sel[:], op=ALU.mult))
                ex(dve, [dd, mid], [hi], lambda: nc.vector.tensor_tensor(out=hi[:], in0=mid[:], in1=dd[:], op=ALU.add))
            ex(dve, [aff, lo], [cmp_], lambda: nc.vector.tensor_tensor(
                out=cmp_[:].rearrange("p (t e) -> p t e", e=NE), in0=aff3, in1=bc(lo), op=ALU.is_gt))
            ex(act, [cmp_], [mkb], lambda: nc.scalar.copy(out=mkb[:], in_=cmp_[:]))
            mm(pw, pw[:], tri, tri[:], mkb, mkb[:])
            mm(ptt, ptt[:], ones_b, ones_b[:], mkb, mkb[:])
            ex(act, [ptt], [tt_], lambda: nc.scalar.copy(out=tt_[:], in_=ptt[:]))
            ex(dve, [], [cum], lambda: nc.vector.memset(cum[:, 0:NE], 0.0))
            for t in range(1, NT):
                ex(dve, [cum, tt_], [cum], lambda t=t: nc.vector.tensor_tensor(
                    out=cum[:, t * NE:(t + 1) * NE], in0=cum[:, (t - 1) * NE:t * NE],
                    in1=tt_[:, (t - 1) * NE:t * NE], op=ALU.add))
            ex(dve, [pw, cum], [pos], lambda: nc.vector.tensor_tensor(out=pos[:], in0=cum[:], in1=pw[:], op=ALU.add))
            ex(dve, [pos], [val], lambda: nc.vector.tensor_scalar(
                out=val[:], in0=pos[:], scalar1=float(CAP) - 0.5, scalar2=None, op0=ALU.is_lt))
            ex(dve, [val, cmp_], [val], lambda: nc.vector.tensor_tensor(out=val[:], in0=val[:], in1=cmp_[:], op=ALU.mult))
            ex(dve, [aff, val], [gm], lambda: nc.vector.tensor_tensor(out=gm[:], in0=aff[:], in1=val[:], op=ALU.mult))
            ex(dve, [pos, eoff], [pos], lambda: nc.vector.tensor_tensor(out=pos[:], in0=pos[:], in1=eoff[:], op=ALU.add))
            ex(dve, [pos], [pos], lambda: nc.vector.tensor_scalar_add(out=pos[:], in0=pos[:], scalar1=-BIG))
            ex(dve, [pos, val], [pos], lambda: nc.vector.tensor_tensor(out=pos[:], in0=pos[:], in1=val[:], op=ALU.mult))
            ex(dve, [pos], [pos], lambda: nc.vector.tensor_scalar_add(out=pos[:], in0=pos[:], scalar1=BIG))
            ex(dve, [pos], [idx_i], lambda: nc.vector.tensor_copy(idx_i[:], pos[:]))
            if DEBUG:
                kb.store(sp, aff.res, aff_d[:, :], aff[:])
        kb.barrier()

        with contextlib.ExitStack() as st:
            xb = [sbuf(st, "F_xb%d" % i, [128, D], BF16) for i in range(3)]
            for tt in range(NT):
                b = xb[tt % 3]
                kb.load(sp, b.res, b[:], x1b_d[tt * 128:(tt + 1) * 128, :], [dram["x1b"]])
                for e in range(NE):
                    col = tt * NE + e
                    if b.res.dout is None:
                        b.res.dout = kb.dsem()
                    kb.dma(pool, None, None, [b.res, idx_i.res], [], b.res.dout,
                           fn=lambda b=b, col=col: nc.gpsimd.indirect_dma_start(
                               out=xin_d[:, :], out_offset=bass.IndirectOffsetOnAxis(ap=idx_i[:, col:col + 1], axis=0),
                               in_=b[:], in_offset=None, bounds_check=bc_reg, oob_is_err=False))
        kb.barrier()

        with contextlib.ExitStack() as st:
            xi = [sbuf(st, "G_xi%d" % i, [128, 4, D], BF16) for i in range(2)]
            xiT = [sbuf(st, "G_xiT%d" % i, [128, 16, 512], BF16) for i in range(2)]
            hT = sbuf(st, "G_hT", [128, 16, 512], BF16)
            wpc = [sbuf(st, "G_wp%d" % i, [128, 16, 512], BF16) for i in range(5)]
            sgl = [sbuf(st, "G_sg%d" % i, [128, 512], F32) for i in range(2)]
            ost = [sbuf(st, "G_os%d" % i, [128, 512], BF16) for i in range(3)]
            ptr = [psum(st, "G_pt%d" % i, BF16) for i in range(2)]
            pgu = [psum(st, "G_pg%d" % i) for i in range(4)]
            pdn = [psum(st, "G_pd%d" % i) for i in range(2)]
            nos = 0
            npd = 0

            def piece(src_d, e, cblk):
                return src_d[e * D:(e + 1) * D, cblk * 512:(cblk + 1) * 512].rearrange("(k p) c -> p k c", p=128)

            wsrcs = []
            for e in range(NE):
                for fb in range(4):
                    wsrcs.append(piece(wg_d, e, fb))
                    wsrcs.append(piece(wu_d, e, fb))
                for db in range(4):
                    wsrcs.append(piece(wd_d, e, db))
            wstream = Stream(kb, pool, wpc, wsrcs, hold=2)

            def load_x(e):
                x_ = xi[e % 2]
                kb.load(sp, x_.res, x_[:], xin_d[e * CAP:(e + 1) * CAP, :].rearrange("(s p) d -> p s d", p=128))

            def transposes(e):
                x_ = xi[e % 2]
                xt_ = xiT[e % 2]
                for kp in range(8):
                    pt = ptr[kp % 2]
                    for k2 in range(2):
                        k = kp * 2 + k2
                        for s_ in range(4):
                            tr(pt, pt[:, (k2 * 4 + s_) * 128:(k2 * 4 + s_ + 1) * 128], x_,
                               x_[:, s_, k * 128:(k + 1) * 128], ident_b)
                    dst = xt_[:, kp * 2:kp * 2 + 2, :]
                    src = pt[:].rearrange("p (a b) -> p a b", a=2)
                    if kp % 2 == 0:
                        ex(act, [pt], [xt_], lambda: nc.scalar.copy(out=dst, in_=src))
                    else:
                        ex(dve, [pt], [xt_], lambda: nc.vector.tensor_copy(dst, src))

            load_x(0)
            wstream.get(0)
            transposes(0)
            for e in range(NE):
                if e + 1 < NE:
                    load_x(e + 1)
                xt_ = xiT[e % 2]
                for fb in range(4):
                    wg = wstream.get(e * 12 + fb * 2)
                    wu = wstream.get(e * 12 + fb * 2 + 1)
                    for fi in range(4):
                        fc = fb * 4 + fi
                        pg = pgu[2 * (fc % 2)]
                        pu = pgu[2 * (fc % 2) + 1]
                        for k in range(16):
                            mm(pg, pg[:], wg, wg[:, k, fi * 128:(fi + 1) * 128], xt_, xt_[:, k, :],
                               start=(k == 0), stop=(k == 15))
                        for k in range(16):
                            mm(pu, pu[:], wu, wu[:, k, fi * 128:(fi + 1) * 128], xt_, xt_[:, k, :],
                               start=(k == 0), stop=(k == 15))
                        s_g = sgl[fc % 2]
                        ex(act, [pg], [s_g], lambda: nc.scalar.activation(out=s_g[:], in_=pg[:], func=AF.Silu))
                        ex(dve, [s_g, pu], [hT], lambda: nc.vector.tensor_tensor(
                            out=hT[:, fc, :], in0=s_g[:], in1=pu[:], op=ALU.mult))
                if e + 1 < NE:
                    transposes(e + 1)
                for db in range(4):
                    wd = wstream.get(e * 12 + 8 + db)
                    for s_ in range(4):
                        pd = pdn[npd % 2]
                        npd += 1
                        for fc in range(16):
                            mm(pd, pd[:], hT, hT[:, fc, s_ * 128:(s_ + 1) * 128], wd, wd[:, fc, :],
                               start=(fc == 0), stop=(fc == 15))
                        o_ = ost[nos % 3]
                        nos += 1
                        if s_ % 2 == 0:
                            ex(act, [pd], [o_], lambda: nc.scalar.copy(out=o_[:], in_=pd[:]))
                        else:
                            ex(dve, [pd], [o_], lambda: nc.vector.tensor_copy(o_[:], pd[:]))
                        r0 = e * CAP + s_ * 128
                        kb.store(sp, o_.res, yo_d[r0:r0 + 128, db * 512:(db + 1) * 512], o_[:], [dram["yo"]])
        kb.barrier()

        with contextlib.ExitStack() as st:
            lnp = sbuf(st, "H_lnp", [128, 2 * D], F32)
            kb.load(sp, lnp.res, lnp[:], ln_d[:, 2 * D:4 * D])
            NG = 32
            xa = [sbuf(st, "H_xa%d" % i, [128, 1, D], F32) for i in range(3)]
            gb = [sbuf(st, "H_g%d" % i, [128, D], BF16) for i in range(NG)]
            dg = [sbuf(st, "H_dg%d" % i, [128, 128], BF16) for i in range(4)]
            junk = sbuf(st, "H_junk", [128, D], F32)
            st1 = sbuf(st, "H_st1", [128, 8], F32)
            pacc = [[psum(st, "H_p%d%d" % (i, c)) for c in range(4)] for i in range(2)]
            gset = [Res("gset0"), Res("gset1")]
            for i_, g_ in enumerate(gb):
                g_.res = gset[i_ // NE]
                ex(dve, [], [g_], lambda: nc.vector.memset(g_[:], 0.0))

            def gathers(tt):
                rs = gset[tt % 2]
                if rs.din is None:
                    rs.din = kb.dsem()
                lst = []
                fns = []
                for e in range(NE):
                    col = tt * NE + e
                    g_ = gb[(tt % 2) * NE + e]
                    g_.res = rs
                    fns.append(lambda g_=g_, col=col: nc.gpsimd.indirect_dma_start(
                        out=g_[:], out_offset=None, in_=yo_d[:, :],
                        in_offset=bass.IndirectOffsetOnAxis(ap=idx_i[:, col:col + 1], axis=0),
                        bounds_check=bc_reg, oob_is_err=False))
                    lst.append(g_)
                kb.dma_group(pool, fns, [idx_i.res, dram["yo"]], [rs], rs.din)
                return lst

            nd = 0
            glists = {0: gathers(0)}

            def stageA(tt):
                nonlocal nd
                a_ = xa[tt % 3]
                r0 = tt * 128
                kb.load(sp, a_.res, a_[:, 0, :], x1_d[r0:r0 + 128, :], [dram["x1"]])
                glist = glists.pop(tt)
                if tt + 1 < NT:
                    glists[tt + 1] = gathers(tt + 1)
                pa = pacc[tt % 2]
                for e in range(NE):
                    col = tt * NE + e
                    d_ = dg[nd % 4]
                    nd += 1
                    ex(dve, [ident_b, gm], [d_], lambda: nc.vector.tensor_scalar(
                        out=d_[:], in0=ident_b[:], scalar1=gm[:, col:col + 1], scalar2=None, op0=ALU.mult))
                    g_ = glist[e]
                    for cb in range(4):
                        mm(pa[cb], pa[cb][:], d_, d_[:], g_, g_[:, cb * 512:(cb + 1) * 512],
                           start=(e == 0), stop=(e == NE - 1))

            def stageA2(tt):
                a_ = xa[tt % 3]
                pa = pacc[tt % 2]
                for cb in range(4):
                    ex(dve, [a_, pa[cb]], [a_], lambda: nc.vector.scalar_tensor_tensor(
                        out=a_[:, 0, cb * 512:(cb + 1) * 512], in0=a_[:, 0, cb * 512:(cb + 1) * 512],
                        scalar=ALPHA, in1=pa[cb][:], op0=ALU.mult, op1=ALU.add))

            def stageB(tt):
                a_ = xa[tt % 3]
                r0 = tt * 128
                layer_norm(nc, kb, ex, a_[:, 0, :], a_, junk, junk[:], st1, lnp, 0)
                kb.store(sp, a_.res, out_d[r0:r0 + 128, :], a_[:, 0, :])

            stageA(0)
            stageA2(0)
            for tt in range(NT):
                if tt + 1 < NT:
                    stageA(tt + 1)
                stageB(tt)
                if tt + 1 < NT:
                    stageA2(tt + 1)
        kb.barrier()
        nc._marks = kb.marks
    return nc


def layer_norm(nc, kb, ex, hv, hb, junk, junk_ap, st1, lnp, off):
    dve, act, pool = kb.dve, kb.act, kb.pool
    ex(dve, [hb], [st1], lambda: nc.vector.reduce_sum(out=st1[:, 0:1], in_=hv, axis=AX.X))
    ex(dve, [st1], [st1], lambda: nc.vector.tensor_scalar_mul(out=st1[:, 1:2], in0=st1[:, 0:1], scalar1=-1.0 / D))
    ex(dve, [hb, st1], [hb], lambda: nc.vector.tensor_scalar(
        out=hv, in0=hv, scalar1=st1[:, 1:2], scalar2=None, op0=ALU.add))
    ex(act, [hb], [junk, st1], lambda: nc.scalar.activation(
        out=junk_ap, in_=hv, func=AF.Square, accum_out=st1[:, 2:3]))
    ex(dve, [st1], [st1], lambda: nc.vector.tensor_scalar(
        out=st1[:, 3:4], in0=st1[:, 2:3], scalar1=1.0 / D, scalar2=LN_EPS, op0=ALU.mult, op1=ALU.add))
    ex(act, [st1], [st1], lambda: nc.scalar.activation(out=st1[:, 3:4], in_=st1[:, 3:4], func=AF.Ln))
    ex(act, [st1], [st1], lambda: nc.scalar.activation(out=st1[:, 3:4], in_=st1[:, 3:4], func=AF.Exp, scale=-0.5))
    ex(dve, [hb, st1, lnp], [hb], lambda: nc.vector.scalar_tensor_tensor(
        out=hv, in0=hv, scalar=st1[:, 3:4], in1=lnp[:, off:off + D], op0=ALU.mult, op1=ALU.mult))
    ex(dve, [hb, lnp], [hb], lambda: nc.vector.tensor_tensor(
        out=hv, in0=hv, in1=lnp[:, off + D:off + 2 * D], op=ALU.add))


DEBUG_OUT = ()


def _consts():
    import ml_dtypes
    bf = ml_dtypes.bfloat16
    pos = np.arange(S, dtype=np.float32)
    c = {}
    for nm, half, rot, nblk in (("A", 16, 32, 4), ("B", 8, 16, 8)):
        inv = (np.float32(500000.0) ** (-2.0 * np.arange(half, dtype=np.float32) / rot)).astype(np.float32)
        ang = (pos[:, None] * inv[None, :]).astype(np.float32)
        cs, sn = np.cos(ang).astype(np.float32), np.sin(ang).astype(np.float32)
        cc = np.concatenate([cs, cs], axis=1)
        ss = np.concatenate([sn, -sn], axis=1)
        c["cc" + nm] = np.ascontiguousarray(np.tile(cc, (1, nblk)))
        c["ss" + nm] = np.ascontiguousarray(np.tile(ss, (1, nblk)))
    a = np.arange(128)[:, None]
    b = np.arange(128)[None, :]
    m = np.concatenate([(a - b >= 64), (np.abs(a - b) <= 64), (b - a >= 64)], axis=1)
    c["bandmask"] = m.astype(np.float32).astype(bf)
    c["ident_f"] = np.eye(128, dtype=np.float32)
    c["ident_b"] = np.eye(128, dtype=np.float32).astype(bf)
    c["tri_b"] = (a < b).astype(np.float32).astype(bf)
    eo = np.tile((np.arange(NE, dtype=np.float32) * CAP)[None, :], (128, NT))
    c["eoff"] = np.ascontiguousarray(eo.astype(np.float32))
    return c


_CACHE = {}


def kernel(x, w_in, lambda_q1, lambda_k1, lambda_q2, lambda_k2, diff_norm_w, w_branch, w_out,
           ln1_g, ln1_b, w_router, w_gate, w_up, w_down, ln2_g, ln2_b):
    f = lambda a: np.ascontiguousarray(np.asarray(a, dtype=np.float32))
    x = f(x)
    if "nc" not in _CACHE:
        _CACHE["nc"] = build_program()
        _CACHE["c"] = _consts()
    nc = _CACHE["nc"]
    c = _CACHE["c"]
    lam4 = np.concatenate([f(lambda_q1)[0], f(lambda_k1)[0], f(lambda_q2)[0], f(lambda_k2)[0]])[None, :]
    lnp = np.concatenate([f(ln1_g)[0], f(ln1_b)[0], f(ln2_g)[0], f(ln2_b)[0]])[None, :]
    shared = {
        "w_in": f(w_in)[0], "w_branch": f(w_branch)[0].reshape(1024, D), "w_out": f(w_out)[0],
        "w_router": f(w_router)[0], "w_gate": f(w_gate)[0].reshape(NE * D, D),
        "w_up": f(w_up)[0].reshape(NE * D, D), "w_down": f(w_down)[0].reshape(NE * D, D),
        "lam4": np.ascontiguousarray(np.broadcast_to(lam4, (128, 256))),
        "nw": np.ascontiguousarray(f(diff_norm_w)[0].reshape(128, 1)),
        "lnp": np.ascontiguousarray(np.broadcast_to(lnp, (128, 4 * D))),
    }
    shared.update(c)
    nb = x.shape[0]
    in_maps = []
    for core in range(8):
        m = dict(shared)
        m["x"] = np.ascontiguousarray(x[(core // 2) % nb])
        in_maps.append(m)
    res = run_bass_kernel_spmd(nc, in_maps, core_ids=list(range(8)))
    out = np.stack([np.asarray(res.results[2 * b]["out"], dtype=np.float32) for b in range(nb)], axis=0)
    return out
```

```python
import contextlib
import numpy as np
import concourse.bass as bass
import concourse.mybir as mybir
from concourse.bass_utils import run_bass_kernel_spmd

F32 = mybir.dt.float32
BF16 = mybir.dt.bfloat16
I32 = mybir.dt.int32
ALU = mybir.AluOpType
AF = mybir.ActivationFunctionType
AX = mybir.AxisListType

S = 4096
D = 2048
NT = S // 128
NE = 16
CAP = 512
ALPHA = 2.0 ** 0.25
LN_EPS = 1e-5
BIG = float(1 << 20)
DEBUG = False
FUSE_WAIT = True


class Ev:
    __slots__ = ("sem", "val")

    def __init__(self, sem, val):
        self.sem = sem
        self.val = val


class DSem:
    def __init__(self, h):
        self.h = h
        self.n = 0


class Res:
    def __init__(self, name):
        self.name = name
        self.w = None
        self.r = {}
        self.din = None
        self.dout = None


class Q:
    def __init__(self, eng, sem, is_pe=False):
        self.eng = eng
        self.sem = sem
        self.n = 0
        self.known = {}
        self.is_pe = is_pe

    def wait(self, ev):
        if ev is None:
            return
        if ev.sem is self.sem and self.is_pe:
            return
        k = id(ev.sem)
        if self.known.get(k, 0) >= ev.val:
            return
        self.eng.wait_ge(ev.sem, ev.val)
        self.known[k] = ev.val


class KB:
    def __init__(self, nc, es):
        self.nc = nc
        self.es = es
        self.nsem = 0
        self.pe = Q(nc.tensor, self.newsem("q_pe"), True)
        self.act = Q(nc.scalar, self.newsem("q_act"))
        self.dve = Q(nc.vector, self.newsem("q_dve"))
        self.pool = Q(nc.gpsimd, self.newsem("q_pool"))
        self.sp = Q(nc.sync, self.newsem("q_sp"))
        self.queues = [self.pe, self.act, self.dve, self.pool, self.sp]
        self.dsems = []

    def newsem(self, name):
        self.nsem += 1
        return self.es.enter_context(self.nc.semaphore(name))

    def dsem(self):
        free = getattr(self, "free_dsems", None)
        if free:
            return free.pop()
        d = DSem(self.newsem("d%d" % len(self.dsems)))
        self.dsems.append(d)
        return d

    def dma_group(self, q, fns, reads, writes, ds):
        self._pre(q, reads, writes)
        for fn in fns:
            inst = fn()
            ds.n += 16
            inst.then_inc(ds.h, 16)
        self._post(Ev(ds.h, ds.n), reads, writes)

    def _need(self, q, reads, writes):
        need = {}

        def add(ev):
            if ev is None:
                return
            if ev.sem is q.sem and q.is_pe:
                return
            k = id(ev.sem)
            if q.known.get(k, 0) >= ev.val:
                return
            if k not in need or need[k].val < ev.val:
                need[k] = ev

        for r in reads:
            add(r.w)
        for w in writes:
            add(w.w)
            for ev in w.r.values():
                add(ev)
        return list(need.values())

    def _pre(self, q, reads, writes, keep_last=False):
        evs = self._need(q, reads, writes)
        last = None
        if keep_last and evs:
            last = evs.pop()
        for ev in evs:
            q.eng.wait_ge(ev.sem, ev.val)
            q.known[id(ev.sem)] = ev.val
        return last

    def _post(self, ev, reads, writes):
        for r in reads:
            r.r[id(ev.sem)] = ev
        for w in writes:
            w.w = ev
            w.r = {}

    def op(self, q, reads, writes, fn):
        last = self._pre(q, reads, writes, keep_last=FUSE_WAIT)
        inst = fn()
        if last is not None:
            inst._wait_ge(last.sem, last.val)
            q.known[id(last.sem)] = last.val
        q.n += 1
        inst.then_inc(q.sem, 1)
        self._post(Ev(q.sem, q.n), reads, writes)

    def dma(self, q, out, in_, reads, writes, ds, fn=None):
        self._pre(q, reads, writes)
        if fn is None:
            inst = q.eng.dma_start(out=out, in_=in_)
        else:
            inst = fn()
        ds.n += 16
        inst.then_inc(ds.h, 16)
        self._post(Ev(ds.h, ds.n), reads, writes)

    def load(self, q, res, out, in_, src=()):
        if res.din is None:
            res.din = self.dsem()
        self.dma(q, out, in_, list(src), [res], res.din)

    def store(self, q, res, out, in_, dst=()):
        if res.dout is None:
            res.dout = self.dsem()
        self.dma(q, out, in_, [res], list(dst), res.dout)

    def barrier(self):
        self.marks = getattr(self, "marks", [])
        self.marks.append({"pe": self.pe.n, "act": self.act.n, "dve": self.dve.n, "pool": self.pool.n})
        for q in self.queues:
            for p in self.queues:
                if p is not q and p.n > 0:
                    q.wait(Ev(p.sem, p.n))
            for d in self.dsems:
                if d.n > 0:
                    q.wait(Ev(d.h, d.n))
        self.free_dsems = list(self.dsems)


class Stream:
    def __init__(self, kb, q, bufs, srcs, hold=1):
        self.kb, self.q, self.bufs, self.srcs = kb, q, bufs, srcs
        self.n = 0
        self.hold = hold

    def get(self, i):
        last = min(i + len(self.bufs) - self.hold, len(self.srcs) - 1)
        while self.n <= last:
            b = self.bufs[self.n % len(self.bufs)]
            self.kb.load(self.q, b.res, b[:], self.srcs[self.n])
            self.n += 1
        return self.bufs[i % len(self.bufs)]


class Buf:
    def __init__(self, t, name):
        self.t = t
        self.res = Res(name)

    def __getitem__(self, k):
        return self.t[k]


def build_program():
    nc = bass.Bass("TRN2", target_bir_lowering=False)
    es = contextlib.ExitStack()

    def din(name, shape, dt=F32):
        return nc.dram_tensor(name, list(shape), dt, kind="ExternalInput").ap()

    def dscr(name, shape, dt):
        kind = "ExternalOutput" if (DEBUG and name in DEBUG_OUT) else "Internal"
        return nc.dram_tensor(name, list(shape), dt, kind=kind).ap()

    x_d = din("x", [S, D])
    win_d = din("w_in", [D, 10240])
    wbr_d = din("w_branch", [1024, D])
    wout_d = din("w_out", [D, D])
    wr_d = din("w_router", [D, NE])
    wg_d = din("w_gate", [NE * D, D])
    wu_d = din("w_up", [NE * D, D])
    wd_d = din("w_down", [NE * D, D])
    lam_d = din("lam4", [128, 256])
    nw_d = din("nw", [128, 1])
    ln_d = din("lnp", [128, 4 * D])
    ccA_d = din("ccA", [S, 128])
    ssA_d = din("ssA", [S, 128])
    ccB_d = din("ccB", [S, 128])
    ssB_d = din("ssB", [S, 128])
    msk_d = din("bandmask", [128, 384], BF16)
    idf_d = din("ident_f", [128, 128])
    idb_d = din("ident_b", [128, 128], BF16)
    tri_d = din("tri_b", [128, 128], BF16)
    eoff_d = din("eoff", [128, NT * NE])
    out_d = nc.dram_tensor("out", [S, D], F32, kind="ExternalOutput").ap()

    QT_d = dscr("QT_s", [16, 128, S], BF16)
    KT_d = dscr("KT_s", [16, 128, S], BF16)
    V_d = dscr("V_s", [S, D], BF16)
    oT_d = dscr("oT_s", [8, 128, S], BF16)
    x1_d = dscr("x1_s", [S, D], F32)
    x1b_d = dscr("x1b_s", [S, D], BF16)
    xin_d = dscr("xin_s", [NE * CAP, D], BF16)
    yo_d = dscr("yo_s", [NE * CAP, D], BF16)
    aff_d = dscr("aff_s", [128, NT * NE], F32)
    G_d = dscr("G_s", [S, 2 * D], BF16)

    with es:
        kb = KB(nc, es)
        pe, act, dve, pool, sp = kb.pe, kb.act, kb.dve, kb.pool, kb.sp

        def sbuf(st, name, shape, dt):
            return Buf(st.enter_context(nc.sbuf_tensor("sb_" + name, list(shape), dt)), name)

        def psum(st, name, dt=F32):
            shape = [128, 512] if dt == F32 else [128, 1024]
            return Buf(st.enter_context(nc.psum_tensor("ps_" + name, shape, dt)), name)

        def mm(out_b, out_ap, l_b, l_ap, r_b, r_ap, start=True, stop=True):
            kb.op(pe, [l_b.res, r_b.res], [out_b.res],
                  lambda: nc.tensor.matmul(out_ap, l_ap, r_ap, start=start, stop=stop))

        def tr(out_b, out_ap, in_b, in_ap, id_b):
            kb.op(pe, [in_b.res, id_b.res], [out_b.res],
                  lambda: nc.tensor.transpose(out_ap, in_ap, id_b[:]))

        def ex(q, reads, writes, fn):
            kb.op(q, [b.res for b in reads], [b.res for b in writes], fn)

        bc_reg = nc.gpsimd.alloc_register("bc_reg")
        nc.gpsimd.reg_mov(bc_reg, NE * CAP - 1)
        dram = {n: Res(n) for n in ["xT", "QT", "KT", "V", "oT", "x1", "x1b", "xin", "yo", "G"]}

        gst = es
        ident_f = sbuf(gst, "ident_f", [128, 128], F32)
        ident_b = sbuf(gst, "ident_b", [128, 128], BF16)
        ones_b = sbuf(gst, "ones_b", [128, 128], BF16)
        ones_f = sbuf(gst, "ones_f", [128, 128], F32)
        lam4 = sbuf(gst, "lam4", [128, 256], F32)
        nw = sbuf(gst, "nw", [128, 1], F32)
        neglam = sbuf(gst, "neglam", [128, 1], F32)
        ltmp = sbuf(gst, "ltmp", [128, 128], F32)
        lsum = sbuf(gst, "lsum", [128, 2], F32)
        aff = sbuf(gst, "aff", [128, NT * NE], F32)
        idx_i = sbuf(gst, "idx_i", [128, NT * NE], I32)
        gm = sbuf(gst, "gm", [128, NT * NE], F32)
        kb.load(sp, ident_f.res, ident_f[:], idf_d[:, :])
        kb.load(sp, ident_b.res, ident_b[:], idb_d[:, :])
        kb.load(sp, lam4.res, lam4[:], lam_d[:, :])
        kb.load(sp, nw.res, nw[:], nw_d[:, :])
        ex(pool, [], [ones_b], lambda: nc.gpsimd.memset(ones_b[:], 1.0))
        ex(pool, [], [ones_f], lambda: nc.gpsimd.memset(ones_f[:], 1.0))
        ex(dve, [lam4], [ltmp], lambda: nc.vector.tensor_tensor(
            out=ltmp[:].rearrange("p (a c) -> p a c", a=2),
            in0=lam4[:].rearrange("p (a b c) -> p a b c", a=2, b=2)[:, :, 0, :],
            in1=lam4[:].rearrange("p (a b c) -> p a b c", a=2, b=2)[:, :, 1, :], op=ALU.mult))
        ex(dve, [ltmp], [lsum], lambda: nc.vector.reduce_sum(
            out=lsum[:], in_=ltmp[:].rearrange("p (a c) -> p a c", a=2), axis=AX.X))
        ex(act, [lsum], [lsum], lambda: nc.scalar.activation(out=lsum[:], in_=lsum[:], func=AF.Exp))
        ex(dve, [lsum], [neglam], lambda: nc.vector.tensor_tensor(
            out=neglam[:], in0=lsum[:, 1:2], in1=lsum[:, 0:1], op=ALU.subtract))
        ex(dve, [neglam], [neglam], lambda: nc.vector.tensor_scalar_add(
            out=neglam[:], in0=neglam[:], scalar1=-0.2))
        ex(dve, [nw], [nw], lambda: nc.vector.tensor_scalar_mul(out=nw[:], in0=nw[:], scalar1=0.8))

        with contextlib.ExitStack() as st:
            xT = sbuf(st, "A_xT", [128, 16, 2048], BF16)
            tabs = [sbuf(st, "A_tab%d" % i, [128, 16, 128], F32) for i in range(4)]
            xt = [sbuf(st, "A_xt%d" % i, [128, D], F32) for i in range(2)]
            wp = [sbuf(st, "A_wp%d" % i, [128, 16, 512], BF16) for i in range(2)]
            zt = [sbuf(st, "A_zt%d" % i, [128, 512], F32) for i in range(2)]
            uu = [sbuf(st, "A_uu%d" % i, [128, 128], F32) for i in range(2)]
            vv = [sbuf(st, "A_vv%d" % i, [128, 128], F32) for i in range(2)]
            zb = [sbuf(st, "A_zb%d" % i, [128, 512], BF16) for i in range(4)]
            qst = [sbuf(st, "A_qst%d" % i, [128, 4, 2048], BF16) for i in range(2)]
            pmm = [psum(st, "A_pm%d" % i) for i in range(4)]
            ptr = [psum(st, "A_pt%d" % i, BF16) for i in range(2)]
            tab_src = [ccA_d, ssA_d, ccB_d, ssB_d]
            nxt = 0
            for hf in range(2):
                t0 = hf * 2048
                for i in range(4):
                    kb.load(sp, tabs[i].res, tabs[i][:],
                            tab_src[i][t0:t0 + 2048, :].rearrange("(t p) c -> p t c", p=128))
                for tl in range(16):
                    xs = xt[tl % 2]
                    r0 = t0 + tl * 128
                    kb.load(sp, xs.res, xs[:], x_d[r0:r0 + 128, :])
                    for kk in range(4):
                        pb = pmm[kk]
                        for j in range(4):
                            k = kk * 4 + j
                            tr(pb, pb[:, j * 128:(j + 1) * 128], xs, xs[:, k * 128:(k + 1) * 128], ident_f)
                        dst = xT[:, kk * 4:(kk + 1) * 4, tl * 128:(tl + 1) * 128]
                        src = pb[:].rearrange("p (a b) -> p a b", a=4)
                        if kk % 2 == 0:
                            ex(act, [pb], [xT], lambda dst=dst, src=src: nc.scalar.copy(out=dst, in_=src))
                        else:
                            ex(dve, [pb], [xT], lambda dst=dst, src=src: nc.vector.tensor_copy(dst, src))
                NCB = 20

                def col_of(cb):
                    return cb * 512 if cb < 12 else 6144 + (cb - 12) * 512

                wsrcs = [win_d[:, col_of(cb):col_of(cb) + 512].rearrange("(k p) c -> p k c", p=128)
                         for cb in range(NCB)]
                wstream = Stream(kb, pool, wp, wsrcs)
                for cb in range(NCB):
                    w = wstream.get(cb)
                    is_v = cb in (6, 7, 8, 11)
                    is_g = cb >= 12
                    is_b = cb in (9, 10)
                    qs = qst[cb % 2]
                    pend = []

                    def flush_one():
                        z_b, tl_ = pend.pop(0)
                        pt = ptr[tl_ % 2]
                        for h in range(4):
                            tr(pt, pt[:, h * 128:(h + 1) * 128], z_b, z_b[:, h * 128:(h + 1) * 128], ident_b)
                        dst = qs[:, :, tl_ * 128:(tl_ + 1) * 128]
                        src = pt[:, 0:512].rearrange("p (a b) -> p a b", a=4)
                        ex(dve, [pt], [qs], lambda: nc.vector.tensor_copy(dst, src))

                    for tl in range(16):
                        pb = pmm[nxt % 4]
                        nxt += 1
                        for k in range(16):
                            mm(pb, pb[:], xT, xT[:, k, tl * 128:(tl + 1) * 128], w, w[:, k, :],
                               start=(k == 0), stop=(k == 15))
                        z_b = zb[tl % 4]
                        r0 = t0 + tl * 128
                        if is_v:
                            ex(act, [pb], [z_b], lambda: nc.scalar.copy(out=z_b[:], in_=pb[:]))
                            c0 = (cb - 6) * 512 if cb < 9 else 1536
                            kb.store(sp, z_b.res, V_d[r0:r0 + 128, c0:c0 + 512], z_b[:], [dram["V"]])
                            continue
                        if is_g:
                            ex(act, [pb], [z_b], lambda: nc.scalar.activation(out=z_b[:], in_=pb[:], func=AF.Sigmoid))
                            c0 = (cb - 12) * 512
                            kb.store(sp, z_b.res, G_d[r0:r0 + 128, c0:c0 + 512], z_b[:], [dram["G"]])
                            continue
                        z_t = zt[tl % 2]
                        u_t = uu[tl % 2]
                        v_t = vv[tl % 2]
                        ex(act, [pb], [z_t], lambda: nc.scalar.copy(out=z_t[:], in_=pb[:]))
                        if not is_b:
                            nb, bw, hh_ = 4, 128, 16
                            cc, ss = tabs[0], tabs[1]
                        else:
                            nb, bw, hh_ = 8, 64, 8
                            cc, ss = tabs[2], tabs[3]
                        z3 = z_t[:].rearrange("p (a b) -> p a b", a=nb)
                        zb3 = z_b[:].rearrange("p (a b) -> p a b", a=nb)
                        u3 = u_t[:].rearrange("p (a b) -> p a b", a=nb)
                        v3 = v_t[:].rearrange("p (a b) -> p a b", a=nb)
                        c3 = cc[:, tl, :].rearrange("p (a b) -> p a b", a=nb)
                        s3 = ss[:, tl, :].rearrange("p (a b) -> p a b", a=nb)
                        rw = 2 * hh_
                        ex(dve, [z_t, cc], [u_t], lambda: nc.vector.tensor_tensor(
                            out=u3, in0=z3[:, :, 0:rw], in1=c3, op=ALU.mult))
                        ex(dve, [z_t, ss], [v_t], lambda: nc.vector.tensor_tensor(
                            out=v3, in0=z3[:, :, 0:rw], in1=s3, op=ALU.mult))
                        ex(pool, [u_t, v_t], [z_b], lambda: nc.gpsimd.tensor_tensor(
                            out=zb3[:, :, 0:hh_], in0=u3[:, :, 0:hh_], in1=v3[:, :, hh_:rw], op=ALU.add))
                        ex(pool, [u_t, v_t], [z_b], lambda: nc.gpsimd.tensor_tensor(
                            out=zb3[:, :, hh_:rw], in0=u3[:, :, hh_:rw], in1=v3[:, :, 0:hh_], op=ALU.add))
                        ex(act, [z_t], [z_b], lambda: nc.scalar.copy(out=zb3[:, :, rw:bw], in_=z3[:, :, rw:bw]))
                        pend.append((z_b, tl))
                        if len(pend) > 2:
                            flush_one()
                    while pend:
                        flush_one()
                    if not (is_v or is_g):
                        if cb < 3:
                            tgt, key, h0 = QT_d, "QT", cb * 4
                        elif cb < 6:
                            tgt, key, h0 = KT_d, "KT", (cb - 3) * 4
                        elif cb == 9:
                            tgt, key, h0 = QT_d, "QT", 12
                        else:
                            tgt, key, h0 = KT_d, "KT", 12
                        for h in range(4):
                            kb.store(sp, qs.res, tgt[h0 + h, :, t0:t0 + 2048], qs[:, h, :], [dram[key]])
        kb.barrier()

        with contextlib.ExitStack() as st:
            msk = sbuf(st, "B_msk", [128, 384], BF16)
            kb.load(sp, msk.res, msk[:], msk_d[:, :])
            qT = [sbuf(st, "B_q%d" % i, [128, S], BF16) for i in range(2)]
            kT = [sbuf(st, "B_k%d" % i, [128, S], BF16) for i in range(2)]
            vs = [sbuf(st, "B_v%d" % i, [128, 32, 128], BF16) for i in range(2)]
            accn = sbuf(st, "B_accn", [128, S], F32)
            accd = sbuf(st, "B_accd", [128, S], F32)
            ob = sbuf(st, "B_ob", [128, S], BF16)
            pT = [sbuf(st, "B_pT%d" % i, [128, 384], BF16) for i in range(3)]
            ps_s = [psum(st, "B_ps%d" % i) for i in range(3)]
            ps_n = [psum(st, "B_pn%d" % i) for i in range(2)]
            ps_d = [psum(st, "B_pd%d" % i) for i in range(2)]
            it = 0
            blk = 0
            for hh in range(4):
                for g, r in enumerate((1, 4, 16)):
                    hd = g * 4 + hh
                    L = S // r
                    nj = L // 128
                    q_, k_, v_ = qT[it % 2], kT[it % 2], vs[it % 2]
                    it += 1
                    kb.load(sp, q_.res, q_[:], QT_d[hd, :, :], [dram["QT"]])
                    kb.load(sp, k_.res, k_[:], KT_d[hd, :, :], [dram["KT"]])
                    for ph in range(r):
                        kb.load(sp, v_.res, v_[:, ph * nj:(ph + 1) * nj, :],
                                V_d[ph:S:r, hd * 128:(hd + 1) * 128].rearrange("(j a) c -> a j c", a=128),
                                [dram["V"]])
                    nblk = r * nj
                    for b0 in range(0, nblk, 4):
                        pn = ps_n[(b0 // 4) % 2]
                        pd = ps_d[(b0 // 4) % 2]
                        for bi in range(b0, b0 + 4):
                            ph, j = bi // nj, bi % nj
                            kts = [kt for kt in (j - 1, j, j + 1) if 0 <= kt < nj]
                            lo = (kts[0] - j + 1) * 128
                            hi = (kts[-1] - j + 2) * 128
                            pss = ps_s[blk % 3]
                            p_t = pT[blk % 3]
                            blk += 1
                            qsl = q_[:, ph + r * 128 * j: ph + r * 128 * j + r * 127 + 1: r]
                            for kt in kts:
                                off = (kt - j + 1) * 128
                                ksl = k_[:, ph + r * 128 * kt: ph + r * 128 * kt + r * 127 + 1: r]
                                mm(pss, pss[:, off:off + 128], k_, ksl, q_, qsl)
                            ex(act, [pss], [p_t], lambda: nc.scalar.activation(
                                out=p_t[:, lo:hi], in_=pss[:, lo:hi], func=AF.Exp, scale=128.0 ** -0.5))
                            ex(dve, [p_t, msk], [p_t], lambda: nc.vector.tensor_tensor(
                                out=p_t[:, lo:hi], in0=p_t[:, lo:hi], in1=msk[:, lo:hi], op=ALU.mult))
                            c0 = (bi - b0) * 128
                            for n_, kt in enumerate(kts):
                                off = (kt - j + 1) * 128
                                mm(pn, pn[:, c0:c0 + 128], v_, v_[:, ph * nj + kt, :], p_t, p_t[:, off:off + 128],
                                   start=(n_ == 0), stop=(n_ == len(kts) - 1))
                            for n_, kt in enumerate(kts):
                                off = (kt - j + 1) * 128
                                mm(pd, pd[:, c0:c0 + 128], ones_b, ones_b[:], p_t, p_t[:, off:off + 128],
                                   start=(n_ == 0), stop=(n_ == len(kts) - 1))
                        ph0, j0 = b0 // nj, b0 % nj
                        outs = []
                        for acc in (accn, accd):
                            av = acc[:].rearrange("p (m r) -> p r m", r=r)
                            if r <= 4:
                                outs.append((av[:, ph0, j0 * 128:j0 * 128 + 512], None))
                            else:
                                outs.append((av[:, ph0:ph0 + 2, :], 2))
                        for (dst, f), pb_, acc, q in ((outs[0], pn, accn, act), (outs[1], pd, accd, dve)):
                            src = pb_[:] if f is None else pb_[:].rearrange("p (f m) -> p f m", f=f)
                            if g == 0:
                                if q is act:
                                    ex(act, [pb_], [acc], lambda dst=dst, src=src: nc.scalar.copy(out=dst, in_=src))
                                else:
                                    ex(dve, [pb_], [acc], lambda dst=dst, src=src: nc.vector.tensor_copy(dst, src))
                            else:
                                ex(dve, [pb_, acc], [acc], lambda dst=dst, src=src: nc.vector.tensor_tensor(
                                    out=dst, in0=dst, in1=src, op=ALU.add))
                ex(dve, [accd], [accd], lambda: nc.vector.reciprocal(out=accd[:], in_=accd[:]))
                ex(dve, [accn, accd], [ob], lambda: nc.vector.tensor_tensor(
                    out=ob[:], in0=accn[:], in1=accd[:], op=ALU.mult))
                kb.store(sp, ob.res, oT_d[hh, :, :], ob[:], [dram["oT"]])
        kb.barrier()

        with contextlib.ExitStack() as st:
            qT = [sbuf(st, "C_q%d" % i, [128, S], BF16) for i in range(2)]
            qZ = [sbuf(st, "C_qz%d" % i, [128, S], BF16) for i in range(2)]
            kT = [sbuf(st, "C_k%d" % i, [128, S], BF16) for i in range(2)]
            vn = [sbuf(st, "C_v%d" % i, [128, 32, 128], BF16) for i in range(2)]
            ob = sbuf(st, "C_ob", [128, S], BF16)
            pT = [sbuf(st, "C_pT%d" % i, [128, 512], BF16) for i in range(4)]
            sacc = [[sbuf(st, "C_sa%d%d" % (i, c), [128, 512], F32) for c in range(2)] for i in range(2)]
            rr = [sbuf(st, "C_rr%d" % i, [128, 512], F32) for i in range(2)]
            aa = [sbuf(st, "C_aa%d" % i, [128, 512], F32) for i in range(2)]
            A_ = sbuf(st, "C_A", [128, 512], F32)
            sq = sbuf(st, "C_sq", [128, 512], F32)
            rs = sbuf(st, "C_rs", [128, 512], F32)
            ps_s = [psum(st, "C_ps%d" % i) for i in range(3)]
            ps_n = [[psum(st, "C_pn%d%d" % (i, c)) for c in range(2)] for i in range(1)]
            ps_d = psum(st, "C_pd")
            ps_q = psum(st, "C_pq")
            gi = 0
            nqb = 0
            for h in range(4):
                q_, k_, v_ = qT[h % 2], kT[h % 2], vn[h % 2]
                kb.load(sp, q_.res, q_[:], QT_d[12 + h, :, :], [dram["QT"]])
                kb.load(sp, k_.res, k_[:], KT_d[12 + h, :, :], [dram["KT"]])
                kb.load(sp, v_.res, v_[:],
                        V_d[:, 1536 + h * 128:1536 + (h + 1) * 128].rearrange("(t p) c -> p t c", p=128),
                        [dram["V"]])
                qz0, qz1 = qZ[0], qZ[1]
                ex(dve, [q_], [qz0], lambda: nc.vector.tensor_copy(qz0[0:64, :], q_[0:64, :]))
                ex(dve, [], [qz0], lambda: nc.vector.memset(qz0[64:128, :], 0.0))
                ex(act, [q_], [qz1], lambda: nc.scalar.copy(out=qz1[64:128, :], in_=q_[64:128, :]))
                ex(dve, [], [qz1], lambda: nc.vector.memset(qz1[0:64, :], 0.0))
                for qb in range(8):
                    pn = ps_n[0]
                    sa = sacc[nqb % 2]
                    nqb += 1
                    steps = [(kt, c) for kt in range(32) for c in range(2)]

                    def qk(i):
                        kt, c = steps[i]
                        pss = ps_s[(gi + i) % 3]
                        mm(pss, pss[:], k_, k_[:, kt * 128:(kt + 1) * 128],
                           qZ[c], qZ[c][:, qb * 512:(qb + 1) * 512])

                    qk(0)
                    for i, (kt, c) in enumerate(steps):
                        if i + 1 < len(steps):
                            qk(i + 1)
                        pss = ps_s[(gi + i) % 3]
                        p_t = pT[(gi + i) % 4]
                        ex(act, [pss], [p_t], lambda: nc.scalar.activation(
                            out=p_t[:], in_=pss[:], func=AF.Exp, scale=0.125))
                        mm(pn[c], pn[c][:], v_, v_[:, kt, :], p_t, p_t[:], start=(kt == 0), stop=(kt == 31))
                        if True:
                            if kt == 0:
                                ex(dve, [p_t], [sa[c]], lambda: nc.vector.tensor_copy(sa[c][:], p_t[:]))
                            else:
                                ex(dve, [p_t, sa[c]], [sa[c]], lambda: nc.vector.tensor_tensor(
                                    out=sa[c][:], in0=sa[c][:], in1=p_t[:], op=ALU.add))
                    gi += len(steps)
                    for c in range(2):
                        mm(ps_q, ps_q[:], ones_f, ones_f[:], sa[c], sa[c][:])
                        pden = ps_q
                        ex(dve, [pden], [rr[c]], lambda: nc.vector.reciprocal(out=rr[c][:], in_=pden[:]))
                        ex(dve, [pn[c], rr[c]], [aa[c]], lambda: nc.vector.tensor_tensor(
                            out=aa[c][:], in0=pn[c][:], in1=rr[c][:], op=ALU.mult))
                    ex(dve, [aa[0], aa[1], neglam], [A_], lambda: nc.vector.scalar_tensor_tensor(
                        out=A_[:], in0=aa[1][:], scalar=neglam[:, 0:1], in1=aa[0][:], op0=ALU.mult, op1=ALU.add))
                    ex(dve, [A_], [sq], lambda: nc.vector.tensor_tensor(out=sq[:], in0=A_[:], in1=A_[:], op=ALU.mult))
                    mm(ps_q, ps_q[:], ones_f, ones_f[:], sq, sq[:])
                    ex(dve, [ps_q], [rs], lambda: nc.vector.tensor_scalar(
                        out=rs[:], in0=ps_q[:], scalar1=1.0 / 128.0, scalar2=1e-5, op0=ALU.mult, op1=ALU.add))
                    ex(act, [rs], [rs], lambda: nc.scalar.activation(out=rs[:], in_=rs[:], func=AF.Ln))
                    ex(act, [rs], [rs], lambda: nc.scalar.activation(out=rs[:], in_=rs[:], func=AF.Exp, scale=-0.5))
                    ex(dve, [A_, rs], [A_], lambda: nc.vector.tensor_tensor(out=A_[:], in0=A_[:], in1=rs[:], op=ALU.mult))
                    ex(dve, [A_, nw], [ob], lambda: nc.vector.tensor_scalar(
                        out=ob[:, qb * 512:(qb + 1) * 512], in0=A_[:], scalar1=nw[:, 0:1], scalar2=None, op0=ALU.mult))
                kb.store(sp, ob.res, oT_d[4 + h, :, :], ob[:], [dram["oT"]])
        kb.barrier()

        with contextlib.ExitStack() as st:
            lnp = sbuf(st, "D_lnp", [128, 2 * D], F32)
            kb.load(sp, lnp.res, lnp[:], ln_d[:, 0:2 * D])
            wr = sbuf(st, "D_wr", [128, 16, NE], F32)
            kb.load(sp, wr.res, wr[:], wr_d.rearrange("(k p) e -> p k e", p=128))
            wbr = sbuf(st, "D_wbr", [128, 8, D], BF16)
            wo = sbuf(st, "D_wo", [128, 16, D], BF16)
            for g in range(2):
                kb.load(pool, wbr.res, wbr[:, 4 * g:4 * g + 4, :],
                        wbr_d[g * 512:(g + 1) * 512, :].rearrange("(k p) c -> p k c", p=128))
            for i in range(4):
                kb.load(pool, wo.res, wo[:, 4 * i:4 * i + 4, :],
                        wout_d[i * 512:(i + 1) * 512, :].rearrange("(k p) c -> p k c", p=128))
            oTb = [sbuf(st, "D_oTb%d" % i, [128, 8, 512], BF16) for i in range(2)]
            Gt = [sbuf(st, "D_G%d" % i, [128, 2 * D], BF16) for i in range(2)]
            xts = [sbuf(st, "D_xt%d" % i, [128, 1, D], F32) for i in range(3)]
            tA = [sbuf(st, "D_tA%d" % i, [128, 512], F32) for i in range(1)]
            tB = [sbuf(st, "D_tB%d" % i, [128, 512], F32) for i in range(1)]
            mg = sbuf(st, "D_mg", [128, D], BF16)
            mT = sbuf(st, "D_mT", [128, 16, 128], BF16)
            x1b = [sbuf(st, "D_x1b%d" % i, [128, D], BF16) for i in range(1)]
            x1T = sbuf(st, "D_x1T", [128, 16, 128], F32)
            junk = Buf(x1T.t, "junkview")
            junk.res = x1T.res
            junk_ap = x1T[:].rearrange("p a b -> p (a b)")
            st1 = sbuf(st, "D_st1", [128, 8], F32)
            eb = sbuf(st, "D_eb", [128, NE], F32)
            pf = [psum(st, "D_p%d" % i) for i in range(6)]
            pbf = [psum(st, "D_pb%d" % i, BF16) for i in range(2)]
            npf = 0

            def prefetch(tt):
                if tt >= NT:
                    return
                if tt % 4 == 0:
                    ob_ = oTb[(tt // 4) % 2]
                    c0_ = tt * 128
                    kb.load(sp, ob_.res, ob_[:], oT_d[:, :, c0_:c0_ + 512].rearrange("k p t -> p k t"), [dram["oT"]])
                kb.load(sp, Gt[tt % 2].res, Gt[tt % 2][:], G_d[tt * 128:(tt + 1) * 128, :], [dram["G"]])
                kb.load(sp, xts[tt % 3].res, xts[tt % 3][:, 0, :], x_d[tt * 128:(tt + 1) * 128, :])

            def stage1(tt):
                nonlocal npf
                ti = tt % 4
                r0 = tt * 128
                ob_ = oTb[(tt // 4) % 2]
                G_ = Gt[tt % 2]
                xt_ = xts[tt % 3]
                hv = xt_[:, 0, :]
                for db in range(4):
                    pa = pf[npf % 6]
                    pb_ = pf[(npf + 1) % 6]
                    npf += 2
                    for g, pp in ((0, pa), (1, pb_)):
                        for c in range(4):
                            mm(pp, pp[:], ob_, ob_[:, 4 * g + c, ti * 128:(ti + 1) * 128],
                               wbr, wbr[:, 4 * g + c, db * 512:(db + 1) * 512], start=(c == 0), stop=(c == 3))
                    ta, tb_ = tA[0], tB[0]
                    ex(dve, [G_, pa], [ta], lambda: nc.vector.tensor_tensor(
                        out=ta[:], in0=G_[:, db * 512:(db + 1) * 512], in1=pa[:], op=ALU.mult))
                    ex(dve, [G_, pb_], [tb_], lambda: nc.vector.tensor_tensor(
                        out=tb_[:], in0=G_[:, D + db * 512:D + (db + 1) * 512], in1=pb_[:], op=ALU.mult))
                    ex(dve, [ta, tb_], [mg], lambda: nc.vector.tensor_tensor(
                        out=mg[:, db * 512:(db + 1) * 512], in0=ta[:], in1=tb_[:], op=ALU.add))
                for half in range(2):
                    pt = pbf[half]
                    for j in range(8):
                        k = half * 8 + j
                        tr(pt, pt[:, j * 128:(j + 1) * 128], mg, mg[:, k * 128:(k + 1) * 128], ident_b)
                    dst = mT[:, half * 8:(half + 1) * 8, :]
                    src = pt[:].rearrange("p (a b) -> p a b", a=8)
                    if half == 0:
                        ex(act, [pt], [mT], lambda: nc.scalar.copy(out=dst, in_=src))
                    else:
                        ex(dve, [pt], [mT], lambda: nc.vector.tensor_copy(dst, src))
                for obk in range(4):
                    pm = pf[npf % 6]
                    npf += 1
                    for k in range(16):
                        mm(pm, pm[:], mT, mT[:, k, :], wo, wo[:, k, obk * 512:(obk + 1) * 512],
                           start=(k == 0), stop=(k == 15))
                    ex(dve, [xt_, pm], [xt_], lambda: nc.vector.scalar_tensor_tensor(
                        out=xt_[:, 0, obk * 512:(obk + 1) * 512], in0=xt_[:, 0, obk * 512:(obk + 1) * 512],
                        scalar=ALPHA, in1=pm[:], op0=ALU.mult, op1=ALU.add))

            def stage2(tt):
                nonlocal npf
                r0 = tt * 128
                xt_ = xts[tt % 3]
                hv = xt_[:, 0, :]
                layer_norm(nc, kb, ex, hv, xt_, junk, junk_ap, st1, lnp, 0)
                kb.store(sp, xt_.res, x1_d[r0:r0 + 128, :], hv, [dram["x1"]])
                xb_ = x1b[0]
                ex(act, [xt_], [xb_], lambda: nc.scalar.copy(out=xb_[:], in_=hv))
                kb.store(sp, xb_.res, x1b_d[r0:r0 + 128, :], xb_[:], [dram["x1b"]])

            def stage2b(tt):
                nonlocal npf
                xt_ = xts[tt % 3]
                for kk in range(4):
                    pb = pf[npf % 6]
                    npf += 1
                    for j in range(4):
                        k = kk * 4 + j
                        tr(pb, pb[:, j * 128:(j + 1) * 128], xt_, xt_[:, 0, k * 128:(k + 1) * 128], ident_f)
                    dst = x1T[:, kk * 4:(kk + 1) * 4, :]
                    src = pb[:].rearrange("p (a b) -> p a b", a=4)
                    if kk % 2 == 0:
                        ex(act, [pb], [x1T], lambda: nc.scalar.copy(out=dst, in_=src))
                    else:
                        ex(dve, [pb], [x1T], lambda: nc.vector.tensor_copy(dst, src))
                pl = pf[npf % 6]
                npf += 1
                for k in range(16):
                    mm(pl, pl[:, 0:NE], x1T, x1T[:, k, :], wr, wr[:, k, :], start=(k == 0), stop=(k == 15))
                ex(dve, [pl], [st1], lambda: nc.vector.reduce_max(out=st1[:, 4:5], in_=pl[:, 0:NE], axis=AX.X))
                ex(dve, [st1], [st1], lambda: nc.vector.tensor_scalar_mul(out=st1[:, 5:6], in0=st1[:, 4:5], scalar1=-1.0))
                ex(act, [pl, st1], [eb, st1], lambda: nc.scalar.activation(
                    out=eb[:], in_=pl[:, 0:NE], func=AF.Exp, bias=st1[:, 5:6], scale=1.0, accum_out=st1[:, 6:7]))
                ex(dve, [st1], [st1], lambda: nc.vector.reciprocal(out=st1[:, 7:8], in_=st1[:, 6:7]))
                ex(dve, [eb, st1], [aff], lambda: nc.vector.tensor_scalar(
                    out=aff[:, tt * NE:(tt + 1) * NE], in0=eb[:], scalar1=st1[:, 7:8], scalar2=None, op0=ALU.mult))

            prefetch(0)
            prefetch(1)
            stage1(0)
            for tt in range(NT):
                prefetch(tt + 2)
                stage2(tt)
                if tt + 1 < NT:
                    stage1(tt + 1)
                stage2b(tt)
        kb.barrier()

        with contextlib.ExitStack() as st:
            lo = sbuf(st, "E_lo", [128, NE], F32)
            hi = sbuf(st, "E_hi", [128, NE], F32)
            mid = sbuf(st, "E_mid", [128, NE], F32)
            dd = sbuf(st, "E_dd", [128, NE], F32)
            sel = sbuf(st, "E_sel", [128, NE], F32)
            cmp_ = sbuf(st, "E_cmp", [128, NT * NE], F32)
            cnt = sbuf(st, "E_cnt", [128, NE], F32)
            mkb = sbuf(st, "E_mkb", [128, NT * NE], BF16)
            tri = sbuf(st, "E_tri", [128, 128], BF16)
            eoff = sbuf(st, "E_eoff", [128, NT * NE], F32)
            tt_ = sbuf(st, "E_tt", [128, NT * NE], F32)
            cum = sbuf(st, "E_cum", [128, NT * NE], F32)
            pos = sbuf(st, "E_pos", [128, NT * NE], F32)
            val = sbuf(st, "E_val", [128, NT * NE], F32)
            pc = psum(st, "E_pc")
            pw = psum(st, "E_pw")
            ptt = psum(st, "E_pt")
            kb.load(sp, tri.res, tri[:], tri_d[:, :])
            kb.load(sp, eoff.res, eoff[:], eoff_d[:, :])
            ex(dve, [], [lo], lambda: nc.vector.memset(lo[:], 0.0))
            ex(dve, [], [hi], lambda: nc.vector.memset(hi[:], 1.0))
            aff3 = aff[:].rearrange("p (t e) -> p t e", e=NE)

            def bc(b):
                a = b[:]
                return bass.AP(a.tensor, a.offset, [list(a.ap[0]), [0, NT], list(a.ap[1])])

            for _ in range(34):
                ex(dve, [lo, hi], [mid], lambda: nc.vector.tensor_tensor(out=mid[:], in0=lo[:], in1=hi[:], op=ALU.add))
                ex(dve, [mid], [mid], lambda: nc.vector.tensor_scalar_mul(out=mid[:], in0=mid[:], scalar1=0.5))
                ex(dve, [aff, mid], [cmp_], lambda: nc.vector.tensor_tensor(
                    out=cmp_[:].rearrange("p (t e) -> p t e", e=NE), in0=aff3, in1=bc(mid), op=ALU.is_gt))
                ex(dve, [cmp_], [cnt], lambda: nc.vector.reduce_sum(
                    out=cnt[:], in_=cmp_[:].rearrange("p (t e) -> p e t", e=NE), axis=AX.X))
                mm(pc, pc[:, 0:NE], ones_f, ones_f[:], cnt, cnt[:])
                ex(dve, [pc], [sel], lambda: nc.vector.tensor_scalar(
                    out=sel[:], in0=pc[:, 0:NE], scalar1=float(CAP) - 0.5, scalar2=None, op0=ALU.is_gt))
                ex(dve, [mid, lo], [dd], lambda: nc.vector.tensor_tensor(out=dd[:], in0=mid[:], in1=lo[:], op=ALU.subtract))
                ex(dve, [dd, sel], [dd], lambda: nc.vector.tensor_tensor(out=dd[:], in0=dd[:], in1=sel[:], op=ALU.mult))
                ex(dve, [dd, lo], [lo], lambda: nc.vector.tensor_tensor(out=lo[:], in0=lo[:], in1=dd[:], op=ALU.add))
                ex(dve, [mid, hi], [dd], lambda: nc.vector.tensor_tensor(out=dd[:], in0=hi[:], in1=mid[:], op=ALU.subtract))
                ex(dve, [dd, sel], [dd], lambda: nc.vector.tensor_tensor(out=dd[:], in0=dd[:], in1=sel[:], op=ALU.mult))
                ex(dve, [dd, mid], [hi], lambda: nc.vector.tensor_tensor(out=hi[:], in0=mid[:], in1=dd[:], op=ALU.add))
            ex(dve, [aff, lo], [cmp_], lambda: nc.vector.tensor_tensor(
                out=cmp_[:].rearrange("p (t e) -> p t e", e=NE), in0=aff3, in1=bc(lo), op=ALU.is_gt))
            ex(act, [cmp_], [mkb], lambda: nc.scalar.copy(out=mkb[:], in_=cmp_[:]))
            mm(pw, pw[:], tri, tri[:], mkb, mkb[:])
            mm(ptt, ptt[:], ones_b, ones_b[:], mkb, mkb[:])
            ex(act, [ptt], [tt_], lambda: nc.scalar.copy(out=tt_[:], in_=ptt[:]))
            ex(dve, [], [cum], lambda: nc.vector.memset(cum[:, 0:NE], 0.0))
            for t in range(1, NT):
                ex(dve, [cum, tt_], [cum], lambda t=t: nc.vector.tensor_tensor(
                    out=cum[:, t * NE:(t + 1) * NE], in0=cum[:, (t - 1) * NE:t * NE],
                    in1=tt_[:, (t - 1) * NE:t * NE], op=ALU.add))
            ex(dve, [pw, cum], [pos], lambda: nc.vector.tensor_tensor(out=pos[:], in0=cum[:], in1=pw[:], op=ALU.add))
            ex(dve, [pos], [val], lambda: nc.vector.tensor_scalar(
                out=val[:], in0=pos[:], scalar1=float(CAP) - 0.5, scalar2=None, op0=ALU.is_lt))
            ex(dve, [val, cmp_], [val], lambda: nc.vector.tensor_tensor(out=val[:], in0=val[:], in1=cmp_[:], op=ALU.mult))
            ex(dve, [aff, val], [gm], lambda: nc.vector.tensor_tensor(out=gm[:], in0=aff[:], in1=val[:], op=ALU.mult))
            ex(dve, [pos, eoff], [pos], lambda: nc.vector.tensor_tensor(out=pos[:], in0=pos[:], in1=eoff[:], op=ALU.add))
            ex(dve, [pos], [pos], lambda: nc.vector.tensor_scalar_add(out=pos[:], in0=pos[:], scalar1=-BIG))
            ex(dve, [pos, val], [pos], lambda: nc.vector.tensor_tensor(out=pos[:], in0=pos[:], in1=val[:], op=ALU.mult))
            ex(dve, [pos], [pos], lambda: nc.vector.tensor_scalar_add(out=pos[:], in0=pos[:], scalar1=BIG))
            ex(dve, [pos], [idx_i], lambda: nc.vector.tensor_copy(idx_i[:], pos[:]))
            if DEBUG:
                kb.store(sp, aff.res, aff_d[:, :], aff[:])
        kb.barrier()

        with contextlib.ExitStack() as st:
            xb = [sbuf(st, "F_xb%d" % i, [128, D], BF16) for i in range(3)]
            for tt in range(NT):
                b = xb[tt % 3]
                kb.load(sp, b.res, b[:], x1b_d[tt * 128:(tt + 1) * 128, :], [dram["x1b"]])
                for e in range(NE):
                    col = tt * NE + e
                    if b.res.dout is None:
                        b.res.dout = kb.dsem()
                    kb.dma(pool, None, None, [b.res, idx_i.res], [], b.res.dout,
                           fn=lambda b=b, col=col: nc.gpsimd.indirect_dma_start(
                               out=xin_d[:, :], out_offset=bass.IndirectOffsetOnAxis(ap=idx_i[:, col:col + 1], axis=0),
                               in_=b[:], in_offset=None, bounds_check=bc_reg, oob_is_err=False))
        kb.barrier()

        with contextlib.ExitStack() as st:
            xi = [sbuf(st, "G_xi%d" % i, [128, 4, D], BF16) for i in range(2)]
            xiT = [sbuf(st, "G_xiT%d" % i, [128, 16, 512], BF16) for i in range(2)]
            hT = sbuf(st, "G_hT", [128, 16, 512], BF16)
            wpc = [sbuf(st, "G_wp%d" % i, [128, 16, 512], BF16) for i in range(5)]
            sgl = [sbuf(st, "G_sg%d" % i, [128, 512], F32) for i in range(2)]
            ost = [sbuf(st, "G_os%d" % i, [128, 512], BF16) for i in range(3)]
            ptr = [psum(st, "G_pt%d" % i, BF16) for i in range(2)]
            pgu = [psum(st, "G_pg%d" % i) for i in range(4)]
            pdn = [psum(st, "G_pd%d" % i) for i in range(2)]
            nos = 0
            npd = 0

            def piece(src_d, e, cblk):
                return src_d[e * D:(e + 1) * D, cblk * 512:(cblk + 1) * 512].rearrange("(k p) c -> p k c", p=128)

            wsrcs = []
            for e in range(NE):
                for fb in range(4):
                    wsrcs.append(piece(wg_d, e, fb))
                    wsrcs.append(piece(wu_d, e, fb))
                for db in range(4):
                    wsrcs.append(piece(wd_d, e, db))
            wstream = Stream(kb, pool, wpc, wsrcs, hold=2)

            def load_x(e):
                x_ = xi[e % 2]
                kb.load(sp, x_.res, x_[:], xin_d[e * CAP:(e + 1) * CAP, :].rearrange("(s p) d -> p s d", p=128))

            def transposes(e):
                x_ = xi[e % 2]
                xt_ = xiT[e % 2]
                for kp in range(8):
                    pt = ptr[kp % 2]
                    for k2 in range(2):
                        k = kp * 2 + k2
                        for s_ in range(4):
                            tr(pt, pt[:, (k2 * 4 + s_) * 128:(k2 * 4 + s_ + 1) * 128], x_,
                               x_[:, s_, k * 128:(k + 1) * 128], ident_b)
                    dst = xt_[:, kp * 2:kp * 2 + 2, :]
                    src = pt[:].rearrange("p (a b) -> p a b", a=2)
                    if kp % 2 == 0:
                        ex(act, [pt], [xt_], lambda: nc.scalar.copy(out=dst, in_=src))
                    else:
                        ex(dve, [pt], [xt_], lambda: nc.vector.tensor_copy(dst, src))

            load_x(0)
            wstream.get(0)
            transposes(0)
            for e in range(NE):
                if e + 1 < NE:
                    load_x(e + 1)
                xt_ = xiT[e % 2]
                for fb in range(4):
                    wg = wstream.get(e * 12 + fb * 2)
                    wu = wstream.get(e * 12 + fb * 2 + 1)
                    for fi in range(4):
                        fc = fb * 4 + fi
                        pg = pgu[2 * (fc % 2)]
                        pu = pgu[2 * (fc % 2) + 1]
                        for k in range(16):
                            mm(pg, pg[:], wg, wg[:, k, fi * 128:(fi + 1) * 128], xt_, xt_[:, k, :],
                               start=(k == 0), stop=(k == 15))
                        for k in range(16):
                            mm(pu, pu[:], wu, wu[:, k, fi * 128:(fi + 1) * 128], xt_, xt_[:, k, :],
                               start=(k == 0), stop=(k == 15))
                        s_g = sgl[fc % 2]
                        ex(act, [pg], [s_g], lambda: nc.scalar.activation(out=s_g[:], in_=pg[:], func=AF.Silu))
                        ex(dve, [s_g, pu], [hT], lambda: nc.vector.tensor_tensor(
                            out=hT[:, fc, :], in0=s_g[:], in1=pu[:], op=ALU.mult))
                if e + 1 < NE:
                    transposes(e + 1)
                for db in range(4):
                    wd = wstream.get(e * 12 + 8 + db)
                    for s_ in range(4):
                        pd = pdn[npd % 2]
                        npd += 1
                        for fc in range(16):
                            mm(pd, pd[:], hT, hT[:, fc, s_ * 128:(s_ + 1) * 128], wd, wd[:, fc, :],
                               start=(fc == 0), stop=(fc == 15))
                        o_ = ost[nos % 3]
                        nos += 1
                        if s_ % 2 == 0:
                            ex(act, [pd], [o_], lambda: nc.scalar.copy(out=o_[:], in_=pd[:]))
                        else:
                            ex(dve, [pd], [o_], lambda: nc.vector.tensor_copy(o_[:], pd[:]))
                        r0 = e * CAP + s_ * 128
                        kb.store(sp, o_.res, yo_d[r0:r0 + 128, db * 512:(db + 1) * 512], o_[:], [dram["yo"]])
        kb.barrier()

        with contextlib.ExitStack() as st:
            lnp = sbuf(st, "H_lnp", [128, 2 * D], F32)
            kb.load(sp, lnp.res, lnp[:], ln_d[:, 2 * D:4 * D])
            NG = 32
            xa = [sbuf(st, "H_xa%d" % i, [128, 1, D], F32) for i in range(3)]
            gb = [sbuf(st, "H_g%d" % i, [128, D], BF16) for i in range(NG)]
            dg = [sbuf(st, "H_dg%d" % i, [128, 128], BF16) for i in range(4)]
            junk = sbuf(st, "H_junk", [128, D], F32)
            st1 = sbuf(st, "H_st1", [128, 8], F32)
            pacc = [[psum(st, "H_p%d%d" % (i, c)) for c in range(4)] for i in range(2)]
            gset = [Res("gset0"), Res("gset1")]
            for i_, g_ in enumerate(gb):
                g_.res = gset[i_ // NE]
                ex(dve, [], [g_], lambda: nc.vector.memset(g_[:], 0.0))

            def gathers(tt):
                rs = gset[tt % 2]
                if rs.din is None:
                    rs.din = kb.dsem()
                lst = []
                fns = []
                for e in range(NE):
                    col = tt * NE + e
                    g_ = gb[(tt % 2) * NE + e]
                    g_.res = rs
                    fns.append(lambda g_=g_, col=col: nc.gpsimd.indirect_dma_start(
                        out=g_[:], out_offset=None, in_=yo_d[:, :],
                        in_offset=bass.IndirectOffsetOnAxis(ap=idx_i[:, col:col + 1], axis=0),
                        bounds_check=bc_reg, oob_is_err=False))
                    lst.append(g_)
                kb.dma_group(pool, fns, [idx_i.res, dram["yo"]], [rs], rs.din)
                return lst

            nd = 0
            glists = {0: gathers(0)}

            def stageA(tt):
                nonlocal nd
                a_ = xa[tt % 3]
                r0 = tt * 128
                kb.load(sp, a_.res, a_[:, 0, :], x1_d[r0:r0 + 128, :], [dram["x1"]])
                glist = glists.pop(tt)
                if tt + 1 < NT:
                    glists[tt + 1] = gathers(tt + 1)
                pa = pacc[tt % 2]
                for e in range(NE):
                    col = tt * NE + e
                    d_ = dg[nd % 4]
                    nd += 1
                    ex(dve, [ident_b, gm], [d_], lambda: nc.vector.tensor_scalar(
                        out=d_[:], in0=ident_b[:], scalar1=gm[:, col:col + 1], scalar2=None, op0=ALU.mult))
                    g_ = glist[e]
                    for cb in range(4):
                        mm(pa[cb], pa[cb][:], d_, d_[:], g_, g_[:, cb * 512:(cb + 1) * 512],
                           start=(e == 0), stop=(e == NE - 1))

            def stageA2(tt):
                a_ = xa[tt % 3]
                pa = pacc[tt % 2]
                for cb in range(4):
                    ex(dve, [a_, pa[cb]], [a_], lambda: nc.vector.scalar_tensor_tensor(
                        out=a_[:, 0, cb * 512:(cb + 1) * 512], in0=a_[:, 0, cb * 512:(cb + 1) * 512],
                        scalar=ALPHA, in1=pa[cb][:], op0=ALU.mult, op1=ALU.add))

            def stageB(tt):
                a_ = xa[tt % 3]
                r0 = tt * 128
                layer_norm(nc, kb, ex, a_[:, 0, :], a_, junk, junk[:], st1, lnp, 0)
                kb.store(sp, a_.res, out_d[r0:r0 + 128, :], a_[:, 0, :])

            stageA(0)
            stageA2(0)
            for tt in range(NT):
                if tt + 1 < NT:
                    stageA(tt + 1)
                stageB(tt)
                if tt + 1 < NT:
                    stageA2(tt + 1)
        kb.barrier()
        nc._marks = kb.marks
    return nc


def layer_norm(nc, kb, ex, hv, hb, junk, junk_ap, st1, lnp, off):
    dve, act, pool = kb.dve, kb.act, kb.pool
    ex(dve, [hb], [st1], lambda: nc.vector.reduce_sum(out=st1[:, 0:1], in_=hv, axis=AX.X))
    ex(dve, [st1], [st1], lambda: nc.vector.tensor_scalar_mul(out=st1[:, 1:2], in0=st1[:, 0:1], scalar1=-1.0 / D))
    ex(dve, [hb, st1], [hb], lambda: nc.vector.tensor_scalar(
        out=hv, in0=hv, scalar1=st1[:, 1:2], scalar2=None, op0=ALU.add))
    ex(act, [hb], [junk, st1], lambda: nc.scalar.activation(
        out=junk_ap, in_=hv, func=AF.Square, accum_out=st1[:, 2:3]))
    ex(dve, [st1], [st1], lambda: nc.vector.tensor_scalar(
        out=st1[:, 3:4], in0=st1[:, 2:3], scalar1=1.0 / D, scalar2=LN_EPS, op0=ALU.mult, op1=ALU.add))
    ex(act, [st1], [st1], lambda: nc.scalar.activation(out=st1[:, 3:4], in_=st1[:, 3:4], func=AF.Ln))
    ex(act, [st1], [st1], lambda: nc.scalar.activation(out=st1[:, 3:4], in_=st1[:, 3:4], func=AF.Exp, scale=-0.5))
    ex(dve, [hb, st1, lnp], [hb], lambda: nc.vector.scalar_tensor_tensor(
        out=hv, in0=hv, scalar=st1[:, 3:4], in1=lnp[:, off:off + D], op0=ALU.mult, op1=ALU.mult))
    ex(dve, [hb, lnp], [hb], lambda: nc.vector.tensor_tensor(
        out=hv, in0=hv, in1=lnp[:, off + D:off + 2 * D], op=ALU.add))


DEBUG_OUT = ()


def _consts():
    import ml_dtypes
    bf = ml_dtypes.bfloat16
    pos = np.arange(S, dtype=np.float32)
    c = {}
    for nm, half, rot, nblk in (("A", 16, 32, 4), ("B", 8, 16, 8)):
        inv = (np.float32(500000.0) ** (-2.0 * np.arange(half, dtype=np.float32) / rot)).astype(np.float32)
        ang = (pos[:, None] * inv[None, :]).astype(np.float32)
        cs, sn = np.cos(ang).astype(np.float32), np.sin(ang).astype(np.float32)
        cc = np.concatenate([cs, cs], axis=1)
        ss = np.concatenate([sn, -sn], axis=1)
        c["cc" + nm] = np.ascontiguousarray(np.tile(cc, (1, nblk)))
        c["ss" + nm] = np.ascontiguousarray(np.tile(ss, (1, nblk)))
    a = np.arange(128)[:, None]
    b = np.arange(128)[None, :]
    m = np.concatenate([(a - b >= 64), (np.abs(a - b) <= 64), (b - a >= 64)], axis=1)
    c["bandmask"] = m.astype(np.float32).astype(bf)
    c["ident_f"] = np.eye(128, dtype=np.float32)
    c["ident_b"] = np.eye(128, dtype=np.float32).astype(bf)
    c["tri_b"] = (a < b).astype(np.float32).astype(bf)
    eo = np.tile((np.arange(NE, dtype=np.float32) * CAP)[None, :], (128, NT))
    c["eoff"] = np.ascontiguousarray(eo.astype(np.float32))
    return c


_CACHE = {}


def kernel(x, w_in, lambda_q1, lambda_k1, lambda_q2, lambda_k2, diff_norm_w, w_branch, w_out,
           ln1_g, ln1_b, w_router, w_gate, w_up, w_down, ln2_g, ln2_b):
    f = lambda a: np.ascontiguousarray(np.asarray(a, dtype=np.float32))
    x = f(x)
    if "nc" not in _CACHE:
        _CACHE["nc"] = build_program()
        _CACHE["c"] = _consts()
    nc = _CACHE["nc"]
    c = _CACHE["c"]
    lam4 = np.concatenate([f(lambda_q1)[0], f(lambda_k1)[0], f(lambda_q2)[0], f(lambda_k2)[0]])[None, :]
    lnp = np.concatenate([f(ln1_g)[0], f(ln1_b)[0], f(ln2_g)[0], f(ln2_b)[0]])[None, :]
    shared = {
        "w_in": f(w_in)[0], "w_branch": f(w_branch)[0].reshape(1024, D), "w_out": f(w_out)[0],
        "w_router": f(w_router)[0], "w_gate": f(w_gate)[0].reshape(NE * D, D),
        "w_up": f(w_up)[0].reshape(NE * D, D), "w_down": f(w_down)[0].reshape(NE * D, D),
        "lam4": np.ascontiguousarray(np.broadcast_to(lam4, (128, 256))),
        "nw": np.ascontiguousarray(f(diff_norm_w)[0].reshape(128, 1)),
        "lnp": np.ascontiguousarray(np.broadcast_to(lnp, (128, 4 * D))),
    }
    shared.update(c)
    nb = x.shape[0]
    in_maps = []
    for core in range(8):
        m = dict(shared)
        m["x"] = np.ascontiguousarray(x[(core // 2) % nb])
        in_maps.append(m)
    res = run_bass_kernel_spmd(nc, in_maps, core_ids=list(range(8)))
    out = np.stack([np.asarray(res.results[2 * b]["out"], dtype=np.float32) for b in range(nb)], axis=0)
    return out
```

```python
import contextlib
import numpy as np
import concourse.bass as bass
import concourse.mybir as mybir
from concourse.bass_utils import run_bass_kernel_spmd

F32 = mybir.dt.float32
BF16 = mybir.dt.bfloat16
I32 = mybir.dt.int32
ALU = mybir.AluOpType
AF = mybir.ActivationFunctionType
AX = mybir.AxisListType

S = 4096
D = 2048
NT = S // 128
NE = 16
CAP = 512
ALPHA = 2.0 ** 0.25
LN_EPS = 1e-5
BIG = float(1 << 20)
DEBUG = False
FUSE_WAIT = True


class Ev:
    __slots__ = ("sem", "val")

    def __init__(self, sem, val):
        self.sem = sem
        self.val = val


class DSem:
    def __init__(self, h):
        self.h = h
        self.n = 0


class Res:
    def __init__(self, name):
        self.name = name
        self.w = None
        self.r = {}
        self.din = None
        self.dout = None


class Q:
    def __init__(self, eng, sem, is_pe=False):
        self.eng = eng
        self.sem = sem
        self.n = 0
        self.known = {}
        self.is_pe = is_pe

    def wait(self, ev):
        if ev is None:
            return
        if ev.sem is self.sem and self.is_pe:
            return
        k = id(ev.sem)
        if self.known.get(k, 0) >= ev.val:
            return
        self.eng.wait_ge(ev.sem, ev.val)
        self.known[k] = ev.val


class KB:
    def __init__(self, nc, es):
        self.nc = nc
        self.es = es
        self.nsem = 0
        self.pe = Q(nc.tensor, self.newsem("q_pe"), True)
        self.act = Q(nc.scalar, self.newsem("q_act"))
        self.dve = Q(nc.vector, self.newsem("q_dve"))
        self.pool = Q(nc.gpsimd, self.newsem("q_pool"))
        self.sp = Q(nc.sync, self.newsem("q_sp"))
        self.queues = [self.pe, self.act, self.dve, self.pool, self.sp]
        self.dsems = []

    def newsem(self, name):
        self.nsem += 1
        return self.es.enter_context(self.nc.semaphore(name))

    def dsem(self):
        free = getattr(self, "free_dsems", None)
        if free:
            return free.pop()
        d = DSem(self.newsem("d%d" % len(self.dsems)))
        self.dsems.append(d)
        return d

    def dma_group(self, q, fns, reads, writes, ds):
        self._pre(q, reads, writes)
        for fn in fns:
            inst = fn()
            ds.n += 16
            inst.then_inc(ds.h, 16)
        self._post(Ev(ds.h, ds.n), reads, writes)

    def _need(self, q, reads, writes):
        need = {}

        def add(ev):
            if ev is None:
                return
            if ev.sem is q.sem and q.is_pe:
                return
            k = id(ev.sem)
            if q.known.get(k, 0) >= ev.val:
                return
            if k not in need or need[k].val < ev.val:
                need[k] = ev

        for r in reads:
            add(r.w)
        for w in writes:
            add(w.w)
            for ev in w.r.values():
                add(ev)
        return list(need.values())

    def _pre(self, q, reads, writes, keep_last=False):
        evs = self._need(q, reads, writes)
        last = None
        if keep_last and evs:
            last = evs.pop()
        for ev in evs:
            q.eng.wait_ge(ev.sem, ev.val)
            q.known[id(ev.sem)] = ev.val
        return last

    def _post(self, ev, reads, writes):
        for r in reads:
            r.r[id(ev.sem)] = ev
        for w in writes:
            w.w = ev
            w.r = {}

    def op(self, q, reads, writes, fn):
        last = self._pre(q, reads, writes, keep_last=FUSE_WAIT)
        inst = fn()
        if last is not None:
            inst._wait_ge(last.sem, last.val)
            q.known[id(last.sem)] = last.val
        q.n += 1
        inst.then_inc(q.sem, 1)
        self._post(Ev(q.sem, q.n), reads, writes)

    def dma(self, q, out, in_, reads, writes, ds, fn=None):
        self._pre(q, reads, writes)
        if fn is None:
            inst = q.eng.dma_start(out=out, in_=in_)
        else:
            inst = fn()
        ds.n += 16
        inst.then_inc(ds.h, 16)
        self._post(Ev(ds.h, ds.n), reads, writes)

    def load(self, q, res, out, in_, src=()):
        if res.din is None:
            res.din = self.dsem()
        self.dma(q, out, in_, list(src), [res], res.din)

    def store(self, q, res, out, in_, dst=()):
        if res.dout is None:
            res.dout = self.dsem()
        self.dma(q, out, in_, [res], list(dst), res.dout)

    def barrier(self):
        self.marks = getattr(self, "marks", [])
        self.marks.append({"pe": self.pe.n, "act": self.act.n, "dve": self.dve.n, "pool": self.pool.n})
        for q in self.queues:
            for p in self.queues:
                if p is not q and p.n > 0:
                    q.wait(Ev(p.sem, p.n))
            for d in self.dsems:
                if d.n > 0:
                    q.wait(Ev(d.h, d.n))
        self.free_dsems = list(self.dsems)


class Stream:
    def __init__(self, kb, q, bufs, srcs, hold=1):
        self.kb, self.q, self.bufs, self.srcs = kb, q, bufs, srcs
        self.n = 0
        self.hold = hold

    def get(self, i):
        last = min(i + len(self.bufs) - self.hold, len(self.srcs) - 1)
        while self.n <= last:
            b = self.bufs[self.n % len(self.bufs)]
            self.kb.load(self.q, b.res, b[:], self.srcs[self.n])
            self.n += 1
        return self.bufs[i % len(self.bufs)]


class Buf:
    def __init__(self, t, name):
        self.t = t
        self.res = Res(name)

    def __getitem__(self, k):
        return self.t[k]


def build_program():
    nc = bass.Bass("TRN2", target_bir_lowering=False)
    es = contextlib.ExitStack()

    def din(name, shape, dt=F32):
        return nc.dram_tensor(name, list(shape), dt, kind="ExternalInput").ap()

    def dscr(name, shape, dt):
        kind = "ExternalOutput" if (DEBUG and name in DEBUG_OUT) else "Internal"
        return nc.dram_tensor(name, list(shape), dt, kind=kind).ap()

    x_d = din("x", [S, D])
    win_d = din("w_in", [D, 10240])
    wbr_d = din("w_branch", [1024, D])
    wout_d = din("w_out", [D, D])
    wr_d = din("w_router", [D, NE])
    wg_d = din("w_gate", [NE * D, D])
    wu_d = din("w_up", [NE * D, D])
    wd_d = din("w_down", [NE * D, D])
    lam_d = din("lam4", [128, 256])
    nw_d = din("nw", [128, 1])
    ln_d = din("lnp", [128, 4 * D])
    ccA_d = din("ccA", [S, 128])
    ssA_d = din("ssA", [S, 128])
    ccB_d = din("ccB", [S, 128])
    ssB_d = din("ssB", [S, 128])
    msk_d = din("bandmask", [128, 384], BF16)
    idf_d = din("ident_f", [128, 128])
    idb_d = din("ident_b", [128, 128], BF16)
    tri_d = din("tri_b", [128, 128], BF16)
    eoff_d = din("eoff", [128, NT * NE])
    out_d = nc.dram_tensor("out", [S, D], F32, kind="ExternalOutput").ap()

    QT_d = dscr("QT_s", [16, 128, S], BF16)
    KT_d = dscr("KT_s", [16, 128, S], BF16)
    V_d = dscr("V_s", [S, D], BF16)
    oT_d = dscr("oT_s", [8, 128, S], BF16)
    x1_d = dscr("x1_s", [S, D], F32)
    x1b_d = dscr("x1b_s", [S, D], BF16)
    xin_d = dscr("xin_s", [NE * CAP, D], BF16)
    yo_d = dscr("yo_s", [NE * CAP, D], BF16)
    aff_d = dscr("aff_s", [128, NT * NE], F32)
    G_d = dscr("G_s", [S, 2 * D], BF16)

    with es:
        kb = KB(nc, es)
        pe, act, dve, pool, sp = kb.pe, kb.act, kb.dve, kb.pool, kb.sp

        def sbuf(st, name, shape, dt):
            return Buf(st.enter_context(nc.sbuf_tensor("sb_" + name, list(shape), dt)), name)

        def psum(st, name, dt=F32):
            shape = [128, 512] if dt == F32 else [128, 1024]
            return Buf(st.enter_context(nc.psum_tensor("ps_" + name, shape, dt)), name)

        def mm(out_b, out_ap, l_b, l_ap, r_b, r_ap, start=True, stop=True):
            kb.op(pe, [l_b.res, r_b.res], [out_b.res],
                  lambda: nc.tensor.matmul(out_ap, l_ap, r_ap, start=start, stop=stop))

        def tr(out_b, out_ap, in_b, in_ap, id_b):
            kb.op(pe, [in_b.res, id_b.res], [out_b.res],
                  lambda: nc.tensor.transpose(out_ap, in_ap, id_b[:]))

        def ex(q, reads, writes, fn):
            kb.op(q, [b.res for b in reads], [b.res for b in writes], fn)

        bc_reg = nc.gpsimd.alloc_register("bc_reg")
        nc.gpsimd.reg_mov(bc_reg, NE * CAP - 1)
        dram = {n: Res(n) for n in ["xT", "QT", "KT", "V", "oT", "x1", "x1b", "xin", "yo", "G"]}

        gst = es
        ident_f = sbuf(gst, "ident_f", [128, 128], F32)
        ident_b = sbuf(gst, "ident_b", [128, 128], BF16)
        ones_b = sbuf(gst, "ones_b", [128, 128], BF16)
        ones_f = sbuf(gst, "ones_f", [128, 128], F32)
        lam4 = sbuf(gst, "lam4", [128, 256], F32)
        nw = sbuf(gst, "nw", [128, 1], F32)
        neglam = sbuf(gst, "neglam", [128, 1], F32)
        ltmp = sbuf(gst, "ltmp", [128, 128], F32)
        lsum = sbuf(gst, "lsum", [128, 2], F32)
        aff = sbuf(gst, "aff", [128, NT * NE], F32)
        idx_i = sbuf(gst, "idx_i", [128, NT * NE], I32)
        gm = sbuf(gst, "gm", [128, NT * NE], F32)
        kb.load(sp, ident_f.res, ident_f[:], idf_d[:, :])
        kb.load(sp, ident_b.res, ident_b[:], idb_d[:, :])
        kb.load(sp, lam4.res, lam4[:], lam_d[:, :])
        kb.load(sp, nw.res, nw[:], nw_d[:, :])
        ex(pool, [], [ones_b], lambda: nc.gpsimd.memset(ones_b[:], 1.0))
        ex(pool, [], [ones_f], lambda: nc.gpsimd.memset(ones_f[:], 1.0))
        ex(dve, [lam4], [ltmp], lambda: nc.vector.tensor_tensor(
            out=ltmp[:].rearrange("p (a c) -> p a c", a=2),
            in0=lam4[:].rearrange("p (a b c) -> p a b c", a=2, b=2)[:, :, 0, :],
            in1=lam4[:].rearrange("p (a b c) -> p a b c", a=2, b=2)[:, :, 1, :], op=ALU.mult))
        ex(dve, [ltmp], [lsum], lambda: nc.vector.reduce_sum(
            out=lsum[:], in_=ltmp[:].rearrange("p (a c) -> p a c", a=2), axis=AX.X))
        ex(act, [lsum], [lsum], lambda: nc.scalar.activation(out=lsum[:], in_=lsum[:], func=AF.Exp))
        ex(dve, [lsum], [neglam], lambda: nc.vector.tensor_tensor(
            out=neglam[:], in0=lsum[:, 1:2], in1=lsum[:, 0:1], op=ALU.subtract))
        ex(dve, [neglam], [neglam], lambda: nc.vector.tensor_scalar_add(
            out=neglam[:], in0=neglam[:], scalar1=-0.2))
        ex(dve, [nw], [nw], lambda: nc.vector.tensor_scalar_mul(out=nw[:], in0=nw[:], scalar1=0.8))

        with contextlib.ExitStack() as st:
            xT = sbuf(st, "A_xT", [128, 16, 2048], BF16)
            tabs = [sbuf(st, "A_tab%d" % i, [128, 16, 128], F32) for i in range(4)]
            xt = [sbuf(st, "A_xt%d" % i, [128, D], F32) for i in range(2)]
            wp = [sbuf(st, "A_wp%d" % i, [128, 16, 512], BF16) for i in range(2)]
            zt = [sbuf(st, "A_zt%d" % i, [128, 512], F32) for i in range(2)]
            uu = [sbuf(st, "A_uu%d" % i, [128, 128], F32) for i in range(2)]
            vv = [sbuf(st, "A_vv%d" % i, [128, 128], F32) for i in range(2)]
            zb = [sbuf(st, "A_zb%d" % i, [128, 512], BF16) for i in range(4)]
            qst = [sbuf(st, "A_qst%d" % i, [128, 4, 2048], BF16) for i in range(2)]
            pmm = [psum(st, "A_pm%d" % i) for i in range(4)]
            ptr = [psum(st, "A_pt%d" % i, BF16) for i in range(2)]
            tab_src = [ccA_d, ssA_d, ccB_d, ssB_d]
            nxt = 0
            for hf in range(2):
                t0 = hf * 2048
                for i in range(4):
                    kb.load(sp, tabs[i].res, tabs[i][:],
                            tab_src[i][t0:t0 + 2048, :].rearrange("(t p) c -> p t c", p=128))
                for tl in range(16):
                    xs = xt[tl % 2]
                    r0 = t0 + tl * 128
                    kb.load(sp, xs.res, xs[:], x_d[r0:r0 + 128, :])
                    for kk in range(4):
                        pb = pmm[kk]
                        for j in range(4):
                            k = kk * 4 + j
                            tr(pb, pb[:, j * 128:(j + 1) * 128], xs, xs[:, k * 128:(k + 1) * 128], ident_f)
                        dst = xT[:, kk * 4:(kk + 1) * 4, tl * 128:(tl + 1) * 128]
                        src = pb[:].rearrange("p (a b) -> p a b", a=4)
                        if kk % 2 == 0:
                            ex(act, [pb], [xT], lambda dst=dst, src=src: nc.scalar.copy(out=dst, in_=src))
                        else:
                            ex(dve, [pb], [xT], lambda dst=dst, src=src: nc.vector.tensor_copy(dst, src))
                NCB = 20

                def col_of(cb):
                    return cb * 512 if cb < 12 else 6144 + (cb - 12) * 512

                wsrcs = [win_d[:, col_of(cb):col_of(cb) + 512].rearrange("(k p) c -> p k c", p=128)
                         for cb in range(NCB)]
                wstream = Stream(kb, pool, wp, wsrcs)
                for cb in range(NCB):
                    w = wstream.get(cb)
                    is_v = cb in (6, 7, 8, 11)
                    is_g = cb >= 12
                    is_b = cb in (9, 10)
                    qs = qst[cb % 2]
                    pend = []

                    def flush_one():
                        z_b, tl_ = pend.pop(0)
                        pt = ptr[tl_ % 2]
                        for h in range(4):
                            tr(pt, pt[:, h * 128:(h + 1) * 128], z_b, z_b[:, h * 128:(h + 1) * 128], ident_b)
                        dst = qs[:, :, tl_ * 128:(tl_ + 1) * 128]
                        src = pt[:, 0:512].rearrange("p (a b) -> p a b", a=4)
                        ex(dve, [pt], [qs], lambda: nc.vector.tensor_copy(dst, src))

                    for tl in range(16):
                        pb = pmm[nxt % 4]
                        nxt += 1
                        for k in range(16):
                            mm(pb, pb[:], xT, xT[:, k, tl * 128:(tl + 1) * 128], w, w[:, k, :],
                               start=(k == 0), stop=(k == 15))
                        z_b = zb[tl % 4]
                        r0 = t0 + tl * 128
                        if is_v:
                            ex(act, [pb], [z_b], lambda: nc.scalar.copy(out=z_b[:], in_=pb[:]))
                            c0 = (cb - 6) * 512 if cb < 9 else 1536
                            kb.store(sp, z_b.res, V_d[r0:r0 + 128, c0:c0 + 512], z_b[:], [dram["V"]])
                            continue
                        if is_g:
                            ex(act, [pb], [z_b], lambda: nc.scalar.activation(out=z_b[:], in_=pb[:], func=AF.Sigmoid))
                            c0 = (cb - 12) * 512
                            kb.store(sp, z_b.res, G_d[r0:r0 + 128, c0:c0 + 512], z_b[:], [dram["G"]])
                            continue
                        z_t = zt[tl % 2]
                        u_t = uu[tl % 2]
                        v_t = vv[tl % 2]
                        ex(act, [pb], [z_t], lambda: nc.scalar.copy(out=z_t[:], in_=pb[:]))
                        if not is_b:
                            nb, bw, hh_ = 4, 128, 16
                            cc, ss = tabs[0], tabs[1]
                        else:
                            nb, bw, hh_ = 8, 64, 8
                            cc, ss = tabs[2], tabs[3]
                        z3 = z_t[:].rearrange("p (a b) -> p a b", a=nb)
                        zb3 = z_b[:].rearrange("p (a b) -> p a b", a=nb)
                        u3 = u_t[:].rearrange("p (a b) -> p a b", a=nb)
                        v3 = v_t[:].rearrange("p (a b) -> p a b", a=nb)
                        c3 = cc[:, tl, :].rearrange("p (a b) -> p a b", a=nb)
                        s3 = ss[:, tl, :].rearrange("p (a b) -> p a b", a=nb)
                        rw = 2 * hh_
                        ex(dve, [z_t, cc], [u_t], lambda: nc.vector.tensor_tensor(
                            out=u3, in0=z3[:, :, 0:rw], in1=c3, op=ALU.mult))
                        ex(dve, [z_t, ss], [v_t], lambda: nc.vector.tensor_tensor(
                            out=v3, in0=z3[:, :, 0:rw], in1=s3, op=ALU.mult))
                        ex(pool, [u_t, v_t], [z_b], lambda: nc.gpsimd.tensor_tensor(
                            out=zb3[:, :, 0:hh_], in0=u3[:, :, 0:hh_], in1=v3[:, :, hh_:rw], op=ALU.add))
                        ex(pool, [u_t, v_t], [z_b], lambda: nc.gpsimd.tensor_tensor(
                            out=zb3[:, :, hh_:rw], in0=u3[:, :, hh_:rw], in1=v3[:, :, 0:hh_], op=ALU.add))
                        ex(act, [z_t], [z_b], lambda: nc.scalar.copy(out=zb3[:, :, rw:bw], in_=z3[:, :, rw:bw]))
                        pend.append((z_b, tl))
                        if len(pend) > 2:
                            flush_one()
                    while pend:
                        flush_one()
                    if not (is_v or is_g):
                        if cb < 3:
                            tgt, key, h0 = QT_d, "QT", cb * 4
                        elif cb < 6:
                            tgt, key, h0 = KT_d, "KT", (cb - 3) * 4
                        elif cb == 9:
                            tgt, key, h0 = QT_d, "QT", 12
                        else:
                            tgt, key, h0 = KT_d, "KT", 12
                        for h in range(4):
                            kb.store(sp, qs.res, tgt[h0 + h, :, t0:t0 + 2048], qs[:, h, :], [dram[key]])
        kb.barrier()

        with contextlib.ExitStack() as st:
            msk = sbuf(st, "B_msk", [128, 384], BF16)
            kb.load(sp, msk.res, msk[:], msk_d[:, :])
            qT = [sbuf(st, "B_q%d" % i, [128, S], BF16) for i in range(2)]
            kT = [sbuf(st, "B_k%d" % i, [128, S], BF16) for i in range(2)]
            vs = [sbuf(st, "B_v%d" % i, [128, 32, 128], BF16) for i in range(2)]
            accn = sbuf(st, "B_accn", [128, S], F32)
            accd = sbuf(st, "B_accd", [128, S], F32)
            ob = sbuf(st, "B_ob", [128, S], BF16)
            pT = [sbuf(st, "B_pT%d" % i, [128, 384], BF16) for i in range(3)]
            ps_s = [psum(st, "B_ps%d" % i) for i in range(3)]
            ps_n = [psum(st, "B_pn%d" % i) for i in range(2)]
            ps_d = [psum(st, "B_pd%d" % i) for i in range(2)]
            it = 0
            blk = 0
            for hh in range(4):
                for g, r in enumerate((1, 4, 16)):
                    hd = g * 4 + hh
                    L = S // r
                    nj = L // 128
                    q_, k_, v_ = qT[it % 2], kT[it % 2], vs[it % 2]
                    it += 1
                    kb.load(sp, q_.res, q_[:], QT_d[hd, :, :], [dram["QT"]])
                    kb.load(sp, k_.res, k_[:], KT_d[hd, :, :], [dram["KT"]])
                    for ph in range(r):
                        kb.load(sp, v_.res, v_[:, ph * nj:(ph + 1) * nj, :],
                                V_d[ph:S:r, hd * 128:(hd + 1) * 128].rearrange("(j a) c -> a j c", a=128),
                                [dram["V"]])
                    nblk = r * nj

                    def qk_blk(bi_, slot):
                        ph_, j_ = bi_ // nj, bi_ % nj
                        pss_ = ps_s[slot % 3]
                        qsl = q_[:, ph_ + r * 128 * j_: ph_ + r * 128 * j_ + r * 127 + 1: r]
                        for kt in (j_ - 1, j_, j_ + 1):
                            if 0 <= kt < nj:
                                off = (kt - j_ + 1) * 128
                                ksl = k_[:, ph_ + r * 128 * kt: ph_ + r * 128 * kt + r * 127 + 1: r]
                                mm(pss_, pss_[:, off:off + 128], k_, ksl, q_, qsl)

                    qk_blk(0, blk)
                    for b0 in range(0, nblk, 4):
                        pn = ps_n[(b0 // 4) % 2]
                        pd = ps_d[(b0 // 4) % 2]
                        for bi in range(b0, b0 + 4):
                            ph, j = bi // nj, bi % nj
                            kts = [kt for kt in (j - 1, j, j + 1) if 0 <= kt < nj]
                            lo = (kts[0] - j + 1) * 128
                            hi = (kts[-1] - j + 2) * 128
                            pss = ps_s[blk % 3]
                            p_t = pT[blk % 3]
                            blk += 1
                            if bi + 1 < nblk:
                                qk_blk(bi + 1, blk)
                            ex(act, [pss], [p_t], lambda: nc.scalar.activation(
                                out=p_t[:, lo:hi], in_=pss[:, lo:hi], func=AF.Exp, scale=128.0 ** -0.5))
                            ex(dve, [p_t, msk], [p_t], lambda: nc.vector.tensor_tensor(
                                out=p_t[:, lo:hi], in0=p_t[:, lo:hi], in1=msk[:, lo:hi], op=ALU.mult))
                            c0 = (bi - b0) * 128
                            for n_, kt in enumerate(kts):
                                off = (kt - j + 1) * 128
                                mm(pn, pn[:, c0:c0 + 128], v_, v_[:, ph * nj + kt, :], p_t, p_t[:, off:off + 128],
                                   start=(n_ == 0), stop=(n_ == len(kts) - 1))
                            for n_, kt in enumerate(kts):
                                off = (kt - j + 1) * 128
                                mm(pd, pd[:, c0:c0 + 128], ones_b, ones_b[:], p_t, p_t[:, off:off + 128],
                                   start=(n_ == 0), stop=(n_ == len(kts) - 1))
                        ph0, j0 = b0 // nj, b0 % nj
                        outs = []
                        for acc in (accn, accd):
                            av = acc[:].rearrange("p (m r) -> p r m", r=r)
                            if r <= 4:
                                outs.append((av[:, ph0, j0 * 128:j0 * 128 + 512], None))
                            else:
                                outs.append((av[:, ph0:ph0 + 2, :], 2))
                        for (dst, f), pb_, acc, q in ((outs[0], pn, accn, act), (outs[1], pd, accd, dve)):
                            src = pb_[:] if f is None else pb_[:].rearrange("p (f m) -> p f m", f=f)
                            if g == 0:
                                if q is act:
                                    ex(act, [pb_], [acc], lambda dst=dst, src=src: nc.scalar.copy(out=dst, in_=src))
                                else:
                                    ex(dve, [pb_], [acc], lambda dst=dst, src=src: nc.vector.tensor_copy(dst, src))
                            else:
                                ex(dve, [pb_, acc], [acc], lambda dst=dst, src=src: nc.vector.tensor_tensor(
                                    out=dst, in0=dst, in1=src, op=ALU.add))
                ex(dve, [accd], [accd], lambda: nc.vector.reciprocal(out=accd[:], in_=accd[:]))
                ex(dve, [accn, accd], [ob], lambda: nc.vector.tensor_tensor(
                    out=ob[:], in0=accn[:], in1=accd[:], op=ALU.mult))
                kb.store(sp, ob.res, oT_d[hh, :, :], ob[:], [dram["oT"]])
        kb.barrier()

        with contextlib.ExitStack() as st:
            qT = [sbuf(st, "C_q%d" % i, [128, S], BF16) for i in range(2)]
            qZ = [sbuf(st, "C_qz%d" % i, [128, S], BF16) for i in range(2)]
            kT = [sbuf(st, "C_k%d" % i, [128, S], BF16) for i in range(2)]
            vn = [sbuf(st, "C_v%d" % i, [128, 32, 128], BF16) for i in range(2)]
            ob = sbuf(st, "C_ob", [128, S], BF16)
            pT = [sbuf(st, "C_pT%d" % i, [128, 512], BF16) for i in range(4)]
            sacc = [[sbuf(st, "C_sa%d%d" % (i, c), [128, 512], F32) for c in range(2)] for i in range(2)]
            rr = [sbuf(st, "C_rr%d" % i, [128, 512], F32) for i in range(2)]
            aa = [sbuf(st, "C_aa%d" % i, [128, 512], F32) for i in range(2)]
            A_ = sbuf(st, "C_A", [128, 512], F32)
            sq = sbuf(st, "C_sq", [128, 512], F32)
            rs = sbuf(st, "C_rs", [128, 512], F32)
            ps_s = [psum(st, "C_ps%d" % i) for i in range(3)]
            ps_n = [[psum(st, "C_pn%d%d" % (i, c)) for c in range(2)] for i in range(1)]
            ps_d = psum(st, "C_pd")
            ps_q = psum(st, "C_pq")
            gi = 0
            nqb = 0
            for h in range(4):
                q_, k_, v_ = qT[h % 2], kT[h % 2], vn[h % 2]
                kb.load(sp, q_.res, q_[:], QT_d[12 + h, :, :], [dram["QT"]])
                kb.load(sp, k_.res, k_[:], KT_d[12 + h, :, :], [dram["KT"]])
                kb.load(sp, v_.res, v_[:],
                        V_d[:, 1536 + h * 128:1536 + (h + 1) * 128].rearrange("(t p) c -> p t c", p=128),
                        [dram["V"]])
                qz0, qz1 = qZ[0], qZ[1]
                ex(dve, [q_], [qz0], lambda: nc.vector.tensor_copy(qz0[0:64, :], q_[0:64, :]))
                ex(dve, [], [qz0], lambda: nc.vector.memset(qz0[64:128, :], 0.0))
                ex(act, [q_], [qz1], lambda: nc.scalar.copy(out=qz1[64:128, :], in_=q_[64:128, :]))
                ex(dve, [], [qz1], lambda: nc.vector.memset(qz1[0:64, :], 0.0))
                for qb in range(8):
                    pn = ps_n[0]
                    sa = sacc[nqb % 2]
                    nqb += 1
                    steps = [(kt, c) for kt in range(32) for c in range(2)]

                    def qk(i):
                        kt, c = steps[i]
                        pss = ps_s[(gi + i) % 3]
                        mm(pss, pss[:], k_, k_[:, kt * 128:(kt + 1) * 128],
                           qZ[c], qZ[c][:, qb * 512:(qb + 1) * 512])

                    qk(0)
                    for i, (kt, c) in enumerate(steps):
                        if i + 1 < len(steps):
                            qk(i + 1)
                        pss = ps_s[(gi + i) % 3]
                        p_t = pT[(gi + i) % 4]
                        ex(act, [pss], [p_t], lambda: nc.scalar.activation(
                            out=p_t[:], in_=pss[:], func=AF.Exp, scale=0.125))
                        mm(pn[c], pn[c][:], v_, v_[:, kt, :], p_t, p_t[:], start=(kt == 0), stop=(kt == 31))
                        if True:
                            if kt == 0:
                                ex(dve, [p_t], [sa[c]], lambda: nc.vector.tensor_copy(sa[c][:], p_t[:]))
                            else:
                                ex(dve, [p_t, sa[c]], [sa[c]], lambda: nc.vector.tensor_tensor(
                                    out=sa[c][:], in0=sa[c][:], in1=p_t[:], op=ALU.add))
                    gi += len(steps)
                    for c in range(2):
                        mm(ps_q, ps_q[:], ones_f, ones_f[:], sa[c], sa[c][:])
                        pden = ps_q
                        ex(dve, [pden], [rr[c]], lambda: nc.vector.reciprocal(out=rr[c][:], in_=pden[:]))
                        ex(dve, [pn[c], rr[c]], [aa[c]], lambda: nc.vector.tensor_tensor(
                            out=aa[c][:], in0=pn[c][:], in1=rr[c][:], op=ALU.mult))
                    ex(dve, [aa[0], aa[1], neglam], [A_], lambda: nc.vector.scalar_tensor_tensor(
                        out=A_[:], in0=aa[1][:], scalar=neglam[:, 0:1], in1=aa[0][:], op0=ALU.mult, op1=ALU.add))
                    ex(dve, [A_], [sq], lambda: nc.vector.tensor_tensor(out=sq[:], in0=A_[:], in1=A_[:], op=ALU.mult))
                    mm(ps_q, ps_q[:], ones_f, ones_f[:], sq, sq[:])
                    ex(dve, [ps_q], [rs], lambda: nc.vector.tensor_scalar(
                        out=rs[:], in0=ps_q[:], scalar1=1.0 / 128.0, scalar2=1e-5, op0=ALU.mult, op1=ALU.add))
                    ex(act, [rs], [rs], lambda: nc.scalar.activation(out=rs[:], in_=rs[:], func=AF.Ln))
                    ex(act, [rs], [rs], lambda: nc.scalar.activation(out=rs[:], in_=rs[:], func=AF.Exp, scale=-0.5))
                    ex(dve, [A_, rs], [A_], lambda: nc.vector.tensor_tensor(out=A_[:], in0=A_[:], in1=rs[:], op=ALU.mult))
                    ex(dve, [A_, nw], [ob], lambda: nc.vector.tensor_scalar(
                        out=ob[:, qb * 512:(qb + 1) * 512], in0=A_[:], scalar1=nw[:, 0:1], scalar2=None, op0=ALU.mult))
                kb.store(sp, ob.res, oT_d[4 + h, :, :], ob[:], [dram["oT"]])
        kb.barrier()

        with contextlib.ExitStack() as st:
            lnp = sbuf(st, "D_lnp", [128, 2 * D], F32)
            kb.load(sp, lnp.res, lnp[:], ln_d[:, 0:2 * D])
            wr = sbuf(st, "D_wr", [128, 16, NE], F32)
            kb.load(sp, wr.res, wr[:], wr_d.rearrange("(k p) e -> p k e", p=128))
            wbr = sbuf(st, "D_wbr", [128, 8, D], BF16)
            wo = sbuf(st, "D_wo", [128, 16, D], BF16)
            for g in range(2):
                kb.load(pool, wbr.res, wbr[:, 4 * g:4 * g + 4, :],
                        wbr_d[g * 512:(g + 1) * 512, :].rearrange("(k p) c -> p k c", p=128))
            for i in range(4):
                kb.load(pool, wo.res, wo[:, 4 * i:4 * i + 4, :],
                        wout_d[i * 512:(i + 1) * 512, :].rearrange("(k p) c -> p k c", p=128))
            oTb = [sbuf(st, "D_oTb%d" % i, [128, 8, 512], BF16) for i in range(2)]
            Gt = [sbuf(st, "D_G%d" % i, [128, 2 * D], BF16) for i in range(2)]
            xts = [sbuf(st, "D_xt%d" % i, [128, 1, D], F32) for i in range(3)]
            tA = [sbuf(st, "D_tA%d" % i, [128, 512], F32) for i in range(1)]
            tB = [sbuf(st, "D_tB%d" % i, [128, 512], F32) for i in range(1)]
            mg = sbuf(st, "D_mg", [128, D], BF16)
            mT = sbuf(st, "D_mT", [128, 16, 128], BF16)
            x1b = [sbuf(st, "D_x1b%d" % i, [128, D], BF16) for i in range(1)]
            x1T = sbuf(st, "D_x1T", [128, 16, 128], F32)
            junk = Buf(x1T.t, "junkview")
            junk.res = x1T.res
            junk_ap = x1T[:].rearrange("p a b -> p (a b)")
            st1 = sbuf(st, "D_st1", [128, 8], F32)
            eb = sbuf(st, "D_eb", [128, NE], F32)
            pf = [psum(st, "D_p%d" % i) for i in range(6)]
            pbf = [psum(st, "D_pb%d" % i, BF16) for i in range(2)]
            npf = 0

            def prefetch(tt):
                if tt >= NT:
                    return
                if tt % 4 == 0:
                    ob_ = oTb[(tt // 4) % 2]
                    c0_ = tt * 128
                    kb.load(sp, ob_.res, ob_[:], oT_d[:, :, c0_:c0_ + 512].rearrange("k p t -> p k t"), [dram["oT"]])
                kb.load(sp, Gt[tt % 2].res, Gt[tt % 2][:], G_d[tt * 128:(tt + 1) * 128, :], [dram["G"]])
                kb.load(sp, xts[tt % 3].res, xts[tt % 3][:, 0, :], x_d[tt * 128:(tt + 1) * 128, :])

            def stage1(tt):
                nonlocal npf
                ti = tt % 4
                r0 = tt * 128
                ob_ = oTb[(tt // 4) % 2]
                G_ = Gt[tt % 2]
                xt_ = xts[tt % 3]
                hv = xt_[:, 0, :]
                for db in range(4):
                    pa = pf[npf % 6]
                    pb_ = pf[(npf + 1) % 6]
                    npf += 2
                    for g, pp in ((0, pa), (1, pb_)):
                        for c in range(4):
                            mm(pp, pp[:], ob_, ob_[:, 4 * g + c, ti * 128:(ti + 1) * 128],
                               wbr, wbr[:, 4 * g + c, db * 512:(db + 1) * 512], start=(c == 0), stop=(c == 3))
                    ta, tb_ = tA[0], tB[0]
                    ex(dve, [G_, pa], [ta], lambda: nc.vector.tensor_tensor(
                        out=ta[:], in0=G_[:, db * 512:(db + 1) * 512], in1=pa[:], op=ALU.mult))
                    ex(dve, [G_, pb_], [tb_], lambda: nc.vector.tensor_tensor(
                        out=tb_[:], in0=G_[:, D + db * 512:D + (db + 1) * 512], in1=pb_[:], op=ALU.mult))
                    ex(dve, [ta, tb_], [mg], lambda: nc.vector.tensor_tensor(
                        out=mg[:, db * 512:(db + 1) * 512], in0=ta[:], in1=tb_[:], op=ALU.add))
                for half in range(2):
                    pt = pbf[half]
                    for j in range(8):
                        k = half * 8 + j
                        tr(pt, pt[:, j * 128:(j + 1) * 128], mg, mg[:, k * 128:(k + 1) * 128], ident_b)
                    dst = mT[:, half * 8:(half + 1) * 8, :]
                    src = pt[:].rearrange("p (a b) -> p a b", a=8)
                    if half == 0:
                        ex(act, [pt], [mT], lambda: nc.scalar.copy(out=dst, in_=src))
                    else:
                        ex(dve, [pt], [mT], lambda: nc.vector.tensor_copy(dst, src))
                for obk in range(4):
                    pm = pf[npf % 6]
                    npf += 1
                    for k in range(16):
                        mm(pm, pm[:], mT, mT[:, k, :], wo, wo[:, k, obk * 512:(obk + 1) * 512],
                           start=(k == 0), stop=(k == 15))
                    ex(dve, [xt_, pm], [xt_], lambda: nc.vector.scalar_tensor_tensor(
                        out=xt_[:, 0, obk * 512:(obk + 1) * 512], in0=xt_[:, 0, obk * 512:(obk + 1) * 512],
                        scalar=ALPHA, in1=pm[:], op0=ALU.mult, op1=ALU.add))

            def stage2(tt):
                nonlocal npf
                r0 = tt * 128
                xt_ = xts[tt % 3]
                hv = xt_[:, 0, :]
                layer_norm(nc, kb, ex, hv, xt_, junk, junk_ap, st1, lnp, 0)
                kb.store(sp, xt_.res, x1_d[r0:r0 + 128, :], hv, [dram["x1"]])
                xb_ = x1b[0]
                ex(act, [xt_], [xb_], lambda: nc.scalar.copy(out=xb_[:], in_=hv))
                kb.store(sp, xb_.res, x1b_d[r0:r0 + 128, :], xb_[:], [dram["x1b"]])

            def stage2b(tt):
                nonlocal npf
                xt_ = xts[tt % 3]
                for kk in range(4):
                    pb = pf[npf % 6]
                    npf += 1
                    for j in range(4):
                        k = kk * 4 + j
                        tr(pb, pb[:, j * 128:(j + 1) * 128], xt_, xt_[:, 0, k * 128:(k + 1) * 128], ident_f)
                    dst = x1T[:, kk * 4:(kk + 1) * 4, :]
                    src = pb[:].rearrange("p (a b) -> p a b", a=4)
                    if kk % 2 == 0:
                        ex(act, [pb], [x1T], lambda: nc.scalar.copy(out=dst, in_=src))
                    else:
                        ex(dve, [pb], [x1T], lambda: nc.vector.tensor_copy(dst, src))
                pl = pf[npf % 6]
                npf += 1
                for k in range(16):
                    mm(pl, pl[:, 0:NE], x1T, x1T[:, k, :], wr, wr[:, k, :], start=(k == 0), stop=(k == 15))
                ex(dve, [pl], [st1], lambda: nc.vector.reduce_max(out=st1[:, 4:5], in_=pl[:, 0:NE], axis=AX.X))
                ex(dve, [st1], [st1], lambda: nc.vector.tensor_scalar_mul(out=st1[:, 5:6], in0=st1[:, 4:5], scalar1=-1.0))
                ex(act, [pl, st1], [eb, st1], lambda: nc.scalar.activation(
                    out=eb[:], in_=pl[:, 0:NE], func=AF.Exp, bias=st1[:, 5:6], scale=1.0, accum_out=st1[:, 6:7]))
                ex(dve, [st1], [st1], lambda: nc.vector.reciprocal(out=st1[:, 7:8], in_=st1[:, 6:7]))
                ex(dve, [eb, st1], [aff], lambda: nc.vector.tensor_scalar(
                    out=aff[:, tt * NE:(tt + 1) * NE], in0=eb[:], scalar1=st1[:, 7:8], scalar2=None, op0=ALU.mult))

            prefetch(0)
            prefetch(1)
            stage1(0)
            for tt in range(NT):
                prefetch(tt + 2)
                stage2(tt)
                if tt + 1 < NT:
                    stage1(tt + 1)
                stage2b(tt)
        kb.barrier()

        with contextlib.ExitStack() as st:
            lo = sbuf(st, "E_lo", [128, NE], F32)
            hi = sbuf(st, "E_hi", [128, NE], F32)
            mid = sbuf(st, "E_mid", [128, NE], F32)
            dd = sbuf(st, "E_dd", [128, NE], F32)
            sel = sbuf(st, "E_sel", [128, NE], F32)
            cmp_ = sbuf(st, "E_cmp", [128, NT * NE], F32)
            cnt = sbuf(st, "E_cnt", [128, NE], F32)
            mkb = sbuf(st, "E_mkb", [128, NT * NE], BF16)
            tri = sbuf(st, "E_tri", [128, 128], BF16)
            eoff = sbuf(st, "E_eoff", [128, NT * NE], F32)
            tt_ = sbuf(st, "E_tt", [128, NT * NE], F32)
            cum = sbuf(st, "E_cum", [128, NT * NE], F32)
            pos = sbuf(st, "E_pos", [128, NT * NE], F32)
            val = sbuf(st, "E_val", [128, NT * NE], F32)
            pc = psum(st, "E_pc")
            pw = psum(st, "E_pw")
            ptt = psum(st, "E_pt")
            kb.load(sp, tri.res, tri[:], tri_d[:, :])
            kb.load(sp, eoff.res, eoff[:], eoff_d[:, :])
            ex(dve, [], [lo], lambda: nc.vector.memset(lo[:], 0.0))
            ex(dve, [], [hi], lambda: nc.vector.memset(hi[:], 1.0))
            aff3 = aff[:].rearrange("p (t e) -> p t e", e=NE)

            def bc(b):
                a = b[:]
                return bass.AP(a.tensor, a.offset, [list(a.ap[0]), [0, NT], list(a.ap[1])])

            for _ in range(34):
                ex(dve, [lo, hi], [mid], lambda: nc.vector.tensor_tensor(out=mid[:], in0=lo[:], in1=hi[:], op=ALU.add))
                ex(dve, [mid], [mid], lambda: nc.vector.tensor_scalar_mul(out=mid[:], in0=mid[:], scalar1=0.5))
                ex(dve, [aff, mid], [cmp_], lambda: nc.vector.tensor_tensor(
                    out=cmp_[:].rearrange("p (t e) -> p t e", e=NE), in0=aff3, in1=bc(mid), op=ALU.is_gt))
                ex(dve, [cmp_], [cnt], lambda: nc.vector.reduce_sum(
                    out=cnt[:], in_=cmp_[:].rearrange("p (t e) -> p e t", e=NE), axis=AX.X))
                mm(pc, pc[:, 0:NE], ones_f, ones_f[:], cnt, cnt[:])
                ex(dve, [pc], [sel], lambda: nc.vector.tensor_scalar(
                    out=sel[:], in0=pc[:, 0:NE], scalar1=float(CAP) - 0.5, scalar2=None, op0=ALU.is_gt))
                ex(dve, [mid, lo], [dd], lambda: nc.vector.tensor_tensor(out=dd[:], in0=mid[:], in1=lo[:], op=ALU.subtract))
                ex(dve, [dd, sel], [dd], lambda: nc.vector.tensor_tensor(out=dd[:], in0=dd[:], in1=sel[:], op=ALU.mult))
                ex(dve, [dd, lo], [lo], lambda: nc.vector.tensor_tensor(out=lo[:], in0=lo[:], in1=dd[:], op=ALU.add))
                ex(dve, [mid, hi], [dd], lambda: nc.vector.tensor_tensor(out=dd[:], in0=hi[:], in1=mid[:], op=ALU.subtract))
                ex(dve, [dd, sel], [dd], lambda: nc.vector.tensor_tensor(out=dd[:], in0=dd[:], in1=sel[:], op=ALU.mult))
                ex(dve, [dd, mid], [hi], lambda: nc.vector.tensor_tensor(out=hi[:], in0=mid[:], in1=dd[:], op=ALU.add))
            ex(dve, [aff, lo], [cmp_], lambda: nc.vector.tensor_tensor(
                out=cmp_[:].rearrange("p (t e) -> p t e", e=NE), in0=aff3, in1=bc(lo), op=ALU.is_gt))
            ex(act, [cmp_], [mkb], lambda: nc.scalar.copy(out=mkb[:], in_=cmp_[:]))
            mm(pw, pw[:], tri, tri[:], mkb, mkb[:])
            mm(ptt, ptt[:], ones_b, ones_b[:], mkb, mkb[:])
            ex(act, [ptt], [tt_], lambda: nc.scalar.copy(out=tt_[:], in_=ptt[:]))
            ex(dve, [], [cum], lambda: nc.vector.memset(cum[:, 0:NE], 0.0))
            for t in range(1, NT):
                ex(dve, [cum, tt_], [cum], lambda t=t: nc.vector.tensor_tensor(
                    out=cum[:, t * NE:(t + 1) * NE], in0=cum[:, (t - 1) * NE:t * NE],
                    in1=tt_[:, (t - 1) * NE:t * NE], op=ALU.add))
            ex(dve, [pw, cum], [pos], lambda: nc.vector.tensor_tensor(out=pos[:], in0=cum[:], in1=pw[:], op=ALU.add))
            ex(dve, [pos], [val], lambda: nc.vector.tensor_scalar(
                out=val[:], in0=pos[:], scalar1=float(CAP) - 0.5, scalar2=None, op0=ALU.is_lt))
            ex(dve, [val, cmp_], [val], lambda: nc.vector.tensor_tensor(out=val[:], in0=val[:], in1=cmp_[:], op=ALU.mult))
            ex(dve, [aff, val], [gm], lambda: nc.vector.tensor_tensor(out=gm[:], in0=aff[:], in1=val[:], op=ALU.mult))
            ex(dve, [pos, eoff], [pos], lambda: nc.vector.tensor_tensor(out=pos[:], in0=pos[:], in1=eoff[:], op=ALU.add))
            ex(dve, [pos], [pos], lambda: nc.vector.tensor_scalar_add(out=pos[:], in0=pos[:], scalar1=-BIG))
            ex(dve, [pos, val], [pos], lambda: nc.vector.tensor_tensor(out=pos[:], in0=pos[:], in1=val[:], op=ALU.mult))
            ex(dve, [pos], [pos], lambda: nc.vector.tensor_scalar_add(out=pos[:], in0=pos[:], scalar1=BIG))
            ex(dve, [pos], [idx_i], lambda: nc.vector.tensor_copy(idx_i[:], pos[:]))
            if DEBUG:
                kb.store(sp, aff.res, aff_d[:, :], aff[:])
        kb.barrier()

        with contextlib.ExitStack() as st:
            xb = [sbuf(st, "F_xb%d" % i, [128, D], BF16) for i in range(3)]
            for tt in range(NT):
                b = xb[tt % 3]
                kb.load(sp, b.res, b[:], x1b_d[tt * 128:(tt + 1) * 128, :], [dram["x1b"]])
                for e in range(NE):
                    col = tt * NE + e
                    if b.res.dout is None:
                        b.res.dout = kb.dsem()
                    kb.dma(pool, None, None, [b.res, idx_i.res], [], b.res.dout,
                           fn=lambda b=b, col=col: nc.gpsimd.indirect_dma_start(
                               out=xin_d[:, :], out_offset=bass.IndirectOffsetOnAxis(ap=idx_i[:, col:col + 1], axis=0),
                               in_=b[:], in_offset=None, bounds_check=bc_reg, oob_is_err=False))
        kb.barrier()

        with contextlib.ExitStack() as st:
            xi = [sbuf(st, "G_xi%d" % i, [128, 4, D], BF16) for i in range(2)]
            xiT = [sbuf(st, "G_xiT%d" % i, [128, 16, 512], BF16) for i in range(2)]
            hT = sbuf(st, "G_hT", [128, 16, 512], BF16)
            wpc = [sbuf(st, "G_wp%d" % i, [128, 16, 512], BF16) for i in range(5)]
            sgl = [sbuf(st, "G_sg%d" % i, [128, 512], F32) for i in range(2)]
            ost = [sbuf(st, "G_os%d" % i, [128, 512], BF16) for i in range(3)]
            ptr = [psum(st, "G_pt%d" % i, BF16) for i in range(2)]
            pgu = [psum(st, "G_pg%d" % i) for i in range(4)]
            pdn = [psum(st, "G_pd%d" % i) for i in range(2)]
            nos = 0
            npd = 0

            def piece(src_d, e, cblk):
                return src_d[e * D:(e + 1) * D, cblk * 512:(cblk + 1) * 512].rearrange("(k p) c -> p k c", p=128)

            wsrcs = []
            for e in range(NE):
                for fb in range(4):
                    wsrcs.append(piece(wg_d, e, fb))
                    wsrcs.append(piece(wu_d, e, fb))
                for db in range(4):
                    wsrcs.append(piece(wd_d, e, db))
            wstream = Stream(kb, pool, wpc, wsrcs, hold=2)

            def load_x(e):
                x_ = xi[e % 2]
                kb.load(sp, x_.res, x_[:], xin_d[e * CAP:(e + 1) * CAP, :].rearrange("(s p) d -> p s d", p=128))

            def transposes(e):
                x_ = xi[e % 2]
                xt_ = xiT[e % 2]
                for kp in range(8):
                    pt = ptr[kp % 2]
                    for k2 in range(2):
                        k = kp * 2 + k2
                        for s_ in range(4):
                            tr(pt, pt[:, (k2 * 4 + s_) * 128:(k2 * 4 + s_ + 1) * 128], x_,
                               x_[:, s_, k * 128:(k + 1) * 128], ident_b)
                    dst = xt_[:, kp * 2:kp * 2 + 2, :]
                    src = pt[:].rearrange("p (a b) -> p a b", a=2)
                    if kp % 2 == 0:
                        ex(act, [pt], [xt_], lambda: nc.scalar.copy(out=dst, in_=src))
                    else:
                        ex(dve, [pt], [xt_], lambda: nc.vector.tensor_copy(dst, src))

            load_x(0)
            wstream.get(0)
            transposes(0)
            for e in range(NE):
                if e + 1 < NE:
                    load_x(e + 1)
                xt_ = xiT[e % 2]
                for fb in range(4):
                    wg = wstream.get(e * 12 + fb * 2)
                    wu = wstream.get(e * 12 + fb * 2 + 1)
                    for fi in range(4):
                        fc = fb * 4 + fi
                        pg = pgu[2 * (fc % 2)]
                        pu = pgu[2 * (fc % 2) + 1]
                        for k in range(16):
                            mm(pg, pg[:], wg, wg[:, k, fi * 128:(fi + 1) * 128], xt_, xt_[:, k, :],
                               start=(k == 0), stop=(k == 15))
                        for k in range(16):
                            mm(pu, pu[:], wu, wu[:, k, fi * 128:(fi + 1) * 128], xt_, xt_[:, k, :],
                               start=(k == 0), stop=(k == 15))
                        s_g = sgl[fc % 2]
                        ex(act, [pg], [s_g], lambda: nc.scalar.activation(out=s_g[:], in_=pg[:], func=AF.Silu))
                        ex(dve, [s_g, pu], [hT], lambda: nc.vector.tensor_tensor(
                            out=hT[:, fc, :], in0=s_g[:], in1=pu[:], op=ALU.mult))
                if e + 1 < NE:
                    transposes(e + 1)
                for db in range(4):
                    wd = wstream.get(e * 12 + 8 + db)
                    for s_ in range(4):
                        pd = pdn[npd % 2]
                        npd += 1
                        for fc in range(16):
                            mm(pd, pd[:], hT, hT[:, fc, s_ * 128:(s_ + 1) * 128], wd, wd[:, fc, :],
                               start=(fc == 0), stop=(fc == 15))
                        o_ = ost[nos % 3]
                        nos += 1
                        if s_ % 2 == 0:
                            ex(act, [pd], [o_], lambda: nc.scalar.copy(out=o_[:], in_=pd[:]))
                        else:
                            ex(dve, [pd], [o_], lambda: nc.vector.tensor_copy(o_[:], pd[:]))
                        r0 = e * CAP + s_ * 128
                        kb.store(sp, o_.res, yo_d[r0:r0 + 128, db * 512:(db + 1) * 512], o_[:], [dram["yo"]])
        kb.barrier()

        with contextlib.ExitStack() as st:
            lnp = sbuf(st, "H_lnp", [128, 2 * D], F32)
            kb.load(sp, lnp.res, lnp[:], ln_d[:, 2 * D:4 * D])
            NG = 32
            xa = [sbuf(st, "H_xa%d" % i, [128, 1, D], F32) for i in range(3)]
            gb = [sbuf(st, "H_g%d" % i, [128, D], BF16) for i in range(NG)]
            dg = [sbuf(st, "H_dg%d" % i, [128, 128], BF16) for i in range(4)]
            junk = sbuf(st, "H_junk", [128, D], F32)
            st1 = sbuf(st, "H_st1", [128, 8], F32)
            pacc = [[psum(st, "H_p%d%d" % (i, c)) for c in range(4)] for i in range(2)]
            gset = [Res("gset0"), Res("gset1")]
            for i_, g_ in enumerate(gb):
                g_.res = gset[i_ // NE]
                ex(dve, [], [g_], lambda: nc.vector.memset(g_[:], 0.0))

            def gathers(tt):
                rs = gset[tt % 2]
                if rs.din is None:
                    rs.din = kb.dsem()
                lst = []
                fns = []
                for e in range(NE):
                    col = tt * NE + e
                    g_ = gb[(tt % 2) * NE + e]
                    g_.res = rs
                    fns.append(lambda g_=g_, col=col: nc.gpsimd.indirect_dma_start(
                        out=g_[:], out_offset=None, in_=yo_d[:, :],
                        in_offset=bass.IndirectOffsetOnAxis(ap=idx_i[:, col:col + 1], axis=0),
                        bounds_check=bc_reg, oob_is_err=False))
                    lst.append(g_)
                kb.dma_group(pool, fns, [idx_i.res, dram["yo"]], [rs], rs.din)
                return lst

            nd = 0
            glists = {0: gathers(0)}

            def stageA(tt):
                nonlocal nd
                a_ = xa[tt % 3]
                r0 = tt * 128
                kb.load(sp, a_.res, a_[:, 0, :], x1_d[r0:r0 + 128, :], [dram["x1"]])
                glist = glists.pop(tt)
                if tt + 1 < NT:
                    glists[tt + 1] = gathers(tt + 1)
                pa = pacc[tt % 2]
                for e in range(NE):
                    col = tt * NE + e
                    d_ = dg[nd % 4]
                    nd += 1
                    ex(dve, [ident_b, gm], [d_], lambda: nc.vector.tensor_scalar(
                        out=d_[:], in0=ident_b[:], scalar1=gm[:, col:col + 1], scalar2=None, op0=ALU.mult))
                    g_ = glist[e]
                    for cb in range(4):
                        mm(pa[cb], pa[cb][:], d_, d_[:], g_, g_[:, cb * 512:(cb + 1) * 512],
                           start=(e == 0), stop=(e == NE - 1))

            def stageA2(tt):
                a_ = xa[tt % 3]
                pa = pacc[tt % 2]
                for cb in range(4):
                    ex(dve, [a_, pa[cb]], [a_], lambda: nc.vector.scalar_tensor_tensor(
                        out=a_[:, 0, cb * 512:(cb + 1) * 512], in0=a_[:, 0, cb * 512:(cb + 1) * 512],
                        scalar=ALPHA, in1=pa[cb][:], op0=ALU.mult, op1=ALU.add))

            def stageB(tt):
                a_ = xa[tt % 3]
                r0 = tt * 128
                layer_norm(nc, kb, ex, a_[:, 0, :], a_, junk, junk[:], st1, lnp, 0)
                kb.store(sp, a_.res, out_d[r0:r0 + 128, :], a_[:, 0, :])

            stageA(0)
            stageA2(0)
            for tt in range(NT):
                if tt + 1 < NT:
                    stageA(tt + 1)
                stageB(tt)
                if tt + 1 < NT:
                    stageA2(tt + 1)
        kb.barrier()
        nc._marks = kb.marks
    return nc


def layer_norm(nc, kb, ex, hv, hb, junk, junk_ap, st1, lnp, off):
    dve, act, pool = kb.dve, kb.act, kb.pool
    ex(dve, [hb], [st1], lambda: nc.vector.reduce_sum(out=st1[:, 0:1], in_=hv, axis=AX.X))
    ex(dve, [st1], [st1], lambda: nc.vector.tensor_scalar_mul(out=st1[:, 1:2], in0=st1[:, 0:1], scalar1=-1.0 / D))
    ex(dve, [hb, st1], [hb], lambda: nc.vector.tensor_scalar(
        out=hv, in0=hv, scalar1=st1[:, 1:2], scalar2=None, op0=ALU.add))
    ex(act, [hb], [junk, st1], lambda: nc.scalar.activation(
        out=junk_ap, in_=hv, func=AF.Square, accum_out=st1[:, 2:3]))
    ex(dve, [st1], [st1], lambda: nc.vector.tensor_scalar(
        out=st1[:, 3:4], in0=st1[:, 2:3], scalar1=1.0 / D, scalar2=LN_EPS, op0=ALU.mult, op1=ALU.add))
    ex(act, [st1], [st1], lambda: nc.scalar.activation(out=st1[:, 3:4], in_=st1[:, 3:4], func=AF.Ln))
    ex(act, [st1], [st1], lambda: nc.scalar.activation(out=st1[:, 3:4], in_=st1[:, 3:4], func=AF.Exp, scale=-0.5))
    ex(dve, [hb, st1, lnp], [hb], lambda: nc.vector.scalar_tensor_tensor(
        out=hv, in0=hv, scalar=st1[:, 3:4], in1=lnp[:, off:off + D], op0=ALU.mult, op1=ALU.mult))
    ex(dve, [hb, lnp], [hb], lambda: nc.vector.tensor_tensor(
        out=hv, in0=hv, in1=lnp[:, off + D:off + 2 * D], op=ALU.add))


DEBUG_OUT = ()


def _consts():
    import ml_dtypes
    bf = ml_dtypes.bfloat16
    pos = np.arange(S, dtype=np.float32)
    c = {}
    for nm, half, rot, nblk in (("A", 16, 32, 4), ("B", 8, 16, 8)):
        inv = (np.float32(500000.0) ** (-2.0 * np.arange(half, dtype=np.float32) / rot)).astype(np.float32)
        ang = (pos[:, None] * inv[None, :]).astype(np.float32)
        cs, sn = np.cos(ang).astype(np.float32), np.sin(ang).astype(np.float32)
        cc = np.concatenate([cs, cs], axis=1)
        ss = np.concatenate([sn, -sn], axis=1)
        c["cc" + nm] = np.ascontiguousarray(np.tile(cc, (1, nblk)))
        c["ss" + nm] = np.ascontiguousarray(np.tile(ss, (1, nblk)))
    a = np.arange(128)[:, None]
    b = np.arange(128)[None, :]
    m = np.concatenate([(a - b >= 64), (np.abs(a - b) <= 64), (b - a >= 64)], axis=1)
    c["bandmask"] = m.astype(np.float32).astype(bf)
    c["ident_f"] = np.eye(128, dtype=np.float32)
    c["ident_b"] = np.eye(128, dtype=np.float32).astype(bf)
    c["tri_b"] = (a < b).astype(np.float32).astype(bf)
    eo = np.tile((np.arange(NE, dtype=np.float32) * CAP)[None, :], (128, NT))
    c["eoff"] = np.ascontiguousarray(eo.astype(np.float32))
    return c


_CACHE = {}


def kernel(x, w_in, lambda_q1, lambda_k1, lambda_q2, lambda_k2, diff_norm_w, w_branch, w_out,
           ln1_g, ln1_b, w_router, w_gate, w_up, w_down, ln2_g, ln2_b):
    f = lambda a: np.ascontiguousarray(np.asarray(a, dtype=np.float32))
    x = f(x)
    if "nc" not in _CACHE:
        _CACHE["nc"] = build_program()
        _CACHE["c"] = _consts()
    nc = _CACHE["nc"]
    c = _CACHE["c"]
    lam4 = np.concatenate([f(lambda_q1)[0], f(lambda_k1)[0], f(lambda_q2)[0], f(lambda_k2)[0]])[None, :]
    lnp = np.concatenate([f(ln1_g)[0], f(ln1_b)[0], f(ln2_g)[0], f(ln2_b)[0]])[None, :]
    shared = {
        "w_in": f(w_in)[0], "w_branch": f(w_branch)[0].reshape(1024, D), "w_out": f(w_out)[0],
        "w_router": f(w_router)[0], "w_gate": f(w_gate)[0].reshape(NE * D, D),
        "w_up": f(w_up)[0].reshape(NE * D, D), "w_down": f(w_down)[0].reshape(NE * D, D),
        "lam4": np.ascontiguousarray(np.broadcast_to(lam4, (128, 256))),
        "nw": np.ascontiguousarray(f(diff_norm_w)[0].reshape(128, 1)),
        "lnp": np.ascontiguousarray(np.broadcast_to(lnp, (128, 4 * D))),
    }
    shared.update(c)
    nb = x.shape[0]
    in_maps = []
    for core in range(8):
        m = dict(shared)
        m["x"] = np.ascontiguousarray(x[(core // 2) % nb])
        in_maps.append(m)
    res = run_bass_kernel_spmd(nc, in_maps, core_ids=list(range(8)))
    out = np.stack([np.asarray(res.results[2 * b]["out"], dtype=np.float32) for b in range(nb)], axis=0)
    return out
```
